# Optimizing a Trainium2 kernel written in Bass

```python
import math
import jax
import jax.numpy as jnp
from jax import lax
import numpy as np

D_MODEL = 1024
BATCH = 16
SEQ = 256
DEPTH = 4
DEC_BATCH = 2
DEC_SEQ = 2048
PAST_LEN = 256

GRID_W = 64
N_MIXERS = 2
N_ATTN_LAYERS = (DEPTH + 1) // 2
N_SSM_LAYERS = DEPTH // 2
N_MOD = 9
D_FF = 2816
MLA_HEADS = 16
Q_LORA = 512
KV_LORA = 256
QK_NOPE = 64
QK_ROPE = 32
V_HEAD = 64
QK_HEAD = QK_NOPE + QK_ROPE
MLA_IN = Q_LORA + KV_LORA + QK_ROPE
CACHE_DIM = KV_LORA + QK_ROPE
ROPE_BASE = 10000.0
Q_BLOCK = 128
SSM_EXPAND = 2
D_INNER = SSM_EXPAND * D_MODEL
SSM_HEADDIM = 64
SSM_HEADS = D_INNER // SSM_HEADDIM
SSM_GROUPS = 4
D_STATE = 128
CONV_W = 5
CONV_DIM = D_INNER + 2 * SSM_GROUPS * D_STATE
SSM_IN = D_INNER + CONV_DIM + 2 * SSM_HEADS
CHUNK = 128
EPS = 1e-6

kernel_name = 'hybrid_mla_ssd_prefix_diffusion_step'


def rmsnorm(x, g):
    x32 = x.astype(jnp.float32)
    y = x32 * lax.rsqrt(jnp.mean(x32 * x32, axis=-1, keepdims=True) + EPS)
    return (y * g.astype(jnp.float32)).astype(x.dtype)


def adaln_input(x, g, shift, scale):
    return rmsnorm(x, g) * (1 + scale) + shift


def modulation(sc, w, b):
    m = (sc @ w + b).reshape(sc.shape[0], N_MOD, D_MODEL)
    return [m[:, k, None, :] for k in range(N_MOD)]


def swiglu(h, w_in, w_out):
    gate, up = jnp.split(h @ w_in, 2, axis=-1)
    return (jax.nn.silu(gate) * up) @ w_out


def ffn_half(x, shift, scale, gate, g, w_in, w_out):
    return x + 0.5 * gate * swiglu(adaln_input(x, g, shift, scale), w_in, w_out)


def axial_rope_tables(seq_len):
    rows = seq_len // GRID_W
    row = jnp.repeat(jnp.arange(rows), GRID_W)
    col = jnp.tile(jnp.arange(GRID_W), rows)
    pos = jnp.stack([row, col], axis=-1).astype(jnp.float32)
    half = QK_ROPE // 2
    freqs = 1.0 / (ROPE_BASE ** (jnp.arange(0, half, 2, dtype=jnp.float32) / half))
    ang = pos[:, :, None] * freqs
    ang = jnp.broadcast_to(ang[:, :, None, :], (seq_len, 2, 2, half // 2))
    return jnp.cos(ang), jnp.sin(ang)


def apply_axial_rope(x, cos, sin):
    xr = x.astype(jnp.float32).reshape(x.shape[:-1] + (2, 2, QK_ROPE // 4))
    rot = jnp.stack([-xr[..., 1, :], xr[..., 0, :]], axis=-2)
    return (xr * cos + rot * sin).reshape(x.shape).astype(x.dtype)


def block_attention(q, k, v):
    b, lq, h, dk = q.shape
    nb = lq // Q_BLOCK
    qb = q.reshape(b, nb, Q_BLOCK, h, dk).transpose(1, 0, 2, 3, 4)
    k32 = k.astype(jnp.float32)
    v32 = v.astype(jnp.float32)
    scale = 1.0 / math.sqrt(dk)

    def one_block(qblk):
        s = jnp.einsum('bqhd,bkhd->bhqk', qblk.astype(jnp.float32), k32) * scale
        p = jax.nn.softmax(s, axis=-1)
        return jnp.einsum('bhqk,bkhd->bqhd', p, v32)

    o = lax.map(one_block, qb)
    return o.transpose(1, 0, 2, 3, 4).reshape(b, lq, h, -1).astype(q.dtype)


def mla_project(h, w_in, q_norm, kv_norm, wq_b):
    b, l, _ = h.shape
    q_a, kv_a, k_r = jnp.split(h @ w_in, [Q_LORA, Q_LORA + KV_LORA], axis=-1)
    q = (rmsnorm(q_a, q_norm) @ wq_b).reshape(b, l, MLA_HEADS, QK_HEAD)
    ckv = rmsnorm(kv_a, kv_norm)
    return q, ckv, k_r


def mla_expand_kv(ckv, k_r, wkv_b):
    b, l, _ = ckv.shape
    kv = (ckv @ wkv_b).reshape(b, l, MLA_HEADS, QK_NOPE + V_HEAD)
    k_nope, v = jnp.split(kv, [QK_NOPE], axis=-1)
    k = jnp.concatenate([k_nope, jnp.broadcast_to(k_r[:, :, None, :], (b, l, MLA_HEADS, QK_ROPE))], axis=-1)
    return k, v


def mla_context(h, w_in, q_norm, kv_norm, wq_b, wkv_b, wo):
    b, l, _ = h.shape
    q, ckv, k_r = mla_project(h, w_in, q_norm, kv_norm, wq_b)
    k, v = mla_expand_kv(ckv, k_r, wkv_b)
    o = block_attention(q, k, v)
    out = o.reshape(b, l, MLA_HEADS * V_HEAD) @ wo
    return out, jnp.concatenate([ckv, k_r], axis=-1)


def mla_latent(h, cache, cos, sin, w_in, q_norm, kv_norm, wq_b, wkv_b, wo):
    b, l, _ = h.shape
    q, ckv, k_r = mla_project(h, w_in, q_norm, kv_norm, wq_b)
    q = jnp.concatenate([q[..., :QK_NOPE], apply_axial_rope(q[..., QK_NOPE:], cos[:, None], sin[:, None])], axis=-1)
    k_r = apply_axial_rope(k_r, cos, sin)
    k_lat, v_lat = mla_expand_kv(ckv, k_r, wkv_b)
    k_ctx, v_ctx = mla_expand_kv(cache[..., :KV_LORA], cache[..., KV_LORA:], wkv_b)
    k = jnp.concatenate([k_ctx, k_lat], axis=1)
    v = jnp.concatenate([v_ctx, v_lat], axis=1)
    o = block_attention(q, k, v)
    return o.reshape(b, l, MLA_HEADS * V_HEAD) @ wo


def centred_depthwise_conv(x, w, bias):
    out = lax.conv_general_dilated(
        x, w[:, None, :], window_strides=(1,), padding=[(CONV_W // 2, CONV_W // 2)],
        dimension_numbers=('NWC', 'WIO', 'NWC'), feature_group_count=x.shape[-1])
    return out + bias


def ssd_chunked(x, dt, a, bmat, cmat, h0):
    b, l, h, p = x.shape
    g = SSM_GROUPS
    r = h // g
    nc = l // CHUNK
    xg = x.reshape(b, nc, CHUNK, g, r, p)
    dtg = dt.reshape(b, nc, CHUNK, g, r)
    bg = bmat.reshape(b, nc, CHUNK, g, D_STATE)
    cg = cmat.reshape(b, nc, CHUNK, g, D_STATE)
    cum = jnp.cumsum(dtg * a.reshape(g, r), axis=2)
    xdt = xg * dtg[..., None]
    seg = cum[:, :, :, None] - cum[:, :, None, :]
    mask = jnp.tril(jnp.ones((CHUNK, CHUNK), dtype=bool))[:, :, None, None]
    lmat = jnp.exp(jnp.where(mask, seg, -jnp.inf))
    cb = jnp.einsum('bcign,bcjgn->bcijg', cg, bg)
    y_diag = jnp.einsum('bcijg,bcijgr,bcjgrp->bcigrp', cb, lmat, xdt)
    decay_end = jnp.exp(cum[:, :, -1:] - cum)
    states = jnp.einsum('bcjgn,bcjgr,bcjgrp->bcgrpn', bg, decay_end, xdt)
    chunk_decay = jnp.exp(cum[:, :, -1])

    def step(hc, inp):
        st, dc = inp
        return hc * dc[..., None, None] + st, hc

    h_fin, h_in = lax.scan(step, h0.reshape(b, g, r, p, D_STATE),
                           (states.transpose(1, 0, 2, 3, 4, 5), chunk_decay.transpose(1, 0, 2, 3)))
    h_in = h_in.transpose(1, 0, 2, 3, 4, 5)
    y_off = jnp.einsum('bcign,bcigr,bcgrpn->bcigrp', cg, jnp.exp(cum), h_in)
    y = (y_diag + y_off).reshape(b, l, h, p)
    return y, h_fin.reshape(b, h, p, D_STATE)


def ssm_mixer(h, h0, w_in, conv_w, conv_b, dt_bias, a_log, d_skip, norm_g, w_out):
    b, l, _ = h.shape
    z, xbc, dt_raw = jnp.split(h @ w_in, [D_INNER, D_INNER + CONV_DIM], axis=-1)
    xbc = jax.nn.silu(centred_depthwise_conv(xbc, conv_w, conv_b))
    xs, bm, cm = jnp.split(xbc, [D_INNER, D_INNER + SSM_GROUPS * D_STATE], axis=-1)
    xs = xs.reshape(b, l, SSM_HEADS, SSM_HEADDIM).astype(jnp.float32)
    bm = bm.reshape(b, l, SSM_GROUPS, D_STATE).astype(jnp.float32)
    cm = cm.reshape(b, l, SSM_GROUPS, D_STATE).astype(jnp.float32)
    dt = jax.nn.softplus(dt_raw.reshape(b, l, 2, SSM_HEADS).astype(jnp.float32) + dt_bias.astype(jnp.float32))
    a = -jnp.exp(a_log.astype(jnp.float32))
    h0f = h0.astype(jnp.float32)
    y_f, h_f = ssd_chunked(xs, dt[:, :, 0], a[0], bm, cm, h0f[:, 0])
    y_b, h_b = ssd_chunked(xs[:, ::-1], dt[:, ::-1, 1], a[1], bm[:, ::-1], cm[:, ::-1], h0f[:, 1])
    dsum = (d_skip[0] + d_skip[1]).astype(jnp.float32)
    y = y_f + y_b[:, ::-1] + dsum[:, None] * xs
    y = y.reshape(b, l, D_INNER).astype(h.dtype)
    y = rmsnorm(y * jax.nn.silu(z), norm_g)
    return y @ w_out, jnp.stack([h_f, h_b], axis=1).astype(h.dtype)


def setup_inputs(seed: int = 0) -> dict:
    key = jax.random.key(seed)
    ks = jax.random.split(key, 32)

    def nrm(k, shape, scale):
        return scale * jax.random.normal(k, shape, jnp.float32)

    na, ns = N_ATTN_LAYERS, N_SSM_LAYERS
    dt0 = jnp.exp(jax.random.uniform(ks[20], (ns, 2, SSM_HEADS), jnp.float32, math.log(1e-3), math.log(1e-1)))
    dt_bias = dt0 + jnp.log(-jnp.expm1(-dt0))
    a_log = jnp.log(jax.random.uniform(ks[21], (ns, 2, SSM_HEADS), jnp.float32, 1.0, 16.0))
    return {
        'x_prompt': nrm(ks[0], (BATCH, SEQ, D_MODEL), 1.0),
        'x_sample': nrm(ks[1], (DEC_BATCH, DEC_SEQ, D_MODEL), 1.0),
        'cache_mla': nrm(ks[2], (DEC_BATCH, na, PAST_LEN, CACHE_DIM), 1.0),
        'state_ssm': nrm(ks[3], (DEC_BATCH, ns, 2, SSM_HEADS, SSM_HEADDIM, D_STATE), 0.5),
        'c': nrm(ks[4], (DEC_BATCH, D_MODEL), 1.0),
        'c_ctx': nrm(ks[5], (D_MODEL,), 1.0),
        'mod_w': nrm(ks[6], (DEPTH, D_MODEL, N_MOD * D_MODEL), 0.5 * D_MODEL ** -0.5),
        'mod_b': nrm(ks[7], (DEPTH, N_MOD * D_MODEL), 0.02),
        'norm_g': 1.0 + nrm(ks[8], (DEPTH, 3, D_MODEL), 0.05),
        'ffn_w_in': nrm(ks[9], (DEPTH, 2, D_MODEL, 2 * D_FF), D_MODEL ** -0.5),
        'ffn_w_out': nrm(ks[10], (DEPTH, 2, D_FF, D_MODEL), D_FF ** -0.5),
        'mla_w_in': nrm(ks[11], (na, D_MODEL, MLA_IN), D_MODEL ** -0.5),
        'mla_q_norm': 1.0 + nrm(ks[12], (na, Q_LORA), 0.05),
        'mla_kv_norm': 1.0 + nrm(ks[13], (na, KV_LORA), 0.05),
        'mla_wq_b': nrm(ks[14], (na, Q_LORA, MLA_HEADS * QK_HEAD), Q_LORA ** -0.5),
        'mla_wkv_b': nrm(ks[15], (na, KV_LORA, MLA_HEADS * (QK_NOPE + V_HEAD)), KV_LORA ** -0.5),
        'mla_wo': nrm(ks[16], (na, MLA_HEADS * V_HEAD, D_MODEL), (MLA_HEADS * V_HEAD) ** -0.5),
        'ssm_w_in': nrm(ks[17], (ns, D_MODEL, SSM_IN), D_MODEL ** -0.5),
        'ssm_conv_w': nrm(ks[18], (ns, CONV_W, CONV_DIM), CONV_W ** -0.5),
        'ssm_conv_b': nrm(ks[19], (ns, CONV_DIM), 0.02),
        'ssm_dt_bias': dt_bias,
        'ssm_a_log': a_log,
        'ssm_d': 1.0 + nrm(ks[22], (ns, 2, SSM_HEADS), 0.1),
        'ssm_norm_g': 1.0 + nrm(ks[23], (ns, D_INNER), 0.05),
        'ssm_w_out': nrm(ks[24], (ns, D_INNER, D_MODEL), D_INNER ** -0.5),
        'final_norm_g': 1.0 + nrm(ks[25], (D_MODEL,), 0.05),
    }


def reference(x_prompt, x_sample, cache_mla, state_ssm, c, c_ctx, mod_w, mod_b, norm_g,
              ffn_w_in, ffn_w_out, mla_w_in, mla_q_norm, mla_kv_norm, mla_wq_b, mla_wkv_b, mla_wo,
              ssm_w_in, ssm_conv_w, ssm_conv_b, ssm_dt_bias, ssm_a_log, ssm_d, ssm_norm_g, ssm_w_out,
              final_norm_g):
    cos, sin = axial_rope_tables(x_sample.shape[1])
    xp, xs = x_prompt, x_sample
    sc_ctx = jax.nn.silu(c_ctx)[None, :]
    sc_lat = jax.nn.silu(c)
    new_mla, new_ssm = [], []
    for i in range(DEPTH):
        m_ctx = modulation(sc_ctx, mod_w[i], mod_b[i])
        m_lat = modulation(sc_lat, mod_w[i], mod_b[i])
        xp = ffn_half(xp, m_ctx[0], m_ctx[1], m_ctx[2], norm_g[i, 0], ffn_w_in[i, 0], ffn_w_out[i, 0])
        xs = ffn_half(xs, m_lat[0], m_lat[1], m_lat[2], norm_g[i, 0], ffn_w_in[i, 0], ffn_w_out[i, 0])
        hp = adaln_input(xp, norm_g[i, 1], m_ctx[3], m_ctx[4])
        hs = adaln_input(xs, norm_g[i, 1], m_lat[3], m_lat[4])
        j = i // N_MIXERS
        if i % N_MIXERS == 0:
            wts = (mla_w_in[j], mla_q_norm[j], mla_kv_norm[j], mla_wq_b[j], mla_wkv_b[j], mla_wo[j])
            op, ctx_entry = mla_context(hp, *wts)
            os_ = mla_latent(hs, cache_mla[:, j], cos, sin, *wts)
            new_mla.append(ctx_entry)
        else:
            wts = (ssm_w_in[j], ssm_conv_w[j], ssm_conv_b[j], ssm_dt_bias[j], ssm_a_log[j], ssm_d[j],
                   ssm_norm_g[j], ssm_w_out[j])
            zero_state = jnp.zeros((xp.shape[0], 2, SSM_HEADS, SSM_HEADDIM, D_STATE), xp.dtype)
            op, ctx_state = ssm_mixer(hp, zero_state, *wts)
            os_, _ = ssm_mixer(hs, state_ssm[:, j], *wts)
            new_ssm.append(ctx_state)
        xp = xp + m_ctx[5] * op
        xs = xs + m_lat[5] * os_
        xp = ffn_half(xp, m_ctx[6], m_ctx[7], m_ctx[8], norm_g[i, 2], ffn_w_in[i, 1], ffn_w_out[i, 1])
        xs = ffn_half(xs, m_lat[6], m_lat[7], m_lat[8], norm_g[i, 2], ffn_w_in[i, 1], ffn_w_out[i, 1])
    y_prompt = rmsnorm(xp, final_norm_g)
    y_sample = rmsnorm(xs, final_norm_g)
    new_cache_mla = jnp.stack(new_mla, axis=1)
    new_state_ssm = jnp.stack(new_ssm, axis=1)
    return (y_prompt, y_sample, new_cache_mla, new_state_ssm)
```

```python
import numpy as np
from contextlib import ExitStack
import concourse.bass as bass
import concourse.mybir as mybir

F32 = mybir.dt.float32
BF16 = mybir.dt.bfloat16
AF = mybir.ActivationFunctionType
ALU = mybir.AluOpType
AX = mybir.AxisListType

EPOCH = 12000


class T:
    __slots__ = ("t", "name", "w", "rd", "dsem", "dcnt", "uid", "psum")
    _n = [0]

    def __init__(self, t, name="v"):
        T._n[0] += 1
        self.uid = T._n[0]
        self.t = t
        self.name = name
        self.w = []
        self.rd = []
        self.dsem = None
        self.dcnt = 0
        self.psum = False

    def __getitem__(self, idx):
        return self.t[idx]


class _Rec:
    def __init__(self):
        self.call = None

    def __getattr__(self, name):
        def f(*a, **kw):
            self.call = (name, a, kw)
            return self
        return f


class Prog:
    ENG = ("pe", "dve", "act", "pool", "sp")

    def __init__(self, nc, stack):
        self.nc = nc
        self.stack = stack
        self.h = {"pe": nc.tensor, "dve": nc.vector, "act": nc.scalar, "pool": nc.gpsimd, "sp": nc.sync}
        self.streams = {e: [] for e in self.ENG}
        self.count = {e: 0 for e in self.ENG}
        self.pending = {e: False for e in self.ENG}
        self.esems = {e: [] for e in self.ENG}
        self.seen = {e: {} for e in self.ENG}
        self.nsem = 0

    def sem(self, name):
        self.nsem += 1
        return self.stack.enter_context(self.nc.semaphore(f"{name}_{self.nsem}"))

    def sb(self, name, shape, dt=F32):
        return T(self.stack.enter_context(self.nc.sbuf_tensor(name, list(shape), dt)), name)

    def ps(self, name, shape, dt=F32):
        t = T(self.stack.enter_context(self.nc.psum_tensor(name, list(shape), dt)), name)
        t.psum = True
        return t

    def dram(self, name, shape, dt=F32, kind="Internal"):
        return T(self.nc.dram_tensor(name, list(shape), dt, kind=kind), name)

    def _esem(self, e, ep):
        while len(self.esems[e]) <= ep:
            self.esems[e].append(self.sem(f"c_{e}_{len(self.esems[e])}"))
        return self.esems[e][ep]

    def _wait(self, eng, ev):
        if ev[0] == "e":
            _, src, idx = ev
            ep, v = (idx - 1) // EPOCH, (idx - 1) % EPOCH + 1
            key = (src, ep)
            sem = self._esem(src, ep)
        else:
            _, sem, v, key = ev
        if self.seen[eng].get(key, 0) >= v:
            return
        if ev[0] == "e":
            for pe in range(ep):
                self.seen[eng][(src, pe)] = EPOCH
        self.seen[eng][key] = v
        self.streams[eng].append(lambda h, sem=sem, v=v: h.wait_ge(sem, v))

    def _deps(self, eng, reads, writes, same_eng_raw=True, waw=True):
        evs = []
        for t in reads:
            evs += t.w
            if t.psum:
                evs += [e for e in t.rd if not (e[0] == "e" and e[1] == eng)]
        for t in writes:
            if waw or t.psum:
                evs += t.w
            evs += t.rd
        for ev in evs:
            if ev[0] == "e" and ev[1] == eng:
                if eng == "pe" or not same_eng_raw:
                    continue
            self._wait(eng, ev)

    def op(self, eng, fn, reads=(), writes=(), sig=True, waw=True):
        reads = [r for r in reads if r is not None]
        writes = [w for w in writes if w is not None]
        self._deps(eng, reads, writes, waw=waw)
        rec = _Rec()
        fn(rec)
        name, a, kw = rec.call
        if sig:
            self.count[eng] += 1
            idx = self.count[eng]
            ep = (idx - 1) // EPOCH
            sem = self._esem(eng, ep)
            self.streams[eng].append(lambda h, name=name, a=a, kw=kw, sem=sem: getattr(h, name)(*a, **kw).then_inc(sem, 1))
            self.pending[eng] = False
        else:
            idx = self.count[eng] + 1
            self.streams[eng].append(lambda h, name=name, a=a, kw=kw: getattr(h, name)(*a, **kw))
            self.pending[eng] = True
        ev = ("e", eng, idx)
        for t in writes:
            if waw or t.psum or t.rd:
                t.w = [ev]
            else:
                t.w = self._compact(t.w + [ev]) if len(t.w) > 12 else t.w + [ev]
            t.rd = []
        for t in reads:
            if t not in writes:
                t.rd.append(ev)
                if len(t.rd) > 24:
                    t.rd = self._compact(t.rd)
        return ev

    @staticmethod
    def _compact(evs):
        best = {}
        out = []
        for ev in evs:
            if ev[0] == "e":
                k = ev[1]
                if k not in best or best[k][2] < ev[2]:
                    best[k] = ev
            else:
                k = ev[3]
                if k not in best or best[k][2] < ev[2]:
                    best[k] = ev
        return list(best.values())

    def dma(self, q, out_ap, in_ap, reads=(), writes=(), sem_tile=None, **kw):
        reads = [r for r in reads if r is not None]
        writes = [w for w in writes if w is not None]
        st = sem_tile if sem_tile is not None else (writes[0] if writes else reads[0])
        cls = "sw" if q == "pool" else "hw"
        if st.dsem is None:
            st.dsem = {}
            st.dcnt = {}
        if cls not in st.dsem:
            st.dsem[cls] = self.sem("d_" + st.name)
            st.dcnt[cls] = 0
        self._deps(q, reads, writes, same_eng_raw=True)
        st.dcnt[cls] += 1
        v = 16 * st.dcnt[cls]
        sem = st.dsem[cls]
        self.streams[q].append(
            lambda h, o=out_ap, i=in_ap, sem=sem, kw=kw: h.dma_start(out=o, in_=i, **kw).then_inc(sem, 16))
        ev = ("d", sem, v, ("d", st.uid, cls))
        for t in writes:
            t.w = [e for e in t.w if e[0] == "d" and e[3][1] == st.uid and e[3] != ev[3]] + [ev]
            t.rd = []
        for t in reads:
            if t not in writes:
                t.rd = [e for e in t.rd if not (e[0] == "d" and e[3] == ev[3])] + [ev]
        return ev

    def collective(self, kind, groups, src, dst):
        self._deps("pool", [src], [dst])
        if dst.dsem is None:
            dst.dsem = {}
            dst.dcnt = {}
        if "cc" not in dst.dsem:
            dst.dsem["cc"] = self.sem("cc_" + dst.name)
            dst.dcnt["cc"] = 0
        dst.dcnt["cc"] += 1
        v = dst.dcnt["cc"]
        sem = dst.dsem["cc"]
        sa, da = src.t.ap().opt(), dst.t.ap().opt()
        self.streams["pool"].append(lambda h: h.collective_compute(
            kind, ALU.bypass, replica_groups=groups, ins=[sa], outs=[da]).then_inc(sem))
        ev = ("d", sem, v, ("d", dst.uid, "cc"))
        dst.w = [ev]
        dst.rd = []
        src.rd.append(ev)
        return ev

    def barrier(self, engs=("pe", "dve", "act", "sp"), tiles=()):
        for e in engs:
            for src in self.ENG:
                if src == e or self.count[src] == 0:
                    continue
                self._wait(e, ("e", src, self.count[src]))
            for t in tiles:
                for ev in t.w + t.rd:
                    if ev[0] == "d":
                        self._wait(e, ev)

    def wait_all(self, eng, tiles):
        for t in tiles:
            for ev in t.w + t.rd:
                self._wait(eng, ev)

    def emit(self):
        nc = self.nc
        with nc.Block() as block:
            def mk(e):
                def body(h):
                    for f in self.streams[e]:
                        f(h)
                return body
            block.tensor(mk("pe"))
            block.vector(mk("dve"))
            block.scalar(mk("act"))
            block.gpsimd(mk("pool"))
            block.sync(mk("sp"))

from concourse.bass_utils import run_bass_kernel_spmd

import math

NCORES = 8
D = 1024
TT = 1024
HT = 512
NCH = 8
DFF = 2816
NF = 22
EPS = 1e-6
DEPTH = 4


class Arena:
    def __init__(self, P, words):
        self.P = P
        self.words = words
        self.t = P.stack.enter_context(P.nc.sbuf_tensor("arena", [128, words], F32))
        self.off = 0
        self.tiles = []

    def alloc(self, shape, dt=F32, name="a"):
        n = 1
        for s in shape[1:]:
            n *= s
        w = n if dt == F32 else (n + 1) // 2
        assert self.off + w <= self.words, ("arena overflow", name, self.off, w, self.words)
        ap = self.t[0:shape[0], self.off:self.off + w]
        if dt != F32:
            ap = ap.bitcast(dt)
        if len(shape) > 2:
            names = " ".join(f"d{i}" for i in range(len(shape) - 1))
            kw = {f"d{i}": shape[i + 1] for i in range(len(shape) - 1)}
            ap = ap.rearrange(f"p ({names}) -> p {names}", **kw)
        self.off += w
        t = T(ap, name)
        self.tiles.append(t)
        return t

    def reset(self, keep=0, keep_tiles=()):
        self.P.barrier(tiles=self.tiles)
        self.off = keep
        self.tiles = list(keep_tiles)


class K:
    pass


def build_program(stage=99):
    nc = bass.Bass("TRN2", target_bir_lowering=False)

    def din(name, shape, dt=F32):
        return nc.dram_tensor(name, list(shape), dt, kind="ExternalInput").ap()

    def dout(name, shape, dt=F32):
        return nc.dram_tensor(name, list(shape), dt, kind="ExternalOutput").ap()

    I = {}
    I["xT"] = din("xT", [128, NCH, TT])
    I["cvec"] = din("cvec", [128, NCH, 2])
    if stage >= 0:
        I["modw"] = din("modw", [DEPTH, 9, 2, 128, NCH * 512])
    I["modb"] = din("modb", [128, DEPTH * 9 * NCH])
    I["normg"] = din("normg", [128, DEPTH * 3 * NCH])
    I["fnormg"] = din("fnormg", [128, NCH])
    if stage >= 0:
        I["ffn_in"] = din("ffn_in", [DEPTH, 2, 11, 128, NCH * 512])
        I["ffn_out"] = din("ffn_out", [DEPTH, 2, NCH, 128, NF * 128])
    I["mla_w1"] = din("mla_w1", [2, 128, NCH * 512])
    I["mla_w2"] = din("mla_w2", [2, 128, NCH * 384])
    I["mla_wq"] = din("mla_wq", [2, 2, 2, 128, 4 * 768])
    I["mla_wkv"] = din("mla_wkv", [2, 128, 2 * 2048])
    I["mla_wo"] = din("mla_wo", [2, 2, 128, 8 * 512])
    I["mla_small"] = din("mla_small", [128, 2 * 6])
    I["ropeT"] = din("ropeT", [96, 2, HT])
    I["cacheT"] = din("cacheT", [2, 288, 256])
    I["ssm_win"] = din("ssm_win", [2, 10, 128, NCH * 512])
    I["ssm_wdt"] = din("ssm_wdt", [2, 128, NCH * 64])
    I["ssm_wout"] = din("ssm_wout", [2, 4, 128, 16 * 256])
    I["ssm_small"] = din("ssm_small", [128, 2 * 192])
    I["ssm_bc"] = din("ssm_bc", [128, 2, 2, 64])
    I["consts"] = din("consts", [128, 5, 128])
    I["posm"] = din("posm", [128, 16])
    I["stateT"] = din("stateT", [2, 2, 128, 2048])
    O = {}
    O["stateO"] = dout("stateO", [2, 2, 2, 128, 2048])
    O["yT"] = dout("yT", [128, NCH, TT])
    O["cacheO"] = dout("cacheO", [2, 288, HT])

    with ExitStack() as st:
        P = Prog(nc, st)
        k = K()
        k.P, k.nc, k.I, k.O = P, nc, I, O
        Xt = st.enter_context(nc.sbuf_tensor("X", [128, NCH, TT], F32))
        k.X = [[T(Xt[:, c, h * HT:(h + 1) * HT], f"X{c}{h}") for h in range(2)] for c in range(NCH)]
        k.Xall = [k.X[c][h] for c in range(NCH) for h in range(2)]
        k.wslots = [P.sb(f"ws{i}", [128, 4096], BF16) for i in range(4)]
        k.wi = 0
        k.out_tiles = []
        k.banks = [P.ps(f"bk{i}", [128, 512], F32) for i in range(8)]
        k.bi = 0
        k.ri = 0
        k.ones = P.sb("ones", [128, 128], BF16)
        k.mod = P.sb("s_mod", [128, DEPTH * 9 * NCH * 2], F32)
        k.modb = P.sb("s_modb", [128, DEPTH * 9 * NCH], F32)
        k.normg = P.sb("s_normg", [128, DEPTH * 3 * NCH], F32)
        k.fnormg = P.sb("s_fnormg", [128, NCH], F32)
        k.cv = P.sb("cv", [128, NCH, 2], F32)
        k.scb = P.sb("scb", [128, NCH, 2], BF16)
        k.gsc = P.sb("gsc", [128, NCH, 2], F32)
        k.hgate = P.sb("hgate", [128, NCH, 2], F32)
        k.arena = Arena(P, 29 * 1024)
        k.ssmall = P.sb("s_ssmall", [128, 384], F32)
        k.sbc = P.sb("s_sbc", [128, 2, 2, 64], F32)
        k.cst = P.sb("s_cst", [128, 5, 128], F32)
        k.posm = P.sb("s_posm", [128, 16], F32)
        k.ones32 = P.sb("ones32", [128, 128], F32)
        k.identb = P.sb("identb", [128, 128], BF16)
        k.oneb = P.sb("oneb", [128, 1], F32)
        k.zer = P.sb("zer", [128, 512], F32)
        P.op("dve", lambda h: h.memset(k.zer[:], 0.0), writes=[k.zer])
        P.dma("sp", k.ssmall[:], I["ssm_small"], writes=[k.ssmall])
        P.dma("sp", k.sbc[:], I["ssm_bc"], writes=[k.sbc])
        P.dma("sp", k.cst[:], I["consts"], writes=[k.cst])
        P.dma("sp", k.posm[:], I["posm"], writes=[k.posm])
        P.op("dve", lambda h: h.memset(k.ones32[:], 1.0), writes=[k.ones32])
        P.op("dve", lambda h: h.memset(k.oneb[:], 1.0), writes=[k.oneb])
        P.op("dve", lambda h: h.tensor_copy(k.identb[:], k.cst[:, 4, :]), reads=[k.cst], writes=[k.identb])
        k.msmall = P.sb("s_msmall", [128, 12], F32)
        k.rope = P.sb("s_rope", [96, 2, HT], F32)
        k.epsb = P.sb("epsb", [128, 1], F32)
        P.op("dve", lambda h: h.memset(k.epsb[:], EPS), writes=[k.epsb])
        P.dma("sp", k.msmall[:], I["mla_small"], writes=[k.msmall])
        P.dma("sp", k.rope[64:96, :, :], I["ropeT"][64:96, :, :], writes=[k.rope])

        P.dma("sp", Xt[:], I["xT"], writes=k.Xall)
        P.dma("sp", k.modb[:], I["modb"], writes=[k.modb])
        P.dma("sp", k.normg[:], I["normg"], writes=[k.normg])
        P.dma("sp", k.fnormg[:], I["fnormg"], writes=[k.fnormg])
        P.dma("sp", k.cv[:], I["cvec"], writes=[k.cv])
        P.op("dve", lambda h: h.memset(k.ones[:], 1.0), writes=[k.ones])
        P.op("act", lambda h: h.activation(k.scb[:], k.cv[:], AF.Silu), reads=[k.cv], writes=[k.scb])

        if stage < 0:
            import os
            k.cut = float(os.environ.get("MLA_CUT", "99"))
            P.op("dve", lambda h: h.memset(k.mod[:], 0.01), writes=[k.mod])
            if stage == -1:
                mla(k, 0, 0)
            else:
                ssm(k, 1, 0)
            k.arena.reset()
        for i in range(DEPTH if stage >= 0 else 0):
            if i == 0:
                modulation(k, 0)
            ffn(k, i, 0)
            k.arena.reset()
            if stage <= 1:
                break
            k.extra = modulation_gen(k, i + 1) if (i % 2 == 1 and i + 1 < DEPTH) else None
            if i % 2 == 0:
                mla(k, i, i // 2)
            else:
                ssm(k, i, i // 2)
            k.arena.reset()
            if stage == 2 + 2 * i:
                break
            nxt = k.extra if k.extra is not None else (modulation_gen(k, i + 1) if i + 1 < DEPTH else None)
            k.extra = None
            ffn(k, i, 1, extra=nxt)
            if nxt is not None:
                for _ in nxt:
                    pass
            k.arena.reset()
        final_norm(k)
        P.emit()
    nc._in_names = list(I.keys())
    return nc


def _mark(k, name):
    pass


def next_bank(k):
    b = k.banks[k.bi % 6]
    k.bi += 1
    return b


def acc_bank(k):
    b = k.banks[6 + k.ri % 2]
    k.ri += 1
    return b


def next_slot(k):
    s = k.wslots[k.wi % 4]
    k.wi += 1
    return s


def modv(k, i, kk):
    base = ((i * 9 + kk) * NCH) * 2
    return k.mod[:, base:base + NCH * 2].rearrange("p (c r) -> p c r", r=2)


def modulation(k, i):
    for _ in modulation_gen(k, i):
        pass


def modulation_gen(k, i):
    P = k.P
    for kk in range(9):
        for hc in range(2):
            yield
            s = next_slot(k)
            P.dma("pool", s[:, 0:NCH * 512], k.I["modw"][i, kk, hc], writes=[s])
            w = s[:, 0:NCH * 512].rearrange("p (c n) -> p c n", n=512)
            b = next_bank(k)
            for m in range(4):
                for c in range(NCH):
                    P.op("pe", lambda h, m=m, c=c, b=b, w=w: h.matmul(
                        b[:, m * 2:m * 2 + 2], w[:, c, m * 128:(m + 1) * 128], k.scb[:, c, :],
                        start=(c == 0), stop=(c == NCH - 1)),
                        reads=[s, k.scb], writes=[b], sig=(c == NCH - 1))
            base = (i * 9 + kk) * NCH + hc * 4
            ob = base * 2
            P.op("dve", lambda h, b=b, base=base, ob=ob: h.tensor_tensor(
                k.mod[:, ob:ob + 8].rearrange("p (c r) -> p c r", r=2),
                b[:, 0:8].rearrange("p (c r) -> p c r", r=2),
                k.modb[:, base:base + 4].unsqueeze(2).broadcast_to([128, 4, 2]), ALU.add),
                reads=[b, k.modb], writes=[k.mod])


def rsqrt_from_bank(k, b, out_rstd, n):
    P = k.P
    P.op("act", lambda hh: hh.activation(out_rstd[:], b[:], AF.Ln, bias=k.epsb[:], scale=1.0 / n),
         reads=[b, k.epsb], writes=[out_rstd])
    P.op("act", lambda hh: hh.activation(out_rstd[:], out_rstd[:], AF.Exp, scale=-0.5),
         reads=[out_rstd], writes=[out_rstd])


def rms_stats(k, h, out_rstd):
    P = k.P
    A = k.arena
    b = next_bank(k)
    for c in range(NCH):
        sq = k.sq[c % 2]
        xt = k.X[c][h]
        P.op("act", lambda hh, sq=sq, xt=xt: hh.activation(sq[:], xt[:], AF.Square), reads=[xt], writes=[sq])
        P.op("pe", lambda hh, sq=sq, b=b, c=c: hh.matmul(b[:], k.ones[:], sq[:], start=(c == 0), stop=(c == NCH - 1)),
             reads=[sq, k.ones], writes=[b], sig=True)
    P.op("act", lambda hh: hh.activation(out_rstd[:], b[:], AF.Ln, bias=k.epsb[:], scale=1.0 / D),
         reads=[b, k.epsb], writes=[out_rstd])
    P.op("act", lambda hh: hh.activation(out_rstd[:], out_rstd[:], AF.Exp, scale=-0.5),
         reads=[out_rstd], writes=[out_rstd])


def adaln(k, i, ksh, ksc, ng, halves=(0, 1)):
    P = k.P
    A = k.arena
    k.sq = [A.alloc([128, HT], BF16, f"sq{j}") for j in range(2)]
    rstd = [A.alloc([128, HT], F32, f"rstd{h}") if h in halves else None for h in range(2)]
    tmp = [A.alloc([128, HT], F32, f"ntmp{j}") for j in range(2)]
    k.last_rstd, k.last_tmp = rstd, tmp
    HN = [[A.alloc([128, HT], BF16, f"hn{c}{h}") if h in halves else None for h in range(2)] for c in range(NCH)]
    gb = (i * 3 + ng) * NCH
    sc = modv(k, i, ksc)
    sh = modv(k, i, ksh)
    P.op("dve", lambda h: h.scalar_tensor_tensor(
        k.gsc[:], sc, 1.0, k.normg[:, gb:gb + NCH].unsqueeze(2).broadcast_to([128, NCH, 2]), ALU.add, ALU.mult),
        reads=[k.mod, k.normg], writes=[k.gsc])
    for h in halves:
        rms_stats(k, h, rstd[h])
    for h in halves:
        for c in range(NCH):
            t = tmp[c % 2]
            xt = k.X[c][h]
            P.op("dve", lambda hh, t=t, xt=xt, c=c, h=h: hh.scalar_tensor_tensor(
                t[:], xt[:], k.gsc[:, c, h:h + 1], rstd[h][:], ALU.mult, ALU.mult),
                reads=[xt, k.gsc, rstd[h]], writes=[t])
            P.op("act", lambda hh, t=t, c=c, h=h: hh.activation(
                HN[c][h][:], t[:], AF.Identity, bias=sh[:, c, h:h + 1], scale=1.0),
                reads=[t, k.mod], writes=[HN[c][h]])
    return HN


def ffn(k, i, j, extra=None):
    P = k.P
    A = k.arena
    k3 = 0 if j == 0 else 6
    HN = adaln(k, i, k3 + 0, k3 + 1, 0 if j == 0 else 2)
    ACTT = [[A.alloc([128, HT], BF16, f"act{f}{h}") for h in range(2)] for f in range(NF)]
    sg = [A.alloc([128, HT], F32, f"sg{j2}") for j2 in range(2)]
    gt = modv(k, i, k3 + 2)
    P.op("dve", lambda h: h.tensor_scalar(k.hgate[:], gt, 0.5, 0.0, ALU.mult, ALU.add), reads=[k.mod], writes=[k.hgate])
    n = 0
    for g in range(11):
        if extra is not None:
            next(extra, None)
        s = next_slot(k)
        P.dma("pool", s[:, 0:NCH * 512], k.I["ffn_in"][i, j, g], writes=[s])
        w = s[:, 0:NCH * 512].rearrange("p (c n) -> p c n", n=512)
        for m in range(2):
            f = 2 * g + m
            for h in range(2):
                ba = next_bank(k)
                bb = next_bank(k)
                for (bk, co) in ((ba, m * 128), (bb, 256 + m * 128)):
                    for c in range(NCH):
                        P.op("pe", lambda hh, bk=bk, co=co, c=c, h=h, w=w: hh.matmul(
                            bk[:], w[:, c, co:co + 128], HN[c][h][:], start=(c == 0), stop=(c == NCH - 1)),
                            reads=[s, HN[c][h]], writes=[bk], sig=(c == NCH - 1))
                sgt = sg[n % 2]
                n += 1
                P.op("act", lambda hh, sgt=sgt, ba=ba: hh.activation(sgt[:], ba[:], AF.Silu), reads=[ba], writes=[sgt])
                P.op("dve", lambda hh, sgt=sgt, bb=bb, f=f, h=h: hh.tensor_tensor(
                    ACTT[f][h][:], sgt[:], bb[:], ALU.mult), reads=[sgt, bb], writes=[ACTT[f][h]])
    for dc in range(NCH):
        if extra is not None:
            next(extra, None)
        s = next_slot(k)
        P.dma("pool", s[:, 0:NF * 128], k.I["ffn_out"][i, j, dc], writes=[s])
        w = s[:, 0:NF * 128].rearrange("p (f n) -> p f n", n=128)
        for h in range(2):
            b = next_bank(k)
            for f in range(NF):
                P.op("pe", lambda hh, b=b, f=f, h=h, w=w: hh.matmul(
                    b[:], w[:, f, :], ACTT[f][h][:], start=(f == 0), stop=(f == NF - 1)),
                    reads=[s, ACTT[f][h]], writes=[b], sig=(f == NF - 1))
            xt = k.X[dc][h]
            P.op("dve", lambda hh, b=b, xt=xt, dc=dc, h=h: hh.scalar_tensor_tensor(
                xt[:], b[:], k.hgate[:, dc, h:h + 1], xt[:], ALU.mult, ALU.add),
                reads=[b, k.hgate, xt], writes=[xt])


GROUPS4 = [[0, 1, 2, 3], [4, 5, 6, 7]]
NKS = 2304
NKC = 18


def mla(k, i, j):
    P, A, I = k.P, k.arena, k.I
    scale = 1.0 / math.sqrt(96.0)
    QT = A.alloc([96, 16, TT], BF16, "QT")
    CKVb = [[A.alloc([128, HT], BF16, f"ckvb{m}{h}") for h in range(2)] for m in range(2)]
    KRb = A.alloc([128, TT], BF16, "KRb")
    P.op("dve", lambda hh: hh.memset(KRb[:], 0.0), writes=[KRb])
    LAT = A.alloc([128, 3, NKS], BF16, "LAT")
    keep, keep_tiles = A.off, list(A.tiles)
    HN = adaln(k, i, 3, 4, 1)
    QA = [[A.alloc([128, HT], F32, f"qa{m}{h}") for h in range(2)] for m in range(4)]
    QN = [[A.alloc([128, HT], BF16, f"qn{m}{h}") for h in range(2)] for m in range(4)]
    CKVf = [[A.alloc([128, HT], F32, f"ckvf{m}{h}") for h in range(2)] for m in range(2)]
    KRf = A.alloc([96, HT], F32, "KRf")
    rq = k.last_rstd
    rk = k.last_rstd
    t1 = [k.last_tmp[0], A.alloc([96, HT], F32, "t1b")]
    t2 = [k.last_tmp[1], A.alloc([96, HT], F32, "t2b")]
    qn = k.msmall[:, j * 6:j * 6 + 4]
    kvn = k.msmall[:, j * 6 + 4:j * 6 + 6]
    cosr = k.rope[64:96, 0, :]
    sinr = k.rope[64:96, 1, :]

    s1 = next_slot(k)
    P.dma("pool", s1[:, 0:NCH * 512], I["mla_w1"][j], writes=[s1])
    w1 = s1[:, 0:NCH * 512].rearrange("p (c n) -> p c n", n=512)
    s2 = next_slot(k)
    P.dma("pool", s2[:, 0:NCH * 384], I["mla_w2"][j], writes=[s2])
    w2 = s2[:, 0:NCH * 384].rearrange("p (c n) -> p c n", n=384)

    def proj_norm(ws, w, col0, nm, raw, rstd, nfeat):
        for h in range(2):
            sb = acc_bank(k)
            for m in range(nm):
                b = next_bank(k)
                for c in range(NCH):
                    P.op("pe", lambda hh, b=b, c=c, m=m, h=h: hh.matmul(
                        b[:], w[:, c, col0 + m * 128:col0 + (m + 1) * 128], HN[c][h][:],
                        start=(c == 0), stop=(c == NCH - 1)), reads=[ws, HN[c][h]], writes=[b], sig=(c == NCH - 1))
                sq = k.sq[m % 2]
                P.op("act", lambda hh, sq=sq, b=b: hh.activation(sq[:], b[:], AF.Square), reads=[b], writes=[sq])
                P.op("pe", lambda hh, sq=sq, sb=sb, m=m: hh.matmul(sb[:], k.ones[:], sq[:], start=(m == 0), stop=(m == nm - 1)),
                     reads=[sq, k.ones], writes=[sb])
                P.op("dve", lambda hh, b=b, m=m, h=h: hh.tensor_tensor(raw[m][h][:], b[:], k.zer[:], ALU.add), reads=[b, k.zer], writes=[raw[m][h]])
            rsqrt_from_bank(k, sb, rstd[h], nfeat)

    if getattr(k, "cut", 99) <= -4:
        return
    proj_norm(s1, w1, 0, 4, QA, rq, 512)
    if getattr(k, "cut", 99) <= -3.5:
        return
    for h in range(2):
        for m in range(4):
            P.op("dve", lambda hh, m=m, h=h: hh.scalar_tensor_tensor(
                QN[m][h][:], QA[m][h][:], qn[:, m:m + 1], rq[h][:], ALU.mult, ALU.mult),
                reads=[QA[m][h], k.msmall, rq[h]], writes=[QN[m][h]])
    if getattr(k, "cut", 99) <= -3:
        return
    KVA = CKVf
    proj_norm(s2, w2, 0, 2, KVA, rk, 256)
    for h in range(2):
        for m in range(2):
            P.op("dve", lambda hh, m=m, h=h: hh.scalar_tensor_tensor(
                CKVf[m][h][:], KVA[m][h][:], kvn[:, m:m + 1], rk[h][:], ALU.mult, ALU.mult),
                reads=[KVA[m][h], k.msmall, rk[h]], writes=[CKVf[m][h]])
            P.op("act", lambda hh, m=m, h=h: hh.activation(CKVb[m][h][:], CKVf[m][h][:], AF.Copy),
                 reads=[CKVf[m][h]], writes=[CKVb[m][h]])
    if getattr(k, "cut", 99) <= -2:
        return
    for h in range(2):
        if getattr(k, "cut", 99) <= -1 and h == 1:
            return
        bk = next_bank(k)
        for c in range(NCH):
            P.op("pe", lambda hh, bk=bk, c=c, h=h: hh.matmul(bk[0:96, :], w2[:, c, 192:288], HN[c][h][:],
                 start=(c == 0), stop=(c == NCH - 1)), reads=[s2, HN[c][h]], writes=[bk], sig=(c == NCH - 1))
        if h == 0:
            P.op("dve", lambda hh, bk=bk: hh.tensor_tensor(KRf[64:96, :], bk[64:96, :], k.zer[64:96, :], ALU.add), reads=[bk, k.zer], writes=[KRf])
            P.op("act", lambda hh, bk=bk: hh.activation(KRb[64:96, 0:HT], bk[64:96, :], AF.Copy), reads=[bk], writes=[KRb])
        else:
            bs = next_bank(k)
            for c in range(NCH):
                P.op("pe", lambda hh, bs=bs, c=c, h=h: hh.matmul(bs[0:96, :], w2[:, c, 288:384], HN[c][h][:],
                     start=(c == 0), stop=(c == NCH - 1)), reads=[s2, HN[c][h]], writes=[bs], sig=(c == NCH - 1))
            P.op("dve", lambda hh, bk=bk: hh.tensor_tensor(t1[0][64:96, :], bk[64:96, :], cosr, ALU.mult),
                 reads=[bk, k.rope], writes=[t1[0]])
            P.op("dve", lambda hh, bs=bs: hh.tensor_tensor(t2[0][64:96, :], bs[64:96, :], sinr, ALU.mult),
                 reads=[bs, k.rope], writes=[t2[0]])
            P.op("dve", lambda hh: hh.tensor_tensor(KRb[64:96, HT:TT], t1[0][64:96, :], t2[0][64:96, :], ALU.add),
                 reads=[t1[0], t2[0]], writes=[KRb])
    if getattr(k, "cut", 99) <= 0:
        return
    for m in range(2):
        P.dma("sp", k.O["cacheO"][j, m * 128:(m + 1) * 128, :], CKVf[m][0][:], reads=[CKVf[m][0]])
    P.dma("sp", k.O["cacheO"][j, 256:288, :], KRf[64:96, :], reads=[KRf])
    if getattr(k, "cut", 99) <= 1:
        return
    latb = P.dram(f"latb{j}", [384, HT], BF16)
    latg = P.dram(f"latg{j}", [4 * 384, HT], BF16)
    for m in range(2):
        P.dma("sp", latb.t.ap()[m * 128:(m + 1) * 128, :], CKVb[m][1][:], reads=[CKVb[m][1]], writes=[latb])
    P.dma("sp", latb.t.ap()[256:384, :], KRb[:, HT:TT], reads=[KRb], writes=[latb])
    import os
    if os.environ.get("NO_CC"):
        P.dma("sp", latg.t.ap()[0:384, :], latb.t.ap(), reads=[latb], writes=[latg])
        for r_ in range(1, 4):
            P.dma("sp", latg.t.ap()[r_ * 384:(r_ + 1) * 384, :], latb.t.ap(), reads=[latb], writes=[latg])
    else:
        P.collective("AllGather", GROUPS4, latb, latg)
    if getattr(k, "cut", 99) <= 2:
        return
    for pc in range(2):
        sq_ = next_slot(k)
        P.dma("pool", sq_[:, 0:4 * 768], I["mla_wq"][j, 0, pc], writes=[sq_])
        wq = sq_[:, 0:4 * 768].rearrange("p (c n) -> p c n", n=768)
        ss_ = next_slot(k)
        P.dma("pool", ss_[:, 0:4 * 768], I["mla_wq"][j, 1, pc], writes=[ss_])
        wqs = ss_[:, 0:4 * 768].rearrange("p (c n) -> p c n", n=768)
        for hl in range(8):
            hd = pc * 8 + hl
            b0 = next_bank(k)
            for c in range(4):
                P.op("pe", lambda hh, b0=b0, c=c, hl=hl, wq=wq: hh.matmul(b0[0:96, :], wq[:, c, hl * 96:(hl + 1) * 96], QN[c][0][:],
                     start=(c == 0), stop=(c == 3)), reads=[sq_, QN[c][0]], writes=[b0], sig=(c == 3))
            P.op("act", lambda hh, b0=b0, hd=hd: hh.activation(QT[0:96, hd, 0:HT], b0[0:96, :], AF.Copy), reads=[b0], writes=[QT], waw=False)
            b1 = next_bank(k)
            for c in range(4):
                P.op("pe", lambda hh, b1=b1, c=c, hl=hl, wq=wq: hh.matmul(b1[0:96, :], wq[:, c, hl * 96:(hl + 1) * 96], QN[c][1][:],
                     start=(c == 0), stop=(c == 3)), reads=[sq_, QN[c][1]], writes=[b1], sig=(c == 3))
            b2 = next_bank(k)
            for c in range(4):
                P.op("pe", lambda hh, b2=b2, c=c, hl=hl, wqs=wqs: hh.matmul(b2[0:96, :], wqs[:, c, hl * 96:(hl + 1) * 96], QN[c][1][:],
                     start=(c == 0), stop=(c == 3)), reads=[ss_, QN[c][1]], writes=[b2], sig=(c == 3))
            P.op("act", lambda hh, b1=b1, hd=hd: hh.activation(QT[0:64, hd, HT:TT], b1[0:64, :], AF.Copy), reads=[b1], writes=[QT], waw=False)
            ta, tb = t1[hl % 2], t2[hl % 2]
            P.op("dve", lambda hh, b1=b1, ta=ta: hh.tensor_tensor(ta[64:96, :], b1[64:96, :], cosr, ALU.mult),
                 reads=[b1, k.rope], writes=[ta])
            P.op("dve", lambda hh, b2=b2, tb=tb: hh.tensor_tensor(tb[64:96, :], b2[64:96, :], sinr, ALU.mult),
                 reads=[b2, k.rope], writes=[tb])
            P.op("dve", lambda hh, ta=ta, tb=tb, hd=hd: hh.tensor_tensor(QT[64:96, hd, HT:TT], ta[64:96, :], tb[64:96, :], ALU.add),
                 reads=[ta, tb], writes=[QT], waw=False)

    lg = latg.t.ap().rearrange("(r c p) n -> p c r n", r=4, c=3)
    for m in range(3):
        P.dma("sp", LAT[:, m, 0:2048].rearrange("p (r n) -> p r n", r=4), lg[:, m, :, :], reads=[latg], writes=[LAT])
    for m in range(2):
        P.dma("pool", LAT[:, m, 2048:NKS], I["cacheT"][j, m * 128:(m + 1) * 128, :], writes=[LAT])
    P.dma("pool", LAT[64:96, 2, 2048:NKS], I["cacheT"][j, 256:288, :], writes=[LAT])

    if getattr(k, "cut", 99) <= 3:
        return
    _mark(k, "attn")
    A.reset(keep, keep_tiles)
    KTs = [A.alloc([96, NKS], BF16, f"KTs{q}") for q in range(2)]
    KTp = [A.alloc([96, HT], BF16, f"KTp{q}") for q in range(2)]
    VEs = [A.alloc([128, NKC, 128], BF16, f"VEs{q}") for q in range(2)]
    VEp = [A.alloc([128, 4, 128], BF16, f"VEp{q}") for q in range(2)]
    PT = [A.alloc([128, HT], BF16, f"PT{q}") for q in range(5)]
    OT = [A.alloc([128, TT], BF16, f"OT{q}") for q in range(8)]
    rec = [A.alloc([128, HT], F32, f"rec{q}") for q in range(2)]
    for q in range(2):
        P.op("dve", lambda hh, q=q: hh.tensor_copy(KTs[q][64:96, :], LAT[64:96, 2, :]), reads=[LAT], writes=[KTs[q]])
        P.op("dve", lambda hh, q=q: hh.tensor_copy(KTp[q][64:96, :], KRb[64:96, 0:HT]), reads=[KRb], writes=[KTp[q]])
        oc = 64 if q == 0 else 0
        P.op("dve", lambda hh, q=q, oc=oc: hh.memset(VEs[q][:, :, oc:oc + 64], 1.0), writes=[VEs[q]])
        P.op("dve", lambda hh, q=q, oc=oc: hh.memset(VEp[q][:, :, oc:oc + 64], 1.0), writes=[VEp[q]])
    sk = next_slot(k)
    P.dma("pool", sk[:, 0:4096], I["mla_wkv"][j], writes=[sk])
    wkv = sk[:, 0:4096].rearrange("p (c n) -> p c n", n=2048)
    ncp = [0]

    def evac(dst_ap, src_ap, reads, writes, zview=None):
        ncp[0] += 1
        if ncp[0] % 2 == 0 or zview is None:
            P.op("act", lambda hh: hh.activation(dst_ap, src_ap, AF.Copy), reads=reads, writes=writes, waw=False)
        else:
            P.op("dve", lambda hh: hh.tensor_tensor(dst_ap, src_ap, zview, ALU.add), reads=list(reads) + [k.zer], writes=writes, waw=False)

    npt = [0]

    def attend(hd, KT, VE, kcs, q0, nq, Ob):
        LOOK = 3
        sbs = {}

        def s_mm(n_):
            kc = kcs[n_]
            Sb = next_bank(k)
            P.op("pe", lambda hh: hh.matmul(Sb[:, 0:nq], KT[0:96, kc * 128:(kc + 1) * 128], QT[0:96, hd, q0:q0 + nq],
                 start=True, stop=True), reads=[KT, QT], writes=[Sb])
            sbs[n_] = Sb

        for n_ in range(min(LOOK, len(kcs))):
            s_mm(n_)
        for n_, kc in enumerate(kcs):
            Sb = sbs.pop(n_)
            pt = PT[npt[0] % 5]
            npt[0] += 1
            P.op("act", lambda hh: hh.activation(pt[:, 0:nq], Sb[:, 0:nq], AF.Exp, scale=scale), reads=[Sb], writes=[pt])
            if n_ + LOOK < len(kcs):
                s_mm(n_ + LOOK)
            P.op("pe", lambda hh: hh.matmul(Ob[:, 0:nq], VE[:, kc, :], pt[:, 0:nq],
                 start=(n_ == 0), stop=(n_ == len(kcs) - 1)), reads=[VE, pt], writes=[Ob], sig=(n_ == len(kcs) - 1))

    def finish(hd, Ob, q0, nq):
        par = hd % 2
        o0, s0 = (0, 64) if par == 0 else (64, 0)
        r = rec[par]
        P.op("dve", lambda hh: hh.tensor_tensor(r[s0:s0 + 64, 0:nq], Ob[s0:s0 + 64, 0:nq], k.zer[s0:s0 + 64, 0:nq], ALU.add), reads=[Ob, k.zer], writes=[r])
        P.op("dve", lambda hh: hh.reciprocal(r[s0:s0 + 64, 0:nq], r[s0:s0 + 64, 0:nq]), reads=[r], writes=[r])
        P.op("dve", lambda hh: hh.tensor_tensor(OT[hd // 2][o0:o0 + 64, q0:q0 + nq], Ob[o0:o0 + 64, 0:nq], r[s0:s0 + 64, 0:nq], ALU.mult),
             reads=[Ob, r], writes=[OT[hd // 2]], waw=False)

    def build_kv(hd):
        par = hd % 2
        voff = 0 if par == 0 else 64
        kcol = hd * 128
        for sl in range(5):
            n = 512 if sl < 4 else 256
            b = next_bank(k)
            for c in range(2):
                P.op("pe", lambda hh, b=b, c=c, sl=sl, n=n: hh.matmul(b[0:64, 0:n], wkv[:, c, kcol:kcol + 64], LAT[:, c, sl * 512:sl * 512 + n],
                     start=(c == 0), stop=(c == 1)), reads=[sk, LAT], writes=[b], sig=(c == 1))
            evac(KTs[par][0:64, sl * 512:sl * 512 + n], b[0:64, 0:n], [b], [KTs[par]], k.zer[0:64, 0:n])
        b = next_bank(k)
        for c in range(2):
            P.op("pe", lambda hh, b=b, c=c: hh.matmul(b[0:64, :], wkv[:, c, kcol:kcol + 64], CKVb[c][0][:],
                 start=(c == 0), stop=(c == 1)), reads=[sk, CKVb[c][0]], writes=[b], sig=(c == 1))
        evac(KTp[par][0:64, :], b[0:64, :], [b], [KTp[par]], k.zer[0:64, :])
        for g0 in range(0, NKC, 8):
            ng = min(8, NKC - g0)
            b = next_bank(k)
            for q in range(ng):
                kc = g0 + q
                for c in range(2):
                    P.op("pe", lambda hh, b=b, c=c, kc=kc, q=q: hh.matmul(b[:, q * 64:(q + 1) * 64], LAT[:, c, kc * 128:(kc + 1) * 128],
                         wkv[:, c, kcol + 64:kcol + 128], start=(c == 0), stop=(c == 1)),
                         reads=[sk, LAT], writes=[b], sig=(c == 1 and q == ng - 1))
            evac(VEs[par][:, g0:g0 + ng, voff:voff + 64], b[:, 0:ng * 64].rearrange("p (q n) -> p q n", n=64), [b], [VEs[par]],
                 k.zer[:, 0:ng * 64].rearrange("p (q n) -> p q n", n=64))
        b = next_bank(k)
        for q in range(4):
            for c in range(2):
                P.op("pe", lambda hh, b=b, c=c, q=q: hh.matmul(b[:, q * 64:(q + 1) * 64], CKVb[c][0][:, q * 128:(q + 1) * 128],
                     wkv[:, c, kcol + 64:kcol + 128], start=(c == 0), stop=(c == 1)),
                     reads=[sk, CKVb[c][0]], writes=[b], sig=(c == 1 and q == 3))
        evac(VEp[par][:, 0:4, voff:voff + 64], b[:, 0:256].rearrange("p (q n) -> p q n", n=64), [b], [VEp[par]],
             k.zer[:, 0:256].rearrange("p (q n) -> p q n", n=64))
    def do_attn(hd):
        par = hd % 2
        Ob = acc_bank(k)
        attend(hd, KTs[par], VEs[par], list(range(NKC)), HT, HT, Ob)
        finish(hd, Ob, HT, HT)
        for s_ in range(2):
            Ob = acc_bank(k)
            attend(hd, KTp[par], VEp[par], [2 * s_, 2 * s_ + 1], s_ * 256, 256, Ob)
            finish(hd, Ob, s_ * 256, 256)

    build_kv(0)
    for hd in range(16):
        if hd + 1 < 16:
            build_kv(hd + 1)
        do_attn(hd)
    if getattr(k, "cut", 99) <= 5:
        return
    _mark(k, "wo")
    g5 = modv(k, i, 5)
    for half in range(2):
        so = next_slot(k)
        P.dma("pool", so[:, 0:4096], I["mla_wo"][j, half], writes=[so])
        wo = so[:, 0:4096].rearrange("p (h n) -> p h n", n=512)
        for dl in range(4):
            dc = half * 4 + dl
            for h in range(2):
                b = next_bank(k)
                for hp in range(8):
                    P.op("pe", lambda hh, b=b, hp=hp, dl=dl, h=h, wo=wo: hh.matmul(b[:], wo[:, hp, dl * 128:(dl + 1) * 128],
                         OT[hp][:, h * HT:(h + 1) * HT], start=(hp == 0), stop=(hp == 7)),
                         reads=[so, OT[hp]], writes=[b], sig=(hp == 7))
                xt = k.X[dc][h]
                P.op("dve", lambda hh, b=b, xt=xt, dc=dc, h=h: hh.scalar_tensor_tensor(
                    xt[:], b[:], g5[:, dc, h:h + 1], xt[:], ALU.mult, ALU.add), reads=[b, k.mod, xt], writes=[xt])


def ssm(k, i, j):
    hg = ssm_halo_prepass(k, i, j)
    k.arena.reset()
    for hf in range(2):
        ssm_half(k, i, j, hf, hg)
        k.arena.reset()


def ssm_halo_prepass(k, i, j):
    P, A, I = k.P, k.arena, k.I
    HN = adaln(k, i, 3, 4, 1, halves=(1,))
    eb = acc_bank(k)
    for pi in range(4, 10):
        s = next_slot(k)
        P.dma("pool", s[:, 0:NCH * 512], I["ssm_win"][j, pi], writes=[s])
        w = s[:, 0:NCH * 512].rearrange("p (c n) -> p c n", n=512)
        for m in range(4):
            ch = (pi - 4) * 4 + m
            for (o0, t0) in ((0, 0), (2, HT - 2)):
                for c in range(NCH):
                    P.op("pe", lambda hh, c=c, m=m, ch=ch, o0=o0, t0=t0, w=w: hh.matmul(
                        eb[:, ch * 4 + o0:ch * 4 + o0 + 2], w[:, c, m * 128:(m + 1) * 128], HN[c][1][:, t0:t0 + 2],
                        start=(c == 0), stop=(c == NCH - 1)), reads=[s, HN[c][1]], writes=[eb],
                        sig=(c == NCH - 1 and o0 == 2 and m == 3))
    EDGE = A.alloc([128, 96], F32, "EDGE")
    P.op("dve", lambda hh: hh.tensor_tensor(EDGE[:], eb[:, 0:96], k.zer[:, 0:96], ALU.add), reads=[eb, k.zer], writes=[EDGE])
    hb = P.dram(f"hb{j}", [128, 96], F32)
    hg = P.dram(f"hg{j}", [4 * 128, 96], F32)
    P.dma("sp", hb.t.ap(), EDGE[:], reads=[EDGE], writes=[hb])
    P.collective("AllGather", GROUPS4, hb, hg)
    return hg


def ssm_half(k, i, j, hf, hg):
    P, A, I = k.P, k.arena, k.I
    sm0 = j * 192
    convw = k.ssmall[:, sm0:sm0 + 120].rearrange("p (c w) -> p c w", w=5)
    convb = k.ssmall[:, sm0 + 120:sm0 + 144]
    ngs = k.ssmall[:, sm0 + 144:sm0 + 160]
    dd = k.ssmall[:, sm0 + 160:sm0 + 192].rearrange("p (c r) -> p c r", r=2)
    dtb_bc = k.sbc[:, j, 0, :]
    alog_bc = k.sbc[:, j, 1, :]
    triI = [k.cst[:, 0, :], k.cst[:, 1, :]]
    SLm = [k.cst[:, 2, :], k.cst[:, 3, :]]
    ones32 = k.ones32
    TCS = [slice(tc * 128, (tc + 1) * 128) for tc in range(4)]

    YT = A.alloc([128, 16, HT], F32, "YT")
    n_y = (A.off, list(A.tiles))
    XTOK = A.alloc([128, 4, 2048], BF16, "XTOK")
    BTOK = A.alloc([128, 4, 512], BF16, "BTOK")
    BT = [A.alloc([128, HT], BF16, f"BT{g}") for g in range(4)]
    CT = [A.alloc([128, HT], BF16, f"CT{g}") for g in range(4)]
    DT = [A.alloc([128, 64], F32, f"DT{t}") for t in range(4)]
    DTA = [A.alloc([128, 64], F32, f"DTA{t}") for t in range(4)]
    n_k = (A.off, list(A.tiles))
    HN = adaln(k, i, 3, 4, 1, halves=(hf,))
    dsum = A.alloc([128, 16], F32, "dsum")
    P.op("dve", lambda hh: hh.tensor_tensor(dsum[:], dd[:, :, 0], dd[:, :, 1], ALU.add), reads=[k.ssmall], writes=[dsum])
    NA = A.alloc([128, 64], F32, "NA")
    P.op("act", lambda hh: hh.activation(NA[:], alog_bc, AF.Exp), reads=[k.sbc], writes=[NA])
    P.op("dve", lambda hh: hh.tensor_scalar(NA[:], NA[:], -1.0, 0.0, ALU.mult, ALU.add), reads=[NA], writes=[NA])
    sdt = next_slot(k)
    P.dma("pool", sdt[:, 0:NCH * 64], I["ssm_wdt"][j], writes=[sdt])
    wdt = sdt[:, 0:NCH * 64].rearrange("p (c n) -> p c n", n=64)
    ut = [A.alloc([128, 64], F32, f"ut{q}") for q in range(2)]
    for tc in range(4):
        b = next_bank(k)
        for c in range(NCH):
            P.op("pe", lambda hh, b=b, c=c, tc=tc: hh.matmul(b[:, 0:64], HN[c][hf][:, TCS[tc]], wdt[:, c, :],
                 start=(c == 0), stop=(c == NCH - 1)), reads=[sdt, HN[c][hf]], writes=[b], sig=(c == NCH - 1))
        u = ut[tc % 2]
        P.op("dve", lambda hh, b=b, u=u: hh.tensor_tensor(u[:], b[:, 0:64], dtb_bc, ALU.add), reads=[b, k.sbc], writes=[u])
        P.op("act", lambda hh, u=u: hh.activation(u[:], u[:], AF.Exp), reads=[u], writes=[u])
        P.op("act", lambda hh, u=u, tc=tc: hh.activation(DT[tc][:], u[:], AF.Ln, bias=k.oneb[:], scale=1.0), reads=[u, k.oneb], writes=[DT[tc]])
        P.op("dve", lambda hh, tc=tc: hh.tensor_tensor(DTA[tc][:], DT[tc][:], NA[:], ALU.mult), reads=[DT[tc], NA], writes=[DTA[tc]])

    HALO = None
    if hf == 1:
        G = A.alloc([128, 4, 96], F32, "G")
        P.dma("sp", G[:], hg.t.ap().rearrange("(r p) n -> p r n", p=128), reads=[hg], writes=[G])
        HALO = A.alloc([128, 24, 4], F32, "HALO")
        Gv = G[:].rearrange("p r (c e) -> p r c e", e=4)
        for (dst, src, mo) in ((slice(0, 2), slice(2, 4), 8), (slice(2, 4), slice(0, 2), 12)):
            P.op("dve", lambda hh, dst=dst, src=src, mo=mo: hh.tensor_scalar(
                HALO[:, :, dst], Gv[:, 0, :, src], k.posm[:, mo:mo + 1], 0.0, ALU.mult, ALU.add),
                reads=[G, k.posm], writes=[HALO])
            for r in range(1, 4):
                P.op("dve", lambda hh, dst=dst, src=src, mo=mo, r=r: hh.scalar_tensor_tensor(
                    HALO[:, :, dst], Gv[:, r, :, src], k.posm[:, mo + r:mo + r + 1], HALO[:, :, dst], ALU.mult, ALU.add),
                    reads=[G, k.posm, HALO], writes=[HALO])

    _mark(k, "xbc")
    PRE = [A.alloc([128, 520], F32, f"PRE{q}") for q in range(3)]
    for q in range(3):
        P.op("dve", lambda hh, q=q: hh.memset(PRE[q][:], 0.0), writes=[PRE[q]])
    acc = [A.alloc([128, 516], F32, f"cacc{q}") for q in range(2)]
    sil = [A.alloc([128, HT], F32, f"sil{q}") for q in range(2)]
    XSr = [A.alloc([128, HT], BF16, f"XSr{q}") for q in range(3)]
    NU = 516 if hf == 0 else 512
    slots = {}

    def chunk_gen(pi, m, n):
        if pi not in slots:
            s_ = next_slot(k)
            P.dma("pool", s_[:, 0:NCH * 512], I["ssm_win"][j, pi], writes=[s_])
            slots[pi] = s_
        s = slots[pi]
        w = s[:, 0:NCH * 512].rearrange("p (c n) -> p c n", n=512)
        ch = (pi - 4) * 4 + m
        b = next_bank(k)
        for c in range(NCH):
            P.op("pe", lambda hh, b=b, c=c, m=m, w=w: hh.matmul(b[:], w[:, c, m * 128:(m + 1) * 128], HN[c][hf][:],
                 start=(c == 0), stop=(c == NCH - 1)), reads=[s, HN[c][hf]], writes=[b], sig=(c == NCH - 1))
        pre = PRE[n % 3]
        a = acc[n % 2]
        if hf == 0:
            P.op("act", lambda hh, b=b, pre=pre: hh.activation(pre[:, 2:258], b[:, 0:256], AF.Copy), reads=[b], writes=[pre])
            P.op("act", lambda hh, b=b, pre=pre: hh.activation(pre[:, 262:518], b[:, 256:512], AF.Copy), reads=[b], writes=[pre])
        else:
            P.op("act", lambda hh, b=b, pre=pre: hh.activation(pre[:, 2:514], b[:], AF.Copy), reads=[b], writes=[pre])
            P.op("dve", lambda hh, pre=pre, ch=ch: hh.tensor_copy(pre[:, 0:2], HALO[:, ch, 0:2]), reads=[HALO], writes=[pre])
            P.op("dve", lambda hh, pre=pre, ch=ch: hh.tensor_copy(pre[:, 514:516], HALO[:, ch, 2:4]), reads=[HALO], writes=[pre])
        yield
        P.op("dve", lambda hh, a=a, pre=pre, ch=ch: hh.tensor_scalar(
            a[:, 0:NU], pre[:, 0:NU], convw[:, ch, 0:1], 0.0, ALU.mult, ALU.add), reads=[pre, k.ssmall], writes=[a])
        yield
        for wi_ in range(1, 5):
            P.op("dve", lambda hh, a=a, pre=pre, ch=ch, wi_=wi_: hh.scalar_tensor_tensor(
                a[:, 0:NU], pre[:, wi_:wi_ + NU], convw[:, ch, wi_:wi_ + 1], a[:, 0:NU], ALU.mult, ALU.add),
                reads=[pre, k.ssmall, a], writes=[a])
            yield
        if ch < 16:
            dst, dt_ = sil[n % 2], sil[n % 2]
        elif ch < 20:
            dst = BT[ch - 16]
        else:
            dst = CT[ch - 20]
        segs = ((0, 0, 256), (256, 260, 256)) if hf == 0 else ((0, 0, 512),)
        for (o0, a0, ln) in segs:
            P.op("act", lambda hh, dst=dst, a=a, ch=ch, o0=o0, a0=a0, ln=ln: hh.activation(
                dst[:, o0:o0 + ln], a[:, a0:a0 + ln], AF.Silu, bias=convb[:, ch:ch + 1], scale=1.0),
                reads=[a, k.ssmall], writes=[dst])
        if ch < 16:
            P.op("dve", lambda hh, dst=dst, ch=ch: hh.tensor_scalar(
                YT[:, ch, :], dst[:], dsum[:, ch:ch + 1], 0.0, ALU.mult, ALU.add), reads=[dst, dsum], writes=[YT], waw=False)
            xs = XSr[n % 3]
            P.op("act", lambda hh, dst=dst, xs=xs: hh.activation(xs[:], dst[:], AF.Copy), reads=[dst], writes=[xs])
            src_t, tok, tcol = xs, XTOK, ch * 128
        elif ch < 20:
            src_t, tok, tcol = dst, BTOK, (ch - 16) * 128
        else:
            src_t = None
        if src_t is not None:
            tb = next_bank(k)
            tbv = tb[:, 0:256].bitcast(BF16)
            for tc in range(4):
                P.op("pe", lambda hh, tbv=tbv, tc=tc, src_t=src_t: hh.transpose(tbv[:, TCS[tc]], src_t[:, TCS[tc]], k.identb[:]),
                     reads=[src_t, k.identb], writes=[tb], sig=(tc == 3))
            P.op("act", lambda hh, tbv=tbv, tok=tok, tcol=tcol: hh.activation(
                tok[:, :, tcol:tcol + 128], tbv.rearrange("p (t n) -> p t n", n=128), AF.Copy), reads=[tb], writes=[tok], waw=False)

    pend = [(pi, m) for pi in range(4, 10) for m in range(4)]
    act_, n = [], 0
    while pend or act_:
        while pend and len(act_) < 2:
            pi_, m_ = pend.pop(0)
            act_.append(chunk_gen(pi_, m_, n))
            n += 1
        for g_ in list(act_):
            try:
                next(g_)
            except StopIteration:
                act_.remove(g_)

    _mark(k, "ssd")
    A.reset(n_k[0], n_k[1])
    H = A.alloc([128, 2048], F32, "H")
    HS = [T(H[:, q_ * 256:(q_ + 1) * 256], f"HS{q_}") for q_ in range(8)]
    Hb = A.alloc([128, 16, 2, 128], BF16, "Hb")
    XD = [A.alloc([128, 2, 2, 128], BF16, f"XD{q}") for q in range(2)]
    XW = [A.alloc([128, 256], BF16, f"XW{q}") for q in range(2)]
    R = [A.alloc([128, 8, 128], F32, f"R{q}") for q in range(2)]
    LT = [A.alloc([128, 4, 128], F32, f"LT{q}") for q in range(2)]
    EC = [A.alloc([128, 4, 128], F32, f"EC{q}") for q in range(2)]
    MT = [A.alloc([128, 4, 128], BF16, f"MT{q}") for q in range(2)]
    CW = [A.alloc([128, 4, 128], BF16, f"CW{q}") for q in range(2)]
    CBm = [A.alloc([128, 4, 128], F32, f"CBm{q}") for q in range(2)]
    W4 = [A.alloc([128, 4], F32, f"W4{q}") for q in range(2)]
    DBt = [A.alloc([128, 64], F32, f"DBt{q}") for q in range(2)]
    P.op("dve", lambda hh: hh.memset(Hb[:], 0.0), writes=[Hb])
    for q in range(2):
        P.op("dve", lambda hh, q=q: hh.memset(XD[q][:], 0.0), writes=[XD[q]])
    cnt = {"p": 0, "q": 0}

    def process(tc, d, state_only):
        last = 127 if d == 0 else 0
        if getattr(k, "extra", None) is not None:
            next(k.extra, None)
        pc = cnt["p"]
        cnt["p"] += 1
        tb_ = next_bank(k)
        P.op("pe", lambda hh: hh.matmul(tb_[:, 0:64], ones32[:], DTA[tc][:], start=True, stop=True),
             reads=[k.ones32, DTA[tc]], writes=[tb_])
        dbt = DBt[pc % 2]
        P.op("act", lambda hh: hh.activation(dbt[:], tb_[:, 0:64], AF.Exp), reads=[tb_], writes=[dbt])
        cbm = CBm[pc % 2]
        if not state_only:
            cb = next_bank(k)
            for g in range(4):
                P.op("pe", lambda hh, g=g: hh.matmul(cb[:, g * 128:(g + 1) * 128], BT[g][:, TCS[tc]], CT[g][:, TCS[tc]],
                     start=True, stop=True), reads=[BT[g], CT[g]], writes=[cb], sig=(g == 3))
            P.op("dve", lambda hh: hh.tensor_tensor(cbm[:], cb[:].rearrange("p (g n) -> p g n", n=128),
                 triI[d].unsqueeze(1).broadcast_to([128, 4, 128]), ALU.mult), reads=[cb, k.cst], writes=[cbm])
            Hv = H[:].rearrange("p (a w e) -> p a w e", w=2, e=64)
            P.op("act", lambda hh: hh.activation(Hb[:, :, 0, 0:64], Hv[:, :, 0, :], AF.Copy), reads=HS, writes=[Hb])
            P.op("act", lambda hh: hh.activation(Hb[:, :, 1, 64:128], Hv[:, :, 1, :], AF.Copy), reads=HS, writes=[Hb])
        def qiter(hg, q):
            r_ = R[hg % 2]
            c0 = d * 32 + hg * 8
            if q == 0:
                P.op("dve", lambda hh, r_=r_, c0=c0: hh.tensor_tensor(
                    r_[:], triI[d].unsqueeze(1).broadcast_to([128, 8, 128]),
                    DTA[tc][:, c0:c0 + 8].unsqueeze(2).broadcast_to([128, 8, 128]), ALU.mult),
                    reads=[k.cst, DTA[tc]], writes=[r_])
            yield
            qq = cnt["q"]
            cnt["q"] += 1
            h0 = hg * 8 + q * 4
            p0 = h0 // 2
            dcol = d * 32 + h0
            hv = H[:, h0 * 64:(h0 + 4) * 64].rearrange("p (a e) -> p a e", e=64)
            P.op("dve", lambda hh, hv=hv, dbt=dbt, dcol=dcol: hh.tensor_tensor(
                hv, hv, dbt[:, dcol:dcol + 4].unsqueeze(2).broadcast_to([128, 4, 64]), ALU.mult), reads=[HS[hg * 2 + q], dbt], writes=[HS[hg * 2 + q]])
            yield
            rr = r_[:, q * 4:(q + 1) * 4, :]
            bs = next_bank(k)
            P.op("pe", lambda hh, bs=bs, rr=rr: hh.matmul(bs[:], SLm[d], rr, start=True, stop=True),
                 reads=[k.cst, r_], writes=[bs])
            lt = LT[qq % 2]
            P.op("act", lambda hh, bs=bs, lt=lt: hh.activation(lt[:], bs[:].rearrange("p (a n) -> p a n", n=128), AF.Exp),
                 reads=[bs], writes=[lt])
            yield
            w4 = W4[qq % 2]
            dcol = d * 32 + h0
            P.op("dve", lambda hh, w4=w4, lt=lt, dcol=dcol: hh.tensor_tensor(
                w4[:], DT[tc][:, dcol:dcol + 4], lt[:, :, last], ALU.mult), reads=[DT[tc], lt], writes=[w4])
            yield
            xw = XW[qq % 2]
            xin = XTOK[:, tc, h0 * 64:(h0 + 4) * 64].rearrange("p (a e) -> p a e", e=64)
            P.op("dve", lambda hh, xw=xw, xin=xin, w4=w4: hh.tensor_tensor(
                xw[:].rearrange("p (a e) -> p a e", e=64), xin, w4[:].unsqueeze(2).broadcast_to([128, 4, 64]), ALU.mult),
                reads=[XTOK, w4], writes=[xw])
            yield
            if not state_only:
                bc = next_bank(k)
                P.op("pe", lambda hh, bc=bc, rr=rr: hh.matmul(bc[:], ones32[:], rr, start=True, stop=True),
                     reads=[k.ones32, r_], writes=[bc])
                ec = EC[qq % 2]
                P.op("act", lambda hh, bc=bc, ec=ec: hh.activation(ec[:], bc[:].rearrange("p (a n) -> p a n", n=128), AF.Exp),
                     reads=[bc], writes=[ec])
                yield
                mt = MT[qq % 2]
                P.op("dve", lambda hh, mt=mt, lt=lt, hg=hg: hh.tensor_tensor(
                    mt[:], lt[:], cbm[:, hg, :].unsqueeze(1).broadcast_to([128, 4, 128]), ALU.mult),
                    reads=[lt, cbm], writes=[mt])
                yield
                cw = CW[qq % 2]
                P.op("dve", lambda hh, cw=cw, ec=ec, hg=hg: hh.tensor_tensor(
                    cw[:], ec[:], CT[hg][:, TCS[tc]].unsqueeze(1).broadcast_to([128, 4, 128]), ALU.mult),
                    reads=[ec, CT[hg]], writes=[cw])
                yield
                xd = XD[qq % 2]
                xdv = xd[:].rearrange("p a w e -> p a (w e)").rearrange("p a (s e) -> p a s e", e=64)[:, :, 0:4:3, :]
                xin4 = XTOK[:, tc, h0 * 64:(h0 + 4) * 64].rearrange("p (a w e) -> p a w e", w=2, e=64)
                dtv = DT[tc][:, dcol:dcol + 4].rearrange("p (a w) -> p a w", w=2).unsqueeze(3).broadcast_to([128, 2, 2, 64])
                P.op("dve", lambda hh, xdv=xdv, xin4=xin4, dtv=dtv: hh.tensor_tensor(xdv, xin4, dtv, ALU.mult),
                     reads=[XTOK, DT[tc]], writes=[xd])
                yield
                yb = next_bank(k)
                for pp in range(2):
                    ops = ((xd[:, pp, 0, :], mt[:, 2 * pp, :]), (xd[:, pp, 1, :], mt[:, 2 * pp + 1, :]),
                           (Hb[:, p0 + pp, 0, :], cw[:, 2 * pp, :]), (Hb[:, p0 + pp, 1, :], cw[:, 2 * pp + 1, :]))
                    for n_, (l_, r2) in enumerate(ops):
                        P.op("pe", lambda hh, yb=yb, pp=pp, l_=l_, r2=r2, n_=n_: hh.matmul(
                            yb[:, pp * 128:(pp + 1) * 128], l_, r2, start=(n_ == 0), stop=(n_ == 3)),
                            reads=[xd, mt, Hb, cw], writes=[yb], sig=(n_ == 3 and pp == 1))
                yv = YT[:, p0:p0 + 2, TCS[tc]]
                P.op("dve", lambda hh, yv=yv, yb=yb: hh.tensor_tensor(
                    yv, yv, yb[:, 0:256].rearrange("p (a n) -> p a n", n=128), ALU.add), reads=[YT, yb], writes=[YT])
                yield
            sbk = next_bank(k)
            P.op("pe", lambda hh, sbk=sbk, xw=xw, hg=hg: hh.matmul(sbk[:, 0:256], BTOK[:, tc, hg * 128:(hg + 1) * 128], xw[:],
                 start=True, stop=True), reads=[BTOK, xw], writes=[sbk])
            P.op("dve", lambda hh, hv=hv, sbk=sbk: hh.tensor_tensor(
                hv, hv, sbk[:, 0:256].rearrange("p (a e) -> p a e", e=64), ALU.add), reads=[HS[hg * 2 + q], sbk], writes=[HS[hg * 2 + q]])

        pending = [(hg, q) for hg in range(4) for q in range(2)]
        active = []
        while pending or active:
            while pending and len(active) < 2:
                active.append(qiter(*pending.pop(0)))
            for g_ in list(active):
                try:
                    next(g_)
                except StopIteration:
                    active.remove(g_)

    if hf == 0:
        for d in range(2):
            for s_ in range(2):
                P.op("dve", lambda hh: hh.memset(H[:], 0.0), writes=HS)
                order = (2 * s_, 2 * s_ + 1) if d == 0 else (2 * s_ + 1, 2 * s_)
                for tc in order:
                    process(tc, d, False)
                P.dma("sp", k.O["stateO"][j, s_, d], H[:], reads=HS)
    else:
        SLD = A.alloc([128, 2048], F32, "SLD")
        TLD = A.alloc([128, 4, 64], F32, "TLD")
        TL = A.alloc([128, 64], F32, "TL")
        EE = A.alloc([128, 32], F32, "EE")
        sbn = [P.dram(f"sbn{j}_{d}", [128, 2048], F32) for d in range(2)]
        sgg = [P.dram(f"sgg{j}_{d}", [4 * 128, 2048], F32) for d in range(2)]
        tlbd = P.dram(f"tlb{j}", [128, 64], F32)
        tlgd = P.dram(f"tlg{j}", [4 * 128, 64], F32)
        tlb = acc_bank(k)
        for tc in range(4):
            P.op("pe", lambda hh, tc=tc: hh.matmul(tlb[:, 0:64], ones32[:], DTA[tc][:], start=(tc == 0), stop=(tc == 3)),
                 reads=[k.ones32, DTA[tc]], writes=[tlb])
        P.op("dve", lambda hh: hh.tensor_tensor(TL[:], tlb[:, 0:64], k.zer[:, 0:64], ALU.add), reads=[tlb, k.zer], writes=[TL])
        P.dma("sp", tlbd.t.ap(), TL[:], reads=[TL], writes=[tlbd])
        P.collective("AllGather", GROUPS4, tlbd, tlgd)
        _mark(k, "prepass")
        for d in range(2):
            P.op("dve", lambda hh: hh.memset(H[:], 0.0), writes=HS)
            for tc in ((0, 1, 2, 3) if d == 0 else (3, 2, 1, 0)):
                process(tc, d, True)
            P.dma("sp", sbn[d].t.ap(), H[:], reads=HS, writes=[sbn[d]])
            P.collective("AllGather", GROUPS4, sbn[d], sgg[d])
        _mark(k, "fold")
        P.dma("sp", TLD[:], tlgd.t.ap().rearrange("(r p) n -> p r n", p=128), reads=[tlgd], writes=[TLD])
        for d in range(2):
            sgv = sgg[d].t.ap()
            P.dma("sp", H[:], I["stateT"][j, d], writes=HS)
            for r in ((0, 1, 2, 3) if d == 0 else (3, 2, 1, 0)):
                mcol = (0 if d == 0 else 4) + r
                mk = k.posm[:, mcol:mcol + 1]
                P.dma("sp", SLD[:], sgv[r * 128:(r + 1) * 128, :], reads=[sgg[d]], writes=[SLD])
                P.op("act", lambda hh, r=r, d=d, mk=mk: hh.activation(EE[:], TLD[:, r, d * 32:(d + 1) * 32], AF.Exp, scale=mk),
                     reads=[TLD, k.posm], writes=[EE])
                hv = H[:].rearrange("p (a e) -> p a e", e=64)
                P.op("dve", lambda hh, hv=hv: hh.tensor_tensor(hv, hv, EE[:].unsqueeze(2).broadcast_to([128, 32, 64]), ALU.mult),
                     reads=HS + [EE], writes=HS)
                P.op("dve", lambda hh, mk=mk: hh.scalar_tensor_tensor(H[:], SLD[:], mk, H[:], ALU.mult, ALU.add),
                     reads=[SLD, k.posm] + HS, writes=HS)
            for tc in ((0, 1, 2, 3) if d == 0 else (3, 2, 1, 0)):
                process(tc, d, False)

    _mark(k, "gate")
    A.reset(n_y[0], n_y[1])
    HN = adaln(k, i, 3, 4, 1, halves=(hf,))
    sz = [A.alloc([128, HT], F32, f"sz{q}") for q in range(2)]
    YN = [A.alloc([128, HT], BF16, f"YN{c}") for c in range(16)]
    rs = A.alloc([128, HT], F32, "rsy")
    stb = acc_bank(k)
    n = 0
    for pz in range(4):
        s = next_slot(k)
        P.dma("pool", s[:, 0:NCH * 512], I["ssm_win"][j, pz], writes=[s])
        w = s[:, 0:NCH * 512].rearrange("p (c n) -> p c n", n=512)
        for m in range(4):
            ch = pz * 4 + m
            b = next_bank(k)
            for c in range(NCH):
                P.op("pe", lambda hh, b=b, c=c, m=m, w=w: hh.matmul(b[:], w[:, c, m * 128:(m + 1) * 128], HN[c][hf][:],
                     start=(c == 0), stop=(c == NCH - 1)), reads=[s, HN[c][hf]], writes=[b], sig=(c == NCH - 1))
            z = sz[n % 2]
            n += 1
            P.op("act", lambda hh, z=z, b=b: hh.activation(z[:], b[:], AF.Silu), reads=[b], writes=[z])
            P.op("dve", lambda hh, z=z, ch=ch: hh.tensor_tensor(YT[:, ch, :], YT[:, ch, :], z[:], ALU.mult), reads=[YT, z], writes=[YT])
            sq = k.sq[ch % 2]
            P.op("act", lambda hh, sq=sq, ch=ch: hh.activation(sq[:], YT[:, ch, :], AF.Square), reads=[YT], writes=[sq])
            P.op("pe", lambda hh, sq=sq, ch=ch: hh.matmul(stb[:], k.ones[:], sq[:], start=(ch == 0), stop=(ch == 15)),
                 reads=[sq, k.ones], writes=[stb])
    rsqrt_from_bank(k, stb, rs, 2048)
    for ch in range(16):
        P.op("dve", lambda hh, ch=ch: hh.scalar_tensor_tensor(
            YN[ch][:], YT[:, ch, :], ngs[:, ch:ch + 1], rs[:], ALU.mult, ALU.mult), reads=[YT, k.ssmall, rs], writes=[YN[ch]])
    g5 = modv(k, i, 5)
    for pw in range(4):
        s = next_slot(k)
        P.dma("pool", s[:, 0:4096], I["ssm_wout"][j, pw], writes=[s])
        w = s[:, 0:4096].rearrange("p (c n) -> p c n", n=256)
        for dl in range(2):
            dc = pw * 2 + dl
            b = next_bank(k)
            for ch in range(16):
                P.op("pe", lambda hh, b=b, ch=ch, dl=dl, w=w: hh.matmul(b[:], w[:, ch, dl * 128:(dl + 1) * 128], YN[ch][:],
                     start=(ch == 0), stop=(ch == 15)), reads=[s, YN[ch]], writes=[b], sig=(ch == 15))
            xt = k.X[dc][hf]
            P.op("dve", lambda hh, b=b, xt=xt, dc=dc: hh.scalar_tensor_tensor(
                xt[:], b[:], g5[:, dc, hf:hf + 1], xt[:], ALU.mult, ALU.add), reads=[b, k.mod, xt], writes=[xt])


def final_norm(k):
    P = k.P
    A = k.arena
    k.sq = [A.alloc([128, HT], BF16, f"sq{j}") for j in range(2)]
    rstd = [A.alloc([128, HT], F32, f"rstd{h}") for h in range(2)]
    yo = [A.alloc([128, HT], F32, f"yo{j}") for j in range(4)]
    n = 0
    outs = []
    for h in range(2):
        rms_stats(k, h, rstd[h])
        for c in range(NCH):
            y = yo[n % 4]
            n += 1
            xt = k.X[c][h]
            P.op("dve", lambda hh, y=y, xt=xt, c=c, h=h: hh.scalar_tensor_tensor(
                y[:], xt[:], k.fnormg[:, c:c + 1], rstd[h][:], ALU.mult, ALU.mult),
                reads=[xt, k.fnormg, rstd[h]], writes=[y])
            P.dma("sp", k.O["yT"][:, c, h * HT:(h + 1) * HT], y[:], reads=[y])
    P.wait_all("sp", yo + k.out_tiles)


def _fm(v):
    v = np.asarray(v, np.float32)
    lead = v.shape[:-1]
    a = v.reshape(lead + (NCH, 128))
    a = np.moveaxis(a, -1, 0)
    return np.ascontiguousarray(a)


def host_prep(inp):
    f32 = np.float32
    sh = {}
    mw = np.asarray(inp["mod_w"], f32).reshape(DEPTH, NCH, 128, 9, 2, 512)
    sh["modw"] = np.ascontiguousarray(mw.transpose(0, 3, 4, 2, 1, 5)).reshape(DEPTH, 9, 2, 128, NCH * 512)
    mb = np.asarray(inp["mod_b"], f32).reshape(DEPTH, 9, NCH, 128)
    sh["modb"] = np.ascontiguousarray(mb.transpose(3, 0, 1, 2)).reshape(128, DEPTH * 9 * NCH)
    ng = np.asarray(inp["norm_g"], f32).reshape(DEPTH, 3, NCH, 128)
    sh["normg"] = np.ascontiguousarray(ng.transpose(3, 0, 1, 2)).reshape(128, DEPTH * 3 * NCH)
    sh["fnormg"] = np.ascontiguousarray(np.asarray(inp["final_norm_g"], f32).reshape(NCH, 128).T)
    wi = np.asarray(inp["ffn_w_in"], f32).reshape(DEPTH, 2, NCH, 128, 2, 11, 256)
    sh["ffn_in"] = np.ascontiguousarray(wi.transpose(0, 1, 5, 3, 2, 4, 6)).reshape(DEPTH, 2, 11, 128, NCH * 512)
    wo = np.asarray(inp["ffn_w_out"], f32).reshape(DEPTH, 2, NF, 128, NCH, 128)
    sh["ffn_out"] = np.ascontiguousarray(wo.transpose(0, 1, 4, 3, 2, 5)).reshape(DEPTH, 2, NCH, 128, NF * 128)
    perm = np.arange(32) ^ 8
    wi_ = np.asarray(inp["mla_w_in"], f32)
    def pcn(w):
        rows, n = w.shape
        return np.ascontiguousarray(w.reshape(rows // 128, 128, n).transpose(1, 0, 2)).reshape(128, (rows // 128) * n)
    sh["mla_w1"] = np.stack([pcn(wi_[j][:, 0:512]) for j in range(2)])
    sh["mla_w2"] = np.stack([pcn(np.concatenate([wi_[j][:, 512:800], wi_[j][:, 704:768], wi_[j][:, 768 + perm]], 1)) for j in range(2)])
    wq_ = np.asarray(inp["mla_wq_b"], f32)
    colsw = np.arange(1536).reshape(16, 96).copy()
    colsw[:, 64:96] = colsw[:, 64 + perm]
    colsw = colsw.reshape(-1)
    sh["mla_wq"] = np.stack([np.stack([np.stack([pcn(w[:, pc * 768:(pc + 1) * 768]) for pc in range(2)])
                                       for w in (wq_[j], wq_[j][:, colsw])]) for j in range(2)])
    sh["mla_wkv"] = np.stack([pcn(np.asarray(inp["mla_wkv_b"], f32)[j]) for j in range(2)])
    wo_ = np.asarray(inp["mla_wo"], f32)
    sh["mla_wo"] = np.stack([np.stack([pcn(wo_[j][:, hf * 512:(hf + 1) * 512]) for hf in range(2)]) for j in range(2)])
    qn_ = np.asarray(inp["mla_q_norm"], f32).reshape(2, 4, 128)
    kn_ = np.asarray(inp["mla_kv_norm"], f32).reshape(2, 2, 128)
    sh["mla_small"] = np.ascontiguousarray(np.concatenate([qn_, kn_], 1).transpose(2, 0, 1)).reshape(128, 12)
    sw = np.asarray(inp["ssm_w_in"], f32)
    sh["ssm_win"] = np.stack([np.stack([pcn(sw[j][:, q * 512:(q + 1) * 512]) for q in range(10)]) for j in range(2)])
    sh["ssm_wdt"] = np.stack([pcn(sw[j][:, 5120:5184]) for j in range(2)])
    so_ = np.asarray(inp["ssm_w_out"], f32)
    sh["ssm_wout"] = np.stack([np.stack([pcn(so_[j][:, q * 256:(q + 1) * 256]) for q in range(4)]) for j in range(2)])
    small = np.zeros((128, 2, 192), f32)
    cw_ = np.asarray(inp["ssm_conv_w"], f32)
    cb_ = np.asarray(inp["ssm_conv_b"], f32)
    sg_ = np.asarray(inp["ssm_norm_g"], f32)
    sd_ = np.asarray(inp["ssm_d"], f32)
    for j in range(2):
        small[:, j, 0:120] = cw_[j].reshape(5, 24, 128).transpose(2, 1, 0).reshape(128, 120)
        small[:, j, 120:144] = cb_[j].reshape(24, 128).T
        small[:, j, 144:160] = sg_[j].reshape(16, 128).T
        hidx = (np.arange(16)[None, :] * 2 + (np.arange(128)[:, None] // 64))
        small[:, j, 160:192] = np.stack([sd_[j, 0][hidx], sd_[j, 1][hidx]], -1).reshape(128, 32)
    sh["ssm_small"] = small.reshape(128, 384)
    bc = np.stack([np.asarray(inp["ssm_dt_bias"], f32).reshape(2, 64), np.asarray(inp["ssm_a_log"], f32).reshape(2, 64)], 1)
    sh["ssm_bc"] = np.ascontiguousarray(np.broadcast_to(bc[None], (128, 2, 2, 64)))
    ii = np.arange(128)
    cst = np.zeros((128, 5, 128), f32)
    cst[:, 0] = (ii[:, None] <= ii[None, :])
    cst[:, 1] = (ii[:, None] >= ii[None, :])
    cst[:, 2] = (ii[:, None] > ii[None, :])
    cst[:, 3] = (ii[:, None] < ii[None, :])
    cst[:, 4] = np.eye(128)
    sh["consts"] = cst
    sst = np.asarray(inp["state_ssm"], f32)
    cache = np.asarray(inp["cache_mla"], f32)
    freqs = 1.0 / (10000.0 ** (np.arange(0, 16, 2, dtype=np.float32) / 16.0))
    xp = np.asarray(inp["x_prompt"], f32)
    xs = np.asarray(inp["x_sample"], f32)
    c = np.asarray(inp["c"], f32)
    cc = np.asarray(inp["c_ctx"], f32)
    per = []
    for r in range(NCORES):
        gi, kq = r // 4, r % 4
        tok = np.concatenate([xp[2 * r], xp[2 * r + 1], xs[gi, kq * HT:(kq + 1) * HT]], 0)
        xT = np.ascontiguousarray(tok.T.reshape(NCH, 128, TT).transpose(1, 0, 2))
        cv = np.stack([cc, c[gi]], -1).reshape(NCH, 128, 2).transpose(1, 0, 2)
        d = dict(sh)
        d["xT"] = xT
        d["cvec"] = np.ascontiguousarray(cv)
        tg = kq * HT + np.arange(HT)
        pos = np.stack([tg // 64, tg % 64], 0).astype(np.float32)
        rope = np.zeros((96, 2, HT), np.float32)
        for ax in range(2):
            for hf in range(2):
                for fr in range(8):
                    f = ax * 16 + hf * 8 + fr
                    ang = (pos[ax] * freqs[fr]).astype(np.float32)
                    rope[64 + f, 0] = np.cos(ang)
                    rope[64 + f, 1] = np.sin(ang) * (-1.0 if hf == 0 else 1.0)
        d["ropeT"] = rope
        d["cacheT"] = np.ascontiguousarray(cache[gi].transpose(0, 2, 1))
        pm = np.zeros((128, 16), f32)
        for r_ in range(4):
            pm[:, r_] = float(r_ < kq)
            pm[:, 4 + r_] = float(r_ > kq)
            pm[:, 8 + r_] = float(r_ == kq - 1)
            pm[:, 12 + r_] = float(r_ == kq + 1)
        d["posm"] = pm
        d["stateT"] = np.ascontiguousarray(sst[gi].reshape(2, 2, 2048, 128).transpose(0, 1, 3, 2))
        per.append(d)
    return per


_NC_CACHE = {}


def run_device(inp, stage=99):
    if stage not in _NC_CACHE:
        _NC_CACHE[stage] = build_program(stage)
    nc = _NC_CACHE[stage]
    per = host_prep(inp)
    per = [{n: d[n] for n in nc._in_names} for d in per]
    res = run_bass_kernel_spmd(nc, per, core_ids=list(range(NCORES)))
    return res.results


def kernel(**inputs):
    return kernel_stage(inputs, 99)


def kernel_stage(inputs, stage):
    res = run_device(inputs, stage)
    B, S = 16, 256
    yp = np.zeros((B, S, D), np.float32)
    ys = np.zeros((2, 2048, D), np.float32)
    for r in range(NCORES):
        yT = res[r]["yT"]
        tok = yT.transpose(2, 1, 0).reshape(TT, D)
        yp[2 * r] = tok[0:256]
        yp[2 * r + 1] = tok[256:512]
        gi, kq = r // 4, r % 4
        ys[gi, kq * HT:(kq + 1) * HT] = tok[512:1024]
    nc_ = np.zeros((B, 2, S, 288), np.float32)
    for r in range(NCORES):
        co = res[r]["cacheO"]
        for s_ in range(2):
            nc_[2 * r + s_] = co[:, :, s_ * 256:(s_ + 1) * 256].transpose(0, 2, 1)
    ns_ = np.zeros((B, 2, 2, 32, 64, 128), np.float32)
    for r in range(NCORES):
        so = res[r]["stateO"]
        for s_ in range(2):
            ns_[2 * r + s_] = so[:, s_].transpose(0, 1, 3, 2).reshape(2, 2, 32, 64, 128)
    return yp, ys, nc_, ns_
```

```python
import numpy as np
from contextlib import ExitStack
import concourse.bass as bass
import concourse.mybir as mybir

F32 = mybir.dt.float32
BF16 = mybir.dt.bfloat16
AF = mybir.ActivationFunctionType
ALU = mybir.AluOpType
AX = mybir.AxisListType

EPOCH = 12000


class T:
    __slots__ = ("t", "name", "w", "rd", "dsem", "dcnt", "uid", "psum")
    _n = [0]

    def __init__(self, t, name="v"):
        T._n[0] += 1
        self.uid = T._n[0]
        self.t = t
        self.name = name
        self.w = []
        self.rd = []
        self.dsem = None
        self.dcnt = 0
        self.psum = False

    def __getitem__(self, idx):
        return self.t[idx]


class _Rec:
    def __init__(self):
        self.call = None

    def __getattr__(self, name):
        def f(*a, **kw):
            self.call = (name, a, kw)
            return self
        return f


class Prog:
    ENG = ("pe", "dve", "act", "pool", "sp")

    def __init__(self, nc, stack):
        self.nc = nc
        self.stack = stack
        self.h = {"pe": nc.tensor, "dve": nc.vector, "act": nc.scalar, "pool": nc.gpsimd, "sp": nc.sync}
        self.streams = {e: [] for e in self.ENG}
        self.count = {e: 0 for e in self.ENG}
        self.pending = {e: False for e in self.ENG}
        self.esems = {e: [] for e in self.ENG}
        self.seen = {e: {} for e in self.ENG}
        self.nsem = 0

    def sem(self, name):
        self.nsem += 1
        return self.stack.enter_context(self.nc.semaphore(f"{name}_{self.nsem}"))

    def sb(self, name, shape, dt=F32):
        return T(self.stack.enter_context(self.nc.sbuf_tensor(name, list(shape), dt)), name)

    def ps(self, name, shape, dt=F32):
        t = T(self.stack.enter_context(self.nc.psum_tensor(name, list(shape), dt)), name)
        t.psum = True
        return t

    def dram(self, name, shape, dt=F32, kind="Internal"):
        return T(self.nc.dram_tensor(name, list(shape), dt, kind=kind), name)

    def _esem(self, e, ep):
        while len(self.esems[e]) <= ep:
            self.esems[e].append(self.sem(f"c_{e}_{len(self.esems[e])}"))
        return self.esems[e][ep]

    def _wait(self, eng, ev):
        if ev[0] == "e":
            _, src, idx = ev
            ep, v = (idx - 1) // EPOCH, (idx - 1) % EPOCH + 1
            key = (src, ep)
            sem = self._esem(src, ep)
        else:
            _, sem, v, key = ev
        if self.seen[eng].get(key, 0) >= v:
            return
        if ev[0] == "e":
            for pe in range(ep):
                self.seen[eng][(src, pe)] = EPOCH
        self.seen[eng][key] = v
        self.streams[eng].append(lambda h, sem=sem, v=v: h.wait_ge(sem, v))

    def _deps(self, eng, reads, writes, same_eng_raw=True, waw=True):
        evs = []
        for t in reads:
            evs += t.w
            if t.psum:
                evs += [e for e in t.rd if not (e[0] == "e" and e[1] == eng)]
        for t in writes:
            if waw or t.psum:
                evs += t.w
            evs += t.rd
        for ev in evs:
            if ev[0] == "e" and ev[1] == eng:
                if eng == "pe" or not same_eng_raw:
                    continue
            self._wait(eng, ev)

    def op(self, eng, fn, reads=(), writes=(), sig=True, waw=True):
        reads = [r for r in reads if r is not None]
        writes = [w for w in writes if w is not None]
        self._deps(eng, reads, writes, waw=waw)
        rec = _Rec()
        fn(rec)
        name, a, kw = rec.call
        if sig:
            self.count[eng] += 1
            idx = self.count[eng]
            ep = (idx - 1) // EPOCH
            sem = self._esem(eng, ep)
            self.streams[eng].append(lambda h, name=name, a=a, kw=kw, sem=sem: getattr(h, name)(*a, **kw).then_inc(sem, 1))
            self.pending[eng] = False
        else:
            idx = self.count[eng] + 1
            self.streams[eng].append(lambda h, name=name, a=a, kw=kw: getattr(h, name)(*a, **kw))
            self.pending[eng] = True
        ev = ("e", eng, idx)
        for t in writes:
            if waw or t.psum or t.rd:
                t.w = [ev]
            else:
                t.w = self._compact(t.w + [ev]) if len(t.w) > 12 else t.w + [ev]
            t.rd = []
        for t in reads:
            if t not in writes:
                t.rd.append(ev)
                if len(t.rd) > 24:
                    t.rd = self._compact(t.rd)
        return ev

    @staticmethod
    def _compact(evs):
        best = {}
        out = []
        for ev in evs:
            if ev[0] == "e":
                k = ev[1]
                if k not in best or best[k][2] < ev[2]:
                    best[k] = ev
            else:
                k = ev[3]
                if k not in best or best[k][2] < ev[2]:
                    best[k] = ev
        return list(best.values())

    def dma(self, q, out_ap, in_ap, reads=(), writes=(), sem_tile=None, **kw):
        reads = [r for r in reads if r is not None]
        writes = [w for w in writes if w is not None]
        st = sem_tile if sem_tile is not None else (writes[0] if writes else reads[0])
        cls = "sw" if q == "pool" else "hw"
        if st.dsem is None:
            st.dsem = {}
            st.dcnt = {}
        if cls not in st.dsem:
            st.dsem[cls] = self.sem("d_" + st.name)
            st.dcnt[cls] = 0
        self._deps(q, reads, writes, same_eng_raw=True)
        st.dcnt[cls] += 1
        v = 16 * st.dcnt[cls]
        sem = st.dsem[cls]
        self.streams[q].append(
            lambda h, o=out_ap, i=in_ap, sem=sem, kw=kw: h.dma_start(out=o, in_=i, **kw).then_inc(sem, 16))
        ev = ("d", sem, v, ("d", st.uid, cls))
        for t in writes:
            t.w = [e for e in t.w if e[0] == "d" and e[3][1] == st.uid and e[3] != ev[3]] + [ev]
            t.rd = []
        for t in reads:
            if t not in writes:
                t.rd = [e for e in t.rd if not (e[0] == "d" and e[3] == ev[3])] + [ev]
        return ev

    def collective(self, kind, groups, src, dst):
        self._deps("pool", [src], [dst])
        if dst.dsem is None:
            dst.dsem = {}
            dst.dcnt = {}
        if "cc" not in dst.dsem:
            dst.dsem["cc"] = self.sem("cc_" + dst.name)
            dst.dcnt["cc"] = 0
        dst.dcnt["cc"] += 1
        v = dst.dcnt["cc"]
        sem = dst.dsem["cc"]
        sa, da = src.t.ap().opt(), dst.t.ap().opt()
        self.streams["pool"].append(lambda h: h.collective_compute(
            kind, ALU.bypass, replica_groups=groups, ins=[sa], outs=[da]).then_inc(sem))
        ev = ("d", sem, v, ("d", dst.uid, "cc"))
        dst.w = [ev]
        dst.rd = []
        src.rd.append(ev)
        return ev

    def barrier(self, engs=("pe", "dve", "act", "sp"), tiles=()):
        for e in engs:
            for src in self.ENG:
                if src == e or self.count[src] == 0:
                    continue
                self._wait(e, ("e", src, self.count[src]))
            for t in tiles:
                for ev in t.w + t.rd:
                    if ev[0] == "d":
                        self._wait(e, ev)

    def wait_all(self, eng, tiles):
        for t in tiles:
            for ev in t.w + t.rd:
                self._wait(eng, ev)

    def emit(self):
        nc = self.nc
        with nc.Block() as block:
            def mk(e):
                def body(h):
                    for f in self.streams[e]:
                        f(h)
                return body
            block.tensor(mk("pe"))
            block.vector(mk("dve"))
            block.scalar(mk("act"))
            block.gpsimd(mk("pool"))
            block.sync(mk("sp"))

from concourse.bass_utils import run_bass_kernel_spmd

import math

NCORES = 8
D = 1024
TT = 1024
HT = 512
NCH = 8
DFF = 2816
NF = 22
EPS = 1e-6
DEPTH = 4


class Arena:
    def __init__(self, P, words):
        self.P = P
        self.words = words
        self.t = P.stack.enter_context(P.nc.sbuf_tensor("arena", [128, words], F32))
        self.off = 0
        self.tiles = []

    def alloc(self, shape, dt=F32, name="a"):
        n = 1
        for s in shape[1:]:
            n *= s
        w = n if dt == F32 else (n + 1) // 2
        assert self.off + w <= self.words, ("arena overflow", name, self.off, w, self.words)
        ap = self.t[0:shape[0], self.off:self.off + w]
        if dt != F32:
            ap = ap.bitcast(dt)
        if len(shape) > 2:
            names = " ".join(f"d{i}" for i in range(len(shape) - 1))
            kw = {f"d{i}": shape[i + 1] for i in range(len(shape) - 1)}
            ap = ap.rearrange(f"p ({names}) -> p {names}", **kw)
        self.off += w
        t = T(ap, name)
        self.tiles.append(t)
        return t

    def reset(self, keep=0, keep_tiles=(), pool=False):
        engs = ("pe", "dve", "act", "sp") + (("pool",) if pool else ())
        self.P.barrier(engs=engs, tiles=self.tiles)
        self.off = keep
        self.tiles = list(keep_tiles)


class K:
    pass


def build_program(stage=99):
    nc = bass.Bass("TRN2", target_bir_lowering=False)

    def din(name, shape, dt=F32):
        return nc.dram_tensor(name, list(shape), dt, kind="ExternalInput").ap()

    def dout(name, shape, dt=F32):
        return nc.dram_tensor(name, list(shape), dt, kind="ExternalOutput").ap()

    I = {}
    I["xT"] = din("xT", [128, NCH, TT])
    I["cvec"] = din("cvec", [128, NCH, 2])
    if stage >= 0:
        I["modw"] = din("modw", [DEPTH, 9, 2, 128, NCH * 512])
    I["modb"] = din("modb", [128, DEPTH * 9 * NCH])
    I["normg"] = din("normg", [128, DEPTH * 3 * NCH])
    I["fnormg"] = din("fnormg", [128, NCH])
    if stage >= 0:
        I["ffn_in"] = din("ffn_in", [DEPTH, 2, 11, 128, NCH * 512])
        I["ffn_out"] = din("ffn_out", [DEPTH, 2, NCH, 128, NF * 128])
    I["mla_w1"] = din("mla_w1", [2, 128, NCH * 512])
    I["mla_w2"] = din("mla_w2", [2, 128, NCH * 384])
    I["mla_wq"] = din("mla_wq", [2, 2, 2, 128, 4 * 768])
    I["mla_wkv"] = din("mla_wkv", [2, 128, 2 * 2048])
    I["mla_wo"] = din("mla_wo", [2, 2, 128, 8 * 512])
    I["mla_small"] = din("mla_small", [128, 2 * 6])
    I["ropeT"] = din("ropeT", [96, 2, HT])
    I["cacheT"] = din("cacheT", [2, 288, 256])
    I["ssm_win"] = din("ssm_win", [2, 10, 128, NCH * 512])
    I["ssm_wdt"] = din("ssm_wdt", [2, 128, NCH * 64])
    I["ssm_wout"] = din("ssm_wout", [2, 4, 128, 16 * 256])
    I["ssm_small"] = din("ssm_small", [128, 2 * 192])
    I["ssm_bc"] = din("ssm_bc", [128, 2, 2, 64])
    I["consts"] = din("consts", [128, 5, 128])
    I["posm"] = din("posm", [128, 16])
    I["stateT"] = din("stateT", [2, 2, 128, 2048])
    O = {}
    O["stateO"] = dout("stateO", [2, 2, 2, 128, 2048])
    O["yT"] = dout("yT", [128, NCH, TT])
    O["cacheO"] = dout("cacheO", [2, 288, HT])

    with ExitStack() as st:
        P = Prog(nc, st)
        k = K()
        k.P, k.nc, k.I, k.O = P, nc, I, O
        Xt = st.enter_context(nc.sbuf_tensor("X", [128, NCH, TT], F32))
        k.X = [[T(Xt[:, c, h * HT:(h + 1) * HT], f"X{c}{h}") for h in range(2)] for c in range(NCH)]
        k.Xall = [k.X[c][h] for c in range(NCH) for h in range(2)]
        k.wslots = [P.sb(f"ws{i}", [128, 4096], BF16) for i in range(4)]
        k.wi = 0
        k.out_tiles = []
        k.banks = [P.ps(f"bk{i}", [128, 512], F32) for i in range(8)]
        k.bi = 0
        k.ri = 0
        k.ones = P.sb("ones", [128, 128], BF16)
        k.mod = P.sb("s_mod", [128, DEPTH * 9 * NCH * 2], F32)
        k.modb = P.sb("s_modb", [128, DEPTH * 9 * NCH], F32)
        k.normg = P.sb("s_normg", [128, DEPTH * 3 * NCH], F32)
        k.fnormg = P.sb("s_fnormg", [128, NCH], F32)
        k.cv = P.sb("cv", [128, NCH, 2], F32)
        k.scb = P.sb("scb", [128, NCH, 2], BF16)
        k.gsc = P.sb("gsc", [128, NCH, 2], F32)
        k.hgate = P.sb("hgate", [128, NCH, 2], F32)
        k.arena = Arena(P, 29 * 1024)
        k.ssmall = P.sb("s_ssmall", [128, 384], F32)
        k.sbc = P.sb("s_sbc", [128, 2, 2, 64], F32)
        k.cst = P.sb("s_cst", [128, 5, 128], F32)
        k.posm = P.sb("s_posm", [128, 16], F32)
        k.ones32 = P.sb("ones32", [128, 128], F32)
        k.identb = P.sb("identb", [128, 128], BF16)
        k.oneb = P.sb("oneb", [128, 1], F32)
        k.zer = P.sb("zer", [128, 512], F32)
        P.op("dve", lambda h: h.memset(k.zer[:], 0.0), writes=[k.zer])
        P.dma("sp", k.ssmall[:], I["ssm_small"], writes=[k.ssmall])
        P.dma("sp", k.sbc[:], I["ssm_bc"], writes=[k.sbc])
        P.dma("sp", k.cst[:], I["consts"], writes=[k.cst])
        P.dma("sp", k.posm[:], I["posm"], writes=[k.posm])
        P.op("dve", lambda h: h.memset(k.ones32[:], 1.0), writes=[k.ones32])
        P.op("dve", lambda h: h.memset(k.oneb[:], 1.0), writes=[k.oneb])
        P.op("dve", lambda h: h.tensor_copy(k.identb[:], k.cst[:, 4, :]), reads=[k.cst], writes=[k.identb])
        k.msmall = P.sb("s_msmall", [128, 12], F32)
        k.rope = P.sb("s_rope", [96, 2, HT], F32)
        k.epsb = P.sb("epsb", [128, 1], F32)
        P.op("dve", lambda h: h.memset(k.epsb[:], EPS), writes=[k.epsb])
        P.dma("sp", k.msmall[:], I["mla_small"], writes=[k.msmall])
        P.dma("sp", k.rope[64:96, :, :], I["ropeT"][64:96, :, :], writes=[k.rope])

        P.dma("sp", Xt[:], I["xT"], writes=k.Xall)
        P.dma("sp", k.modb[:], I["modb"], writes=[k.modb])
        P.dma("sp", k.normg[:], I["normg"], writes=[k.normg])
        P.dma("sp", k.fnormg[:], I["fnormg"], writes=[k.fnormg])
        P.dma("sp", k.cv[:], I["cvec"], writes=[k.cv])
        P.op("dve", lambda h: h.memset(k.ones[:], 1.0), writes=[k.ones])
        P.op("act", lambda h: h.activation(k.scb[:], k.cv[:], AF.Silu), reads=[k.cv], writes=[k.scb])

        if stage < 0:
            import os
            k.cut = float(os.environ.get("MLA_CUT", "99"))
            P.op("dve", lambda h: h.memset(k.mod[:], 0.01), writes=[k.mod])
            if stage == -1:
                mla(k, 0, 0)
            else:
                ssm(k, 1, 0)
            k.arena.reset()
        for i in range(DEPTH if stage >= 0 else 0):
            if i == 0:
                modulation(k, 0)
            ffn(k, i, 0)
            k.arena.reset()
            if stage <= 1:
                break
            if i % 2 == 0:
                mla(k, i, i // 2)
            else:
                ssm(k, i, i // 2)
            k.arena.reset()
            if stage == 2 + 2 * i:
                break
            nxt = modulation_gen(k, i + 1) if i + 1 < DEPTH else None
            ffn(k, i, 1, extra=nxt)
            if nxt is not None:
                for _ in nxt:
                    pass
            k.arena.reset()
        final_norm(k)
        P.emit()
    nc._in_names = list(I.keys())
    return nc


def _mark(k, name):
    pass


def next_bank(k):
    b = k.banks[k.bi % 6]
    k.bi += 1
    return b


def acc_bank(k):
    b = k.banks[6 + k.ri % 2]
    k.ri += 1
    return b


def next_slot(k):
    s = k.wslots[k.wi % 4]
    k.wi += 1
    return s


def modv(k, i, kk):
    base = ((i * 9 + kk) * NCH) * 2
    return k.mod[:, base:base + NCH * 2].rearrange("p (c r) -> p c r", r=2)


def modulation(k, i):
    for _ in modulation_gen(k, i):
        pass


def modulation_gen(k, i):
    P = k.P
    for kk in range(9):
        for hc in range(2):
            yield
            s = next_slot(k)
            P.dma("pool", s[:, 0:NCH * 512], k.I["modw"][i, kk, hc], writes=[s])
            w = s[:, 0:NCH * 512].rearrange("p (c n) -> p c n", n=512)
            b = next_bank(k)
            for m in range(4):
                for c in range(NCH):
                    P.op("pe", lambda h, m=m, c=c, b=b, w=w: h.matmul(
                        b[:, m * 2:m * 2 + 2], w[:, c, m * 128:(m + 1) * 128], k.scb[:, c, :],
                        start=(c == 0), stop=(c == NCH - 1)),
                        reads=[s, k.scb], writes=[b], sig=(c == NCH - 1))
            base = (i * 9 + kk) * NCH + hc * 4
            ob = base * 2
            P.op("dve", lambda h, b=b, base=base, ob=ob: h.tensor_tensor(
                k.mod[:, ob:ob + 8].rearrange("p (c r) -> p c r", r=2),
                b[:, 0:8].rearrange("p (c r) -> p c r", r=2),
                k.modb[:, base:base + 4].unsqueeze(2).broadcast_to([128, 4, 2]), ALU.add),
                reads=[b, k.modb], writes=[k.mod])


def rsqrt_from_bank(k, b, out_rstd, n):
    P = k.P
    P.op("act", lambda hh: hh.activation(out_rstd[:], b[:], AF.Ln, bias=k.epsb[:], scale=1.0 / n),
         reads=[b, k.epsb], writes=[out_rstd])
    P.op("act", lambda hh: hh.activation(out_rstd[:], out_rstd[:], AF.Exp, scale=-0.5),
         reads=[out_rstd], writes=[out_rstd])


def rms_stats(k, h, out_rstd):
    P = k.P
    A = k.arena
    b = next_bank(k)
    for c in range(NCH):
        sq = k.sq[c % 2]
        xt = k.X[c][h]
        P.op("act", lambda hh, sq=sq, xt=xt: hh.activation(sq[:], xt[:], AF.Square), reads=[xt], writes=[sq])
        P.op("pe", lambda hh, sq=sq, b=b, c=c: hh.matmul(b[:], k.ones[:], sq[:], start=(c == 0), stop=(c == NCH - 1)),
             reads=[sq, k.ones], writes=[b], sig=True)
    P.op("act", lambda hh: hh.activation(out_rstd[:], b[:], AF.Ln, bias=k.epsb[:], scale=1.0 / D),
         reads=[b, k.epsb], writes=[out_rstd])
    P.op("act", lambda hh: hh.activation(out_rstd[:], out_rstd[:], AF.Exp, scale=-0.5),
         reads=[out_rstd], writes=[out_rstd])


def adaln(k, i, ksh, ksc, ng, halves=(0, 1)):
    P = k.P
    A = k.arena
    k.sq = [A.alloc([128, HT], BF16, f"sq{j}") for j in range(2)]
    rstd = [A.alloc([128, HT], F32, f"rstd{h}") if h in halves else None for h in range(2)]
    tmp = [A.alloc([128, HT], F32, f"ntmp{j}") for j in range(2)]
    k.last_rstd, k.last_tmp = rstd, tmp
    HN = [[A.alloc([128, HT], BF16, f"hn{c}{h}") if h in halves else None for h in range(2)] for c in range(NCH)]
    gb = (i * 3 + ng) * NCH
    sc = modv(k, i, ksc)
    sh = modv(k, i, ksh)
    P.op("dve", lambda h: h.scalar_tensor_tensor(
        k.gsc[:], sc, 1.0, k.normg[:, gb:gb + NCH].unsqueeze(2).broadcast_to([128, NCH, 2]), ALU.add, ALU.mult),
        reads=[k.mod, k.normg], writes=[k.gsc])
    for h in halves:
        rms_stats(k, h, rstd[h])
    for h in halves:
        for c in range(NCH):
            t = tmp[c % 2]
            xt = k.X[c][h]
            P.op("dve", lambda hh, t=t, xt=xt, c=c, h=h: hh.scalar_tensor_tensor(
                t[:], xt[:], k.gsc[:, c, h:h + 1], rstd[h][:], ALU.mult, ALU.mult),
                reads=[xt, k.gsc, rstd[h]], writes=[t])
            P.op("act", lambda hh, t=t, c=c, h=h: hh.activation(
                HN[c][h][:], t[:], AF.Identity, bias=sh[:, c, h:h + 1], scale=1.0),
                reads=[t, k.mod], writes=[HN[c][h]])
    return HN


def ffn(k, i, j, extra=None):
    P = k.P
    A = k.arena
    k3 = 0 if j == 0 else 6
    HN = adaln(k, i, k3 + 0, k3 + 1, 0 if j == 0 else 2)
    ACTT = [[A.alloc([128, HT], BF16, f"act{f}{h}") for h in range(2)] for f in range(NF)]
    sg = [A.alloc([128, HT], F32, f"sg{j2}") for j2 in range(2)]
    gt = modv(k, i, k3 + 2)
    P.op("dve", lambda h: h.tensor_scalar(k.hgate[:], gt, 0.5, 0.0, ALU.mult, ALU.add), reads=[k.mod], writes=[k.hgate])
    n = 0
    for g in range(11):
        if extra is not None:
            next(extra, None)
        s = next_slot(k)
        P.dma("pool", s[:, 0:NCH * 512], k.I["ffn_in"][i, j, g], writes=[s])
        w = s[:, 0:NCH * 512].rearrange("p (c n) -> p c n", n=512)
        for m in range(2):
            f = 2 * g + m
            for h in range(2):
                ba = next_bank(k)
                bb = next_bank(k)
                for (bk, co) in ((ba, m * 128), (bb, 256 + m * 128)):
                    for c in range(NCH):
                        P.op("pe", lambda hh, bk=bk, co=co, c=c, h=h, w=w: hh.matmul(
                            bk[:], w[:, c, co:co + 128], HN[c][h][:], start=(c == 0), stop=(c == NCH - 1)),
                            reads=[s, HN[c][h]], writes=[bk], sig=(c == NCH - 1))
                sgt = sg[n % 2]
                n += 1
                P.op("act", lambda hh, sgt=sgt, ba=ba: hh.activation(sgt[:], ba[:], AF.Silu), reads=[ba], writes=[sgt])
                P.op("dve", lambda hh, sgt=sgt, bb=bb, f=f, h=h: hh.tensor_tensor(
                    ACTT[f][h][:], sgt[:], bb[:], ALU.mult), reads=[sgt, bb], writes=[ACTT[f][h]])
    for dc in range(NCH):
        if extra is not None:
            next(extra, None)
        s = next_slot(k)
        P.dma("pool", s[:, 0:NF * 128], k.I["ffn_out"][i, j, dc], writes=[s])
        w = s[:, 0:NF * 128].rearrange("p (f n) -> p f n", n=128)
        for h in range(2):
            b = next_bank(k)
            for f in range(NF):
                P.op("pe", lambda hh, b=b, f=f, h=h, w=w: hh.matmul(
                    b[:], w[:, f, :], ACTT[f][h][:], start=(f == 0), stop=(f == NF - 1)),
                    reads=[s, ACTT[f][h]], writes=[b], sig=(f == NF - 1))
            xt = k.X[dc][h]
            P.op("dve", lambda hh, b=b, xt=xt, dc=dc, h=h: hh.scalar_tensor_tensor(
                xt[:], b[:], k.hgate[:, dc, h:h + 1], xt[:], ALU.mult, ALU.add),
                reads=[b, k.hgate, xt], writes=[xt])


GROUPS4 = [[0, 1, 2, 3], [4, 5, 6, 7]]
NKS = 2304
NKC = 18


def mla(k, i, j):
    P, A, I = k.P, k.arena, k.I
    scale = 1.0 / math.sqrt(96.0)
    QT = A.alloc([96, 16, TT], BF16, "QT")
    CKVb = [[A.alloc([128, HT], BF16, f"ckvb{m}{h}") for h in range(2)] for m in range(2)]
    KRb = A.alloc([128, TT], BF16, "KRb")
    P.op("dve", lambda hh: hh.memset(KRb[:], 0.0), writes=[KRb])
    LAT = A.alloc([128, 3, NKS], BF16, "LAT")
    keep, keep_tiles = A.off, list(A.tiles)
    HN = adaln(k, i, 3, 4, 1)
    QA = [[A.alloc([128, HT], F32, f"qa{m}{h}") for h in range(2)] for m in range(4)]
    QN = [[A.alloc([128, HT], BF16, f"qn{m}{h}") for h in range(2)] for m in range(4)]
    CKVf = [[A.alloc([128, HT], F32, f"ckvf{m}{h}") for h in range(2)] for m in range(2)]
    KRf = A.alloc([96, HT], F32, "KRf")
    rq = k.last_rstd
    rk = k.last_rstd
    t1 = [k.last_tmp[0], A.alloc([96, HT], F32, "t1b")]
    t2 = [k.last_tmp[1], A.alloc([96, HT], F32, "t2b")]
    qn = k.msmall[:, j * 6:j * 6 + 4]
    kvn = k.msmall[:, j * 6 + 4:j * 6 + 6]
    cosr = k.rope[64:96, 0, :]
    sinr = k.rope[64:96, 1, :]

    s1 = next_slot(k)
    P.dma("pool", s1[:, 0:NCH * 512], I["mla_w1"][j], writes=[s1])
    w1 = s1[:, 0:NCH * 512].rearrange("p (c n) -> p c n", n=512)
    s2 = next_slot(k)
    P.dma("pool", s2[:, 0:NCH * 384], I["mla_w2"][j], writes=[s2])
    w2 = s2[:, 0:NCH * 384].rearrange("p (c n) -> p c n", n=384)

    def proj_norm(ws, w, col0, nm, raw, rstd, nfeat):
        for h in range(2):
            sb = acc_bank(k)
            for m in range(nm):
                b = next_bank(k)
                for c in range(NCH):
                    P.op("pe", lambda hh, b=b, c=c, m=m, h=h: hh.matmul(
                        b[:], w[:, c, col0 + m * 128:col0 + (m + 1) * 128], HN[c][h][:],
                        start=(c == 0), stop=(c == NCH - 1)), reads=[ws, HN[c][h]], writes=[b], sig=(c == NCH - 1))
                sq = k.sq[m % 2]
                P.op("act", lambda hh, sq=sq, b=b: hh.activation(sq[:], b[:], AF.Square), reads=[b], writes=[sq])
                P.op("pe", lambda hh, sq=sq, sb=sb, m=m: hh.matmul(sb[:], k.ones[:], sq[:], start=(m == 0), stop=(m == nm - 1)),
                     reads=[sq, k.ones], writes=[sb])
                P.op("dve", lambda hh, b=b, m=m, h=h: hh.tensor_tensor(raw[m][h][:], b[:], k.zer[:], ALU.add), reads=[b, k.zer], writes=[raw[m][h]])
            rsqrt_from_bank(k, sb, rstd[h], nfeat)

    if getattr(k, "cut", 99) <= -4:
        return
    proj_norm(s1, w1, 0, 4, QA, rq, 512)
    if getattr(k, "cut", 99) <= -3.5:
        return
    for h in range(2):
        for m in range(4):
            P.op("dve", lambda hh, m=m, h=h: hh.scalar_tensor_tensor(
                QN[m][h][:], QA[m][h][:], qn[:, m:m + 1], rq[h][:], ALU.mult, ALU.mult),
                reads=[QA[m][h], k.msmall, rq[h]], writes=[QN[m][h]])
    if getattr(k, "cut", 99) <= -3:
        return
    KVA = CKVf
    proj_norm(s2, w2, 0, 2, KVA, rk, 256)
    for h in range(2):
        for m in range(2):
            P.op("dve", lambda hh, m=m, h=h: hh.scalar_tensor_tensor(
                CKVf[m][h][:], KVA[m][h][:], kvn[:, m:m + 1], rk[h][:], ALU.mult, ALU.mult),
                reads=[KVA[m][h], k.msmall, rk[h]], writes=[CKVf[m][h]])
            P.op("act", lambda hh, m=m, h=h: hh.activation(CKVb[m][h][:], CKVf[m][h][:], AF.Copy),
                 reads=[CKVf[m][h]], writes=[CKVb[m][h]])
    if getattr(k, "cut", 99) <= -2:
        return
    for h in range(2):
        if getattr(k, "cut", 99) <= -1 and h == 1:
            return
        bk = next_bank(k)
        for c in range(NCH):
            P.op("pe", lambda hh, bk=bk, c=c, h=h: hh.matmul(bk[0:96, :], w2[:, c, 192:288], HN[c][h][:],
                 start=(c == 0), stop=(c == NCH - 1)), reads=[s2, HN[c][h]], writes=[bk], sig=(c == NCH - 1))
        if h == 0:
            P.op("dve", lambda hh, bk=bk: hh.tensor_tensor(KRf[64:96, :], bk[64:96, :], k.zer[64:96, :], ALU.add), reads=[bk, k.zer], writes=[KRf])
            P.op("act", lambda hh, bk=bk: hh.activation(KRb[64:96, 0:HT], bk[64:96, :], AF.Copy), reads=[bk], writes=[KRb])
        else:
            bs = next_bank(k)
            for c in range(NCH):
                P.op("pe", lambda hh, bs=bs, c=c, h=h: hh.matmul(bs[0:96, :], w2[:, c, 288:384], HN[c][h][:],
                     start=(c == 0), stop=(c == NCH - 1)), reads=[s2, HN[c][h]], writes=[bs], sig=(c == NCH - 1))
            P.op("dve", lambda hh, bk=bk: hh.tensor_tensor(t1[0][64:96, :], bk[64:96, :], cosr, ALU.mult),
                 reads=[bk, k.rope], writes=[t1[0]])
            P.op("dve", lambda hh, bs=bs: hh.tensor_tensor(t2[0][64:96, :], bs[64:96, :], sinr, ALU.mult),
                 reads=[bs, k.rope], writes=[t2[0]])
            P.op("dve", lambda hh: hh.tensor_tensor(KRb[64:96, HT:TT], t1[0][64:96, :], t2[0][64:96, :], ALU.add),
                 reads=[t1[0], t2[0]], writes=[KRb])
    if getattr(k, "cut", 99) <= 0:
        return
    for m in range(2):
        P.dma("sp", k.O["cacheO"][j, m * 128:(m + 1) * 128, :], CKVf[m][0][:], reads=[CKVf[m][0]])
    P.dma("sp", k.O["cacheO"][j, 256:288, :], KRf[64:96, :], reads=[KRf])
    if getattr(k, "cut", 99) <= 1:
        return
    latb = P.dram(f"latb{j}", [384, HT], BF16)
    latg = P.dram(f"latg{j}", [4 * 384, HT], BF16)
    for m in range(2):
        P.dma("sp", latb.t.ap()[m * 128:(m + 1) * 128, :], CKVb[m][1][:], reads=[CKVb[m][1]], writes=[latb])
    P.dma("sp", latb.t.ap()[256:384, :], KRb[:, HT:TT], reads=[KRb], writes=[latb])
    import os
    if os.environ.get("NO_CC"):
        P.dma("sp", latg.t.ap()[0:384, :], latb.t.ap(), reads=[latb], writes=[latg])
        for r_ in range(1, 4):
            P.dma("sp", latg.t.ap()[r_ * 384:(r_ + 1) * 384, :], latb.t.ap(), reads=[latb], writes=[latg])
    else:
        P.collective("AllGather", GROUPS4, latb, latg)
    if getattr(k, "cut", 99) <= 2:
        return
    for pc in range(2):
        sq_ = next_slot(k)
        P.dma("pool", sq_[:, 0:4 * 768], I["mla_wq"][j, 0, pc], writes=[sq_])
        wq = sq_[:, 0:4 * 768].rearrange("p (c n) -> p c n", n=768)
        ss_ = next_slot(k)
        P.dma("pool", ss_[:, 0:4 * 768], I["mla_wq"][j, 1, pc], writes=[ss_])
        wqs = ss_[:, 0:4 * 768].rearrange("p (c n) -> p c n", n=768)
        for hl in range(8):
            hd = pc * 8 + hl
            b0 = next_bank(k)
            for c in range(4):
                P.op("pe", lambda hh, b0=b0, c=c, hl=hl, wq=wq: hh.matmul(b0[0:96, :], wq[:, c, hl * 96:(hl + 1) * 96], QN[c][0][:],
                     start=(c == 0), stop=(c == 3)), reads=[sq_, QN[c][0]], writes=[b0], sig=(c == 3))
            P.op("act", lambda hh, b0=b0, hd=hd: hh.activation(QT[0:96, hd, 0:HT], b0[0:96, :], AF.Copy), reads=[b0], writes=[QT], waw=False)
            b1 = next_bank(k)
            for c in range(4):
                P.op("pe", lambda hh, b1=b1, c=c, hl=hl, wq=wq: hh.matmul(b1[0:96, :], wq[:, c, hl * 96:(hl + 1) * 96], QN[c][1][:],
                     start=(c == 0), stop=(c == 3)), reads=[sq_, QN[c][1]], writes=[b1], sig=(c == 3))
            b2 = next_bank(k)
            for c in range(4):
                P.op("pe", lambda hh, b2=b2, c=c, hl=hl, wqs=wqs: hh.matmul(b2[0:96, :], wqs[:, c, hl * 96:(hl + 1) * 96], QN[c][1][:],
                     start=(c == 0), stop=(c == 3)), reads=[ss_, QN[c][1]], writes=[b2], sig=(c == 3))
            P.op("act", lambda hh, b1=b1, hd=hd: hh.activation(QT[0:64, hd, HT:TT], b1[0:64, :], AF.Copy), reads=[b1], writes=[QT], waw=False)
            ta, tb = t1[hl % 2], t2[hl % 2]
            P.op("dve", lambda hh, b1=b1, ta=ta: hh.tensor_tensor(ta[64:96, :], b1[64:96, :], cosr, ALU.mult),
                 reads=[b1, k.rope], writes=[ta])
            P.op("dve", lambda hh, b2=b2, tb=tb: hh.tensor_tensor(tb[64:96, :], b2[64:96, :], sinr, ALU.mult),
                 reads=[b2, k.rope], writes=[tb])
            P.op("dve", lambda hh, ta=ta, tb=tb, hd=hd: hh.tensor_tensor(QT[64:96, hd, HT:TT], ta[64:96, :], tb[64:96, :], ALU.add),
                 reads=[ta, tb], writes=[QT], waw=False)

    lg = latg.t.ap().rearrange("(r c p) n -> p c r n", r=4, c=3)
    for m in range(3):
        P.dma("sp", LAT[:, m, 0:2048].rearrange("p (r n) -> p r n", r=4), lg[:, m, :, :], reads=[latg], writes=[LAT])
    for m in range(2):
        P.dma("pool", LAT[:, m, 2048:NKS], I["cacheT"][j, m * 128:(m + 1) * 128, :], writes=[LAT])
    P.dma("pool", LAT[64:96, 2, 2048:NKS], I["cacheT"][j, 256:288, :], writes=[LAT])

    if getattr(k, "cut", 99) <= 3:
        return
    _mark(k, "attn")
    A.reset(keep, keep_tiles)
    KTs = [A.alloc([96, NKS], BF16, f"KTs{q}") for q in range(2)]
    KTp = [A.alloc([96, HT], BF16, f"KTp{q}") for q in range(2)]
    VEs = [A.alloc([128, NKC, 128], BF16, f"VEs{q}") for q in range(2)]
    VEp = [A.alloc([128, 4, 128], BF16, f"VEp{q}") for q in range(2)]
    PT = [A.alloc([128, HT], BF16, f"PT{q}") for q in range(5)]
    OT = [A.alloc([128, TT], BF16, f"OT{q}") for q in range(8)]
    rec = [A.alloc([128, HT], F32, f"rec{q}") for q in range(2)]
    for q in range(2):
        P.op("dve", lambda hh, q=q: hh.tensor_copy(KTs[q][64:96, :], LAT[64:96, 2, :]), reads=[LAT], writes=[KTs[q]])
        P.op("dve", lambda hh, q=q: hh.tensor_copy(KTp[q][64:96, :], KRb[64:96, 0:HT]), reads=[KRb], writes=[KTp[q]])
        oc = 64 if q == 0 else 0
        P.op("dve", lambda hh, q=q, oc=oc: hh.memset(VEs[q][:, :, oc:oc + 64], 1.0), writes=[VEs[q]])
        P.op("dve", lambda hh, q=q, oc=oc: hh.memset(VEp[q][:, :, oc:oc + 64], 1.0), writes=[VEp[q]])
    sk = next_slot(k)
    P.dma("pool", sk[:, 0:4096], I["mla_wkv"][j], writes=[sk])
    wkv = sk[:, 0:4096].rearrange("p (c n) -> p c n", n=2048)
    ncp = [0]

    def evac(dst_ap, src_ap, reads, writes, zview=None):
        ncp[0] += 1
        if ncp[0] % 2 == 0 or zview is None:
            P.op("act", lambda hh: hh.activation(dst_ap, src_ap, AF.Copy), reads=reads, writes=writes, waw=False)
        else:
            P.op("dve", lambda hh: hh.tensor_tensor(dst_ap, src_ap, zview, ALU.add), reads=list(reads) + [k.zer], writes=writes, waw=False)

    npt = [0]

    def attend(hd, KT, VE, kcs, q0, nq, Ob):
        LOOK = 3
        sbs = {}

        def s_mm(n_):
            kc = kcs[n_]
            Sb = next_bank(k)
            P.op("pe", lambda hh: hh.matmul(Sb[:, 0:nq], KT[0:96, kc * 128:(kc + 1) * 128], QT[0:96, hd, q0:q0 + nq],
                 start=True, stop=True), reads=[KT, QT], writes=[Sb])
            sbs[n_] = Sb

        for n_ in range(min(LOOK, len(kcs))):
            s_mm(n_)
        for n_, kc in enumerate(kcs):
            Sb = sbs.pop(n_)
            pt = PT[npt[0] % 5]
            npt[0] += 1
            P.op("act", lambda hh: hh.activation(pt[:, 0:nq], Sb[:, 0:nq], AF.Exp, scale=scale), reads=[Sb], writes=[pt])
            if n_ + LOOK < len(kcs):
                s_mm(n_ + LOOK)
            P.op("pe", lambda hh: hh.matmul(Ob[:, 0:nq], VE[:, kc, :], pt[:, 0:nq],
                 start=(n_ == 0), stop=(n_ == len(kcs) - 1)), reads=[VE, pt], writes=[Ob], sig=(n_ == len(kcs) - 1))

    def finish(hd, Ob, q0, nq):
        par = hd % 2
        o0, s0 = (0, 64) if par == 0 else (64, 0)
        r = rec[par]
        P.op("dve", lambda hh: hh.tensor_tensor(r[s0:s0 + 64, 0:nq], Ob[s0:s0 + 64, 0:nq], k.zer[s0:s0 + 64, 0:nq], ALU.add), reads=[Ob, k.zer], writes=[r])
        P.op("dve", lambda hh: hh.reciprocal(r[s0:s0 + 64, 0:nq], r[s0:s0 + 64, 0:nq]), reads=[r], writes=[r])
        P.op("dve", lambda hh: hh.tensor_tensor(OT[hd // 2][o0:o0 + 64, q0:q0 + nq], Ob[o0:o0 + 64, 0:nq], r[s0:s0 + 64, 0:nq], ALU.mult),
             reads=[Ob, r], writes=[OT[hd // 2]], waw=False)

    def build_kv(hd):
        par = hd % 2
        voff = 0 if par == 0 else 64
        kcol = hd * 128
        for sl in range(5):
            n = 512 if sl < 4 else 256
            b = next_bank(k)
            for c in range(2):
                P.op("pe", lambda hh, b=b, c=c, sl=sl, n=n: hh.matmul(b[0:64, 0:n], wkv[:, c, kcol:kcol + 64], LAT[:, c, sl * 512:sl * 512 + n],
                     start=(c == 0), stop=(c == 1)), reads=[sk, LAT], writes=[b], sig=(c == 1))
            evac(KTs[par][0:64, sl * 512:sl * 512 + n], b[0:64, 0:n], [b], [KTs[par]], k.zer[0:64, 0:n])
        b = next_bank(k)
        for c in range(2):
            P.op("pe", lambda hh, b=b, c=c: hh.matmul(b[0:64, :], wkv[:, c, kcol:kcol + 64], CKVb[c][0][:],
                 start=(c == 0), stop=(c == 1)), reads=[sk, CKVb[c][0]], writes=[b], sig=(c == 1))
        evac(KTp[par][0:64, :], b[0:64, :], [b], [KTp[par]], k.zer[0:64, :])
        for g0 in range(0, NKC, 8):
            ng = min(8, NKC - g0)
            b = next_bank(k)
            for q in range(ng):
                kc = g0 + q
                for c in range(2):
                    P.op("pe", lambda hh, b=b, c=c, kc=kc, q=q: hh.matmul(b[:, q * 64:(q + 1) * 64], LAT[:, c, kc * 128:(kc + 1) * 128],
                         wkv[:, c, kcol + 64:kcol + 128], start=(c == 0), stop=(c == 1)),
                         reads=[sk, LAT], writes=[b], sig=(c == 1 and q == ng - 1))
            evac(VEs[par][:, g0:g0 + ng, voff:voff + 64], b[:, 0:ng * 64].rearrange("p (q n) -> p q n", n=64), [b], [VEs[par]],
                 k.zer[:, 0:ng * 64].rearrange("p (q n) -> p q n", n=64))
        b = next_bank(k)
        for q in range(4):
            for c in range(2):
                P.op("pe", lambda hh, b=b, c=c, q=q: hh.matmul(b[:, q * 64:(q + 1) * 64], CKVb[c][0][:, q * 128:(q + 1) * 128],
                     wkv[:, c, kcol + 64:kcol + 128], start=(c == 0), stop=(c == 1)),
                     reads=[sk, CKVb[c][0]], writes=[b], sig=(c == 1 and q == 3))
        evac(VEp[par][:, 0:4, voff:voff + 64], b[:, 0:256].rearrange("p (q n) -> p q n", n=64), [b], [VEp[par]],
             k.zer[:, 0:256].rearrange("p (q n) -> p q n", n=64))
    def do_attn(hd):
        par = hd % 2
        Ob = acc_bank(k)
        attend(hd, KTs[par], VEs[par], list(range(NKC)), HT, HT, Ob)
        finish(hd, Ob, HT, HT)
        for s_ in range(2):
            Ob = acc_bank(k)
            attend(hd, KTp[par], VEp[par], [2 * s_, 2 * s_ + 1], s_ * 256, 256, Ob)
            finish(hd, Ob, s_ * 256, 256)

    build_kv(0)
    for hd in range(16):
        if hd + 1 < 16:
            build_kv(hd + 1)
        do_attn(hd)
    if getattr(k, "cut", 99) <= 5:
        return
    _mark(k, "wo")
    g5 = modv(k, i, 5)
    for half in range(2):
        so = next_slot(k)
        P.dma("pool", so[:, 0:4096], I["mla_wo"][j, half], writes=[so])
        wo = so[:, 0:4096].rearrange("p (h n) -> p h n", n=512)
        for dl in range(4):
            dc = half * 4 + dl
            for h in range(2):
                b = next_bank(k)
                for hp in range(8):
                    P.op("pe", lambda hh, b=b, hp=hp, dl=dl, h=h, wo=wo: hh.matmul(b[:], wo[:, hp, dl * 128:(dl + 1) * 128],
                         OT[hp][:, h * HT:(h + 1) * HT], start=(hp == 0), stop=(hp == 7)),
                         reads=[so, OT[hp]], writes=[b], sig=(hp == 7))
                xt = k.X[dc][h]
                P.op("dve", lambda hh, b=b, xt=xt, dc=dc, h=h: hh.scalar_tensor_tensor(
                    xt[:], b[:], g5[:, dc, h:h + 1], xt[:], ALU.mult, ALU.add), reads=[b, k.mod, xt], writes=[xt])


def ssm(k, i, j):
    hg = ssm_halo_prepass(k, i, j)
    k.arena.reset()
    for hf in range(2):
        ssm_half(k, i, j, hf, hg)
        k.arena.reset()


def ssm_halo_prepass(k, i, j):
    P, A, I = k.P, k.arena, k.I
    HN = adaln(k, i, 3, 4, 1, halves=(1,))
    eb = acc_bank(k)
    for pi in range(4, 10):
        s = next_slot(k)
        P.dma("pool", s[:, 0:NCH * 512], I["ssm_win"][j, pi], writes=[s])
        w = s[:, 0:NCH * 512].rearrange("p (c n) -> p c n", n=512)
        for m in range(4):
            ch = (pi - 4) * 4 + m
            for (o0, t0) in ((0, 0), (2, HT - 2)):
                for c in range(NCH):
                    P.op("pe", lambda hh, c=c, m=m, ch=ch, o0=o0, t0=t0, w=w: hh.matmul(
                        eb[:, ch * 4 + o0:ch * 4 + o0 + 2], w[:, c, m * 128:(m + 1) * 128], HN[c][1][:, t0:t0 + 2],
                        start=(c == 0), stop=(c == NCH - 1)), reads=[s, HN[c][1]], writes=[eb],
                        sig=(c == NCH - 1 and o0 == 2 and m == 3))
    EDGE = A.alloc([128, 96], F32, "EDGE")
    P.op("dve", lambda hh: hh.tensor_tensor(EDGE[:], eb[:, 0:96], k.zer[:, 0:96], ALU.add), reads=[eb, k.zer], writes=[EDGE])
    hb = P.dram(f"hb{j}", [128, 96], F32)
    hg = P.dram(f"hg{j}", [4 * 128, 96], F32)
    P.dma("sp", hb.t.ap(), EDGE[:], reads=[EDGE], writes=[hb])
    P.collective("AllGather", GROUPS4, hb, hg)
    return hg


def ssm_half(k, i, j, hf, hg):
    P, A, I = k.P, k.arena, k.I
    sm0 = j * 192
    convw = k.ssmall[:, sm0:sm0 + 120].rearrange("p (c w) -> p c w", w=5)
    convb = k.ssmall[:, sm0 + 120:sm0 + 144]
    ngs = k.ssmall[:, sm0 + 144:sm0 + 160]
    dd = k.ssmall[:, sm0 + 160:sm0 + 192].rearrange("p (c r) -> p c r", r=2)
    dtb_bc = k.sbc[:, j, 0, :]
    alog_bc = k.sbc[:, j, 1, :]
    triI = [k.cst[:, 0, :], k.cst[:, 1, :]]
    SLm = [k.cst[:, 2, :], k.cst[:, 3, :]]
    ones32 = k.ones32
    TCS = [slice(tc * 128, (tc + 1) * 128) for tc in range(4)]

    YT = A.alloc([128, 16, HT], F32, "YT")
    n_y = (A.off, list(A.tiles))
    XTOK = A.alloc([128, 4, 2048], BF16, "XTOK")
    BTOK = A.alloc([128, 4, 512], BF16, "BTOK")
    BT = [A.alloc([128, HT], BF16, f"BT{g}") for g in range(4)]
    CT = [A.alloc([128, HT], BF16, f"CT{g}") for g in range(4)]
    DT = [A.alloc([128, 64], F32, f"DT{t}") for t in range(4)]
    DTA = [A.alloc([128, 64], F32, f"DTA{t}") for t in range(4)]
    n_k = (A.off, list(A.tiles))
    HN = adaln(k, i, 3, 4, 1, halves=(hf,))
    dsum = A.alloc([128, 16], F32, "dsum")
    P.op("dve", lambda hh: hh.tensor_tensor(dsum[:], dd[:, :, 0], dd[:, :, 1], ALU.add), reads=[k.ssmall], writes=[dsum])
    NA = A.alloc([128, 64], F32, "NA")
    P.op("act", lambda hh: hh.activation(NA[:], alog_bc, AF.Exp), reads=[k.sbc], writes=[NA])
    P.op("dve", lambda hh: hh.tensor_scalar(NA[:], NA[:], -1.0, 0.0, ALU.mult, ALU.add), reads=[NA], writes=[NA])
    sdt = next_slot(k)
    P.dma("pool", sdt[:, 0:NCH * 64], I["ssm_wdt"][j], writes=[sdt])
    wdt = sdt[:, 0:NCH * 64].rearrange("p (c n) -> p c n", n=64)
    ut = [A.alloc([128, 64], F32, f"ut{q}") for q in range(2)]
    for tc in range(4):
        b = next_bank(k)
        for c in range(NCH):
            P.op("pe", lambda hh, b=b, c=c, tc=tc: hh.matmul(b[:, 0:64], HN[c][hf][:, TCS[tc]], wdt[:, c, :],
                 start=(c == 0), stop=(c == NCH - 1)), reads=[sdt, HN[c][hf]], writes=[b], sig=(c == NCH - 1))
        u = ut[tc % 2]
        P.op("dve", lambda hh, b=b, u=u: hh.tensor_tensor(u[:], b[:, 0:64], dtb_bc, ALU.add), reads=[b, k.sbc], writes=[u])
        P.op("act", lambda hh, u=u: hh.activation(u[:], u[:], AF.Exp), reads=[u], writes=[u])
        P.op("act", lambda hh, u=u, tc=tc: hh.activation(DT[tc][:], u[:], AF.Ln, bias=k.oneb[:], scale=1.0), reads=[u, k.oneb], writes=[DT[tc]])
        P.op("dve", lambda hh, tc=tc: hh.tensor_tensor(DTA[tc][:], DT[tc][:], NA[:], ALU.mult), reads=[DT[tc], NA], writes=[DTA[tc]])

    HALO = None
    if hf == 1:
        G = A.alloc([128, 4, 96], F32, "G")
        P.dma("sp", G[:], hg.t.ap().rearrange("(r p) n -> p r n", p=128), reads=[hg], writes=[G])
        HALO = A.alloc([128, 24, 4], F32, "HALO")
        Gv = G[:].rearrange("p r (c e) -> p r c e", e=4)
        for (dst, src, mo) in ((slice(0, 2), slice(2, 4), 8), (slice(2, 4), slice(0, 2), 12)):
            P.op("dve", lambda hh, dst=dst, src=src, mo=mo: hh.tensor_scalar(
                HALO[:, :, dst], Gv[:, 0, :, src], k.posm[:, mo:mo + 1], 0.0, ALU.mult, ALU.add),
                reads=[G, k.posm], writes=[HALO])
            for r in range(1, 4):
                P.op("dve", lambda hh, dst=dst, src=src, mo=mo, r=r: hh.scalar_tensor_tensor(
                    HALO[:, :, dst], Gv[:, r, :, src], k.posm[:, mo + r:mo + r + 1], HALO[:, :, dst], ALU.mult, ALU.add),
                    reads=[G, k.posm, HALO], writes=[HALO])

    _mark(k, "xbc")
    PRE = [A.alloc([128, 520], F32, f"PRE{q}") for q in range(3)]
    for q in range(3):
        P.op("dve", lambda hh, q=q: hh.memset(PRE[q][:], 0.0), writes=[PRE[q]])
    acc = [A.alloc([128, 516], F32, f"cacc{q}") for q in range(2)]
    sil = [A.alloc([128, HT], F32, f"sil{q}") for q in range(2)]
    XSr = [A.alloc([128, HT], BF16, f"XSr{q}") for q in range(3)]
    NU = 516 if hf == 0 else 512
    slots = {}

    def chunk_gen(pi, m, n):
        if pi not in slots:
            s_ = next_slot(k)
            P.dma("pool", s_[:, 0:NCH * 512], I["ssm_win"][j, pi], writes=[s_])
            slots[pi] = s_
        s = slots[pi]
        w = s[:, 0:NCH * 512].rearrange("p (c n) -> p c n", n=512)
        ch = (pi - 4) * 4 + m
        b = next_bank(k)
        for c in range(NCH):
            P.op("pe", lambda hh, b=b, c=c, m=m, w=w: hh.matmul(b[:], w[:, c, m * 128:(m + 1) * 128], HN[c][hf][:],
                 start=(c == 0), stop=(c == NCH - 1)), reads=[s, HN[c][hf]], writes=[b], sig=(c == NCH - 1))
        pre = PRE[n % 3]
        a = acc[n % 2]
        if hf == 0:
            P.op("act", lambda hh, b=b, pre=pre: hh.activation(pre[:, 2:258], b[:, 0:256], AF.Copy), reads=[b], writes=[pre])
            P.op("act", lambda hh, b=b, pre=pre: hh.activation(pre[:, 262:518], b[:, 256:512], AF.Copy), reads=[b], writes=[pre])
        else:
            P.op("act", lambda hh, b=b, pre=pre: hh.activation(pre[:, 2:514], b[:], AF.Copy), reads=[b], writes=[pre])
            P.op("dve", lambda hh, pre=pre, ch=ch: hh.tensor_copy(pre[:, 0:2], HALO[:, ch, 0:2]), reads=[HALO], writes=[pre])
            P.op("dve", lambda hh, pre=pre, ch=ch: hh.tensor_copy(pre[:, 514:516], HALO[:, ch, 2:4]), reads=[HALO], writes=[pre])
        yield
        P.op("dve", lambda hh, a=a, pre=pre, ch=ch: hh.tensor_scalar(
            a[:, 0:NU], pre[:, 0:NU], convw[:, ch, 0:1], 0.0, ALU.mult, ALU.add), reads=[pre, k.ssmall], writes=[a])
        yield
        for wi_ in range(1, 5):
            P.op("dve", lambda hh, a=a, pre=pre, ch=ch, wi_=wi_: hh.scalar_tensor_tensor(
                a[:, 0:NU], pre[:, wi_:wi_ + NU], convw[:, ch, wi_:wi_ + 1], a[:, 0:NU], ALU.mult, ALU.add),
                reads=[pre, k.ssmall, a], writes=[a])
            yield
        if ch < 16:
            dst, dt_ = sil[n % 2], sil[n % 2]
        elif ch < 20:
            dst = BT[ch - 16]
        else:
            dst = CT[ch - 20]
        segs = ((0, 0, 256), (256, 260, 256)) if hf == 0 else ((0, 0, 512),)
        for (o0, a0, ln) in segs:
            P.op("act", lambda hh, dst=dst, a=a, ch=ch, o0=o0, a0=a0, ln=ln: hh.activation(
                dst[:, o0:o0 + ln], a[:, a0:a0 + ln], AF.Silu, bias=convb[:, ch:ch + 1], scale=1.0),
                reads=[a, k.ssmall], writes=[dst])
        if ch < 16:
            P.op("dve", lambda hh, dst=dst, ch=ch: hh.tensor_scalar(
                YT[:, ch, :], dst[:], dsum[:, ch:ch + 1], 0.0, ALU.mult, ALU.add), reads=[dst, dsum], writes=[YT], waw=False)
            xs = XSr[n % 3]
            P.op("act", lambda hh, dst=dst, xs=xs: hh.activation(xs[:], dst[:], AF.Copy), reads=[dst], writes=[xs])
            src_t, tok, tcol = xs, XTOK, ch * 128
        elif ch < 20:
            src_t, tok, tcol = dst, BTOK, (ch - 16) * 128
        else:
            src_t = None
        if src_t is not None:
            tb = next_bank(k)
            tbv = tb[:, 0:256].bitcast(BF16)
            for tc in range(4):
                P.op("pe", lambda hh, tbv=tbv, tc=tc, src_t=src_t: hh.transpose(tbv[:, TCS[tc]], src_t[:, TCS[tc]], k.identb[:]),
                     reads=[src_t, k.identb], writes=[tb], sig=(tc == 3))
            P.op("act", lambda hh, tbv=tbv, tok=tok, tcol=tcol: hh.activation(
                tok[:, :, tcol:tcol + 128], tbv.rearrange("p (t n) -> p t n", n=128), AF.Copy), reads=[tb], writes=[tok], waw=False)

    pend = [(pi, m) for pi in range(4, 10) for m in range(4)]
    act_, n = [], 0
    while pend or act_:
        while pend and len(act_) < 2:
            pi_, m_ = pend.pop(0)
            act_.append(chunk_gen(pi_, m_, n))
            n += 1
        for g_ in list(act_):
            try:
                next(g_)
            except StopIteration:
                act_.remove(g_)

    _mark(k, "ssd")
    A.reset(n_k[0], n_k[1], pool=True)
    H = A.alloc([128, 2048], F32, "H")
    HS = [T(H[:, q_ * 256:(q_ + 1) * 256], f"HS{q_}") for q_ in range(8)]
    Hb = A.alloc([128, 16, 2, 128], BF16, "Hb")
    XD = [A.alloc([128, 2, 2, 128], BF16, f"XD{q}") for q in range(2)]
    XW = [A.alloc([128, 256], BF16, f"XW{q}") for q in range(2)]
    R = [A.alloc([128, 8, 128], F32, f"R{q}") for q in range(2)]
    LT = [A.alloc([128, 4, 128], F32, f"LT{q}") for q in range(2)]
    EC = [A.alloc([128, 4, 128], F32, f"EC{q}") for q in range(2)]
    MT = [A.alloc([128, 4, 128], BF16, f"MT{q}") for q in range(2)]
    CW = [A.alloc([128, 4, 128], BF16, f"CW{q}") for q in range(2)]
    CBm = [A.alloc([128, 4, 128], F32, f"CBm{q}") for q in range(2)]
    W4 = [A.alloc([128, 4], F32, f"W4{q}") for q in range(2)]
    DBt = [A.alloc([128, 64], F32, f"DBt{q}") for q in range(2)]
    P.op("dve", lambda hh: hh.memset(Hb[:], 0.0), writes=[Hb])
    for q in range(2):
        P.op("dve", lambda hh, q=q: hh.memset(XD[q][:], 0.0), writes=[XD[q]])
    cnt = {"p": 0, "q": 0}

    def process(tc, d, state_only):
        last = 127 if d == 0 else 0
        pc = cnt["p"]
        cnt["p"] += 1
        tb_ = next_bank(k)
        P.op("pe", lambda hh: hh.matmul(tb_[:, 0:64], ones32[:], DTA[tc][:], start=True, stop=True),
             reads=[k.ones32, DTA[tc]], writes=[tb_])
        dbt = DBt[pc % 2]
        P.op("act", lambda hh: hh.activation(dbt[:], tb_[:, 0:64], AF.Exp), reads=[tb_], writes=[dbt])
        cbm = CBm[pc % 2]
        if not state_only:
            cb = next_bank(k)
            for g in range(4):
                P.op("pe", lambda hh, g=g: hh.matmul(cb[:, g * 128:(g + 1) * 128], BT[g][:, TCS[tc]], CT[g][:, TCS[tc]],
                     start=True, stop=True), reads=[BT[g], CT[g]], writes=[cb], sig=(g == 3))
            P.op("dve", lambda hh: hh.tensor_tensor(cbm[:], cb[:].rearrange("p (g n) -> p g n", n=128),
                 triI[d].unsqueeze(1).broadcast_to([128, 4, 128]), ALU.mult), reads=[cb, k.cst], writes=[cbm])
            Hv = H[:].rearrange("p (a w e) -> p a w e", w=2, e=64)
            P.op("act", lambda hh: hh.activation(Hb[:, :, 0, 0:64], Hv[:, :, 0, :], AF.Copy), reads=HS, writes=[Hb])
            P.op("act", lambda hh: hh.activation(Hb[:, :, 1, 64:128], Hv[:, :, 1, :], AF.Copy), reads=HS, writes=[Hb])
        def qiter(hg, q):
            r_ = R[hg % 2]
            c0 = d * 32 + hg * 8
            if q == 0:
                P.op("pool", lambda hh, r_=r_, c0=c0: hh.tensor_tensor(
                    r_[:], triI[d].unsqueeze(1).broadcast_to([128, 8, 128]),
                    DTA[tc][:, c0:c0 + 8].unsqueeze(2).broadcast_to([128, 8, 128]), ALU.mult),
                    reads=[k.cst, DTA[tc]], writes=[r_])
            yield
            qq = cnt["q"]
            cnt["q"] += 1
            h0 = hg * 8 + q * 4
            p0 = h0 // 2
            dcol = d * 32 + h0
            hv = H[:, h0 * 64:(h0 + 4) * 64].rearrange("p (a e) -> p a e", e=64)
            P.op("dve", lambda hh, hv=hv, dbt=dbt, dcol=dcol: hh.tensor_tensor(
                hv, hv, dbt[:, dcol:dcol + 4].unsqueeze(2).broadcast_to([128, 4, 64]), ALU.mult), reads=[HS[hg * 2 + q], dbt], writes=[HS[hg * 2 + q]])
            yield
            rr = r_[:, q * 4:(q + 1) * 4, :]
            bs = next_bank(k)
            P.op("pe", lambda hh, bs=bs, rr=rr: hh.matmul(bs[:], SLm[d], rr, start=True, stop=True),
                 reads=[k.cst, r_], writes=[bs])
            lt = LT[qq % 2]
            P.op("act", lambda hh, bs=bs, lt=lt: hh.activation(lt[:], bs[:].rearrange("p (a n) -> p a n", n=128), AF.Exp),
                 reads=[bs], writes=[lt])
            yield
            w4 = W4[qq % 2]
            dcol = d * 32 + h0
            P.op("dve", lambda hh, w4=w4, lt=lt, dcol=dcol: hh.tensor_tensor(
                w4[:], DT[tc][:, dcol:dcol + 4], lt[:, :, last], ALU.mult), reads=[DT[tc], lt], writes=[w4])
            yield
            xw = XW[qq % 2]
            xin = XTOK[:, tc, h0 * 64:(h0 + 4) * 64].rearrange("p (a e) -> p a e", e=64)
            P.op("dve", lambda hh, xw=xw, xin=xin, w4=w4: hh.tensor_tensor(
                xw[:].rearrange("p (a e) -> p a e", e=64), xin, w4[:].unsqueeze(2).broadcast_to([128, 4, 64]), ALU.mult),
                reads=[XTOK, w4], writes=[xw])
            yield
            if not state_only:
                bc = next_bank(k)
                P.op("pe", lambda hh, bc=bc, rr=rr: hh.matmul(bc[:], ones32[:], rr, start=True, stop=True),
                     reads=[k.ones32, r_], writes=[bc])
                ec = EC[qq % 2]
                P.op("act", lambda hh, bc=bc, ec=ec: hh.activation(ec[:], bc[:].rearrange("p (a n) -> p a n", n=128), AF.Exp),
                     reads=[bc], writes=[ec])
                yield
                mt = MT[qq % 2]
                P.op("dve", lambda hh, mt=mt, lt=lt, hg=hg: hh.tensor_tensor(
                    mt[:], lt[:], cbm[:, hg, :].unsqueeze(1).broadcast_to([128, 4, 128]), ALU.mult),
                    reads=[lt, cbm], writes=[mt])
                yield
                cw = CW[qq % 2]
                P.op("dve", lambda hh, cw=cw, ec=ec, hg=hg: hh.tensor_tensor(
                    cw[:], ec[:], CT[hg][:, TCS[tc]].unsqueeze(1).broadcast_to([128, 4, 128]), ALU.mult),
                    reads=[ec, CT[hg]], writes=[cw])
                yield
                xd = XD[qq % 2]
                xdv = xd[:].rearrange("p a w e -> p a (w e)").rearrange("p a (s e) -> p a s e", e=64)[:, :, 0:4:3, :]
                xin4 = XTOK[:, tc, h0 * 64:(h0 + 4) * 64].rearrange("p (a w e) -> p a w e", w=2, e=64)
                dtv = DT[tc][:, dcol:dcol + 4].rearrange("p (a w) -> p a w", w=2).unsqueeze(3).broadcast_to([128, 2, 2, 64])
                P.op("dve", lambda hh, xdv=xdv, xin4=xin4, dtv=dtv: hh.tensor_tensor(xdv, xin4, dtv, ALU.mult),
                     reads=[XTOK, DT[tc]], writes=[xd])
                yield
                yb = next_bank(k)
                for pp in range(2):
                    ops = ((xd[:, pp, 0, :], mt[:, 2 * pp, :]), (xd[:, pp, 1, :], mt[:, 2 * pp + 1, :]),
                           (Hb[:, p0 + pp, 0, :], cw[:, 2 * pp, :]), (Hb[:, p0 + pp, 1, :], cw[:, 2 * pp + 1, :]))
                    for n_, (l_, r2) in enumerate(ops):
                        P.op("pe", lambda hh, yb=yb, pp=pp, l_=l_, r2=r2, n_=n_: hh.matmul(
                            yb[:, pp * 128:(pp + 1) * 128], l_, r2, start=(n_ == 0), stop=(n_ == 3)),
                            reads=[xd, mt, Hb, cw], writes=[yb], sig=(n_ == 3 and pp == 1))
                yv = YT[:, p0:p0 + 2, TCS[tc]]
                P.op("dve", lambda hh, yv=yv, yb=yb: hh.tensor_tensor(
                    yv, yv, yb[:, 0:256].rearrange("p (a n) -> p a n", n=128), ALU.add), reads=[YT, yb], writes=[YT])
                yield
            sbk = next_bank(k)
            P.op("pe", lambda hh, sbk=sbk, xw=xw, hg=hg: hh.matmul(sbk[:, 0:256], BTOK[:, tc, hg * 128:(hg + 1) * 128], xw[:],
                 start=True, stop=True), reads=[BTOK, xw], writes=[sbk])
            P.op("dve", lambda hh, hv=hv, sbk=sbk: hh.tensor_tensor(
                hv, hv, sbk[:, 0:256].rearrange("p (a e) -> p a e", e=64), ALU.add), reads=[HS[hg * 2 + q], sbk], writes=[HS[hg * 2 + q]])

        pending = [(hg, q) for hg in range(4) for q in range(2)]
        active = []
        while pending or active:
            while pending and len(active) < 2:
                active.append(qiter(*pending.pop(0)))
            for g_ in list(active):
                try:
                    next(g_)
                except StopIteration:
                    active.remove(g_)

    if hf == 0:
        for d in range(2):
            for s_ in range(2):
                P.op("dve", lambda hh: hh.memset(H[:], 0.0), writes=HS)
                order = (2 * s_, 2 * s_ + 1) if d == 0 else (2 * s_ + 1, 2 * s_)
                for tc in order:
                    process(tc, d, False)
                P.dma("sp", k.O["stateO"][j, s_, d], H[:], reads=HS)
    else:
        SLD = A.alloc([128, 2048], F32, "SLD")
        TLD = A.alloc([128, 4, 64], F32, "TLD")
        TL = A.alloc([128, 64], F32, "TL")
        EE = A.alloc([128, 32], F32, "EE")
        sbn = [P.dram(f"sbn{j}_{d}", [128, 2048], F32) for d in range(2)]
        sgg = [P.dram(f"sgg{j}_{d}", [4 * 128, 2048], F32) for d in range(2)]
        tlbd = P.dram(f"tlb{j}", [128, 64], F32)
        tlgd = P.dram(f"tlg{j}", [4 * 128, 64], F32)
        tlb = acc_bank(k)
        for tc in range(4):
            P.op("pe", lambda hh, tc=tc: hh.matmul(tlb[:, 0:64], ones32[:], DTA[tc][:], start=(tc == 0), stop=(tc == 3)),
                 reads=[k.ones32, DTA[tc]], writes=[tlb])
        P.op("dve", lambda hh: hh.tensor_tensor(TL[:], tlb[:, 0:64], k.zer[:, 0:64], ALU.add), reads=[tlb, k.zer], writes=[TL])
        P.dma("sp", tlbd.t.ap(), TL[:], reads=[TL], writes=[tlbd])
        P.collective("AllGather", GROUPS4, tlbd, tlgd)
        _mark(k, "prepass")
        for d in range(2):
            P.op("dve", lambda hh: hh.memset(H[:], 0.0), writes=HS)
            for tc in ((0, 1, 2, 3) if d == 0 else (3, 2, 1, 0)):
                process(tc, d, True)
            P.dma("sp", sbn[d].t.ap(), H[:], reads=HS, writes=[sbn[d]])
            P.collective("AllGather", GROUPS4, sbn[d], sgg[d])
        _mark(k, "fold")
        P.dma("sp", TLD[:], tlgd.t.ap().rearrange("(r p) n -> p r n", p=128), reads=[tlgd], writes=[TLD])
        for d in range(2):
            sgv = sgg[d].t.ap()
            P.dma("sp", H[:], I["stateT"][j, d], writes=HS)
            for r in ((0, 1, 2, 3) if d == 0 else (3, 2, 1, 0)):
                mcol = (0 if d == 0 else 4) + r
                mk = k.posm[:, mcol:mcol + 1]
                P.dma("sp", SLD[:], sgv[r * 128:(r + 1) * 128, :], reads=[sgg[d]], writes=[SLD])
                P.op("act", lambda hh, r=r, d=d, mk=mk: hh.activation(EE[:], TLD[:, r, d * 32:(d + 1) * 32], AF.Exp, scale=mk),
                     reads=[TLD, k.posm], writes=[EE])
                hv = H[:].rearrange("p (a e) -> p a e", e=64)
                P.op("dve", lambda hh, hv=hv: hh.tensor_tensor(hv, hv, EE[:].unsqueeze(2).broadcast_to([128, 32, 64]), ALU.mult),
                     reads=HS + [EE], writes=HS)
                P.op("dve", lambda hh, mk=mk: hh.scalar_tensor_tensor(H[:], SLD[:], mk, H[:], ALU.mult, ALU.add),
                     reads=[SLD, k.posm] + HS, writes=HS)
            for tc in ((0, 1, 2, 3) if d == 0 else (3, 2, 1, 0)):
                process(tc, d, False)

    _mark(k, "gate")
    A.reset(n_y[0], n_y[1])
    HN = adaln(k, i, 3, 4, 1, halves=(hf,))
    sz = [A.alloc([128, HT], F32, f"sz{q}") for q in range(2)]
    YN = [A.alloc([128, HT], BF16, f"YN{c}") for c in range(16)]
    rs = A.alloc([128, HT], F32, "rsy")
    stb = acc_bank(k)
    n = 0
    for pz in range(4):
        s = next_slot(k)
        P.dma("pool", s[:, 0:NCH * 512], I["ssm_win"][j, pz], writes=[s])
        w = s[:, 0:NCH * 512].rearrange("p (c n) -> p c n", n=512)
        for m in range(4):
            ch = pz * 4 + m
            b = next_bank(k)
            for c in range(NCH):
                P.op("pe", lambda hh, b=b, c=c, m=m, w=w: hh.matmul(b[:], w[:, c, m * 128:(m + 1) * 128], HN[c][hf][:],
                     start=(c == 0), stop=(c == NCH - 1)), reads=[s, HN[c][hf]], writes=[b], sig=(c == NCH - 1))
            z = sz[n % 2]
            n += 1
            P.op("act", lambda hh, z=z, b=b: hh.activation(z[:], b[:], AF.Silu), reads=[b], writes=[z])
            P.op("dve", lambda hh, z=z, ch=ch: hh.tensor_tensor(YT[:, ch, :], YT[:, ch, :], z[:], ALU.mult), reads=[YT, z], writes=[YT])
            sq = k.sq[ch % 2]
            P.op("act", lambda hh, sq=sq, ch=ch: hh.activation(sq[:], YT[:, ch, :], AF.Square), reads=[YT], writes=[sq])
            P.op("pe", lambda hh, sq=sq, ch=ch: hh.matmul(stb[:], k.ones[:], sq[:], start=(ch == 0), stop=(ch == 15)),
                 reads=[sq, k.ones], writes=[stb])
    rsqrt_from_bank(k, stb, rs, 2048)
    for ch in range(16):
        P.op("dve", lambda hh, ch=ch: hh.scalar_tensor_tensor(
            YN[ch][:], YT[:, ch, :], ngs[:, ch:ch + 1], rs[:], ALU.mult, ALU.mult), reads=[YT, k.ssmall, rs], writes=[YN[ch]])
    g5 = modv(k, i, 5)
    for pw in range(4):
        s = next_slot(k)
        P.dma("pool", s[:, 0:4096], I["ssm_wout"][j, pw], writes=[s])
        w = s[:, 0:4096].rearrange("p (c n) -> p c n", n=256)
        for dl in range(2):
            dc = pw * 2 + dl
            b = next_bank(k)
            for ch in range(16):
                P.op("pe", lambda hh, b=b, ch=ch, dl=dl, w=w: hh.matmul(b[:], w[:, ch, dl * 128:(dl + 1) * 128], YN[ch][:],
                     start=(ch == 0), stop=(ch == 15)), reads=[s, YN[ch]], writes=[b], sig=(ch == 15))
            xt = k.X[dc][hf]
            P.op("dve", lambda hh, b=b, xt=xt, dc=dc: hh.scalar_tensor_tensor(
                xt[:], b[:], g5[:, dc, hf:hf + 1], xt[:], ALU.mult, ALU.add), reads=[b, k.mod, xt], writes=[xt])


def final_norm(k):
    P = k.P
    A = k.arena
    k.sq = [A.alloc([128, HT], BF16, f"sq{j}") for j in range(2)]
    rstd = [A.alloc([128, HT], F32, f"rstd{h}") for h in range(2)]
    yo = [A.alloc([128, HT], F32, f"yo{j}") for j in range(4)]
    n = 0
    outs = []
    for h in range(2):
        rms_stats(k, h, rstd[h])
        for c in range(NCH):
            y = yo[n % 4]
            n += 1
            xt = k.X[c][h]
            P.op("dve", lambda hh, y=y, xt=xt, c=c, h=h: hh.scalar_tensor_tensor(
                y[:], xt[:], k.fnormg[:, c:c + 1], rstd[h][:], ALU.mult, ALU.mult),
                reads=[xt, k.fnormg, rstd[h]], writes=[y])
            P.dma("sp", k.O["yT"][:, c, h * HT:(h + 1) * HT], y[:], reads=[y])
    P.wait_all("sp", yo + k.out_tiles)


def _fm(v):
    v = np.asarray(v, np.float32)
    lead = v.shape[:-1]
    a = v.reshape(lead + (NCH, 128))
    a = np.moveaxis(a, -1, 0)
    return np.ascontiguousarray(a)


def host_prep(inp):
    f32 = np.float32
    sh = {}
    mw = np.asarray(inp["mod_w"], f32).reshape(DEPTH, NCH, 128, 9, 2, 512)
    sh["modw"] = np.ascontiguousarray(mw.transpose(0, 3, 4, 2, 1, 5)).reshape(DEPTH, 9, 2, 128, NCH * 512)
    mb = np.asarray(inp["mod_b"], f32).reshape(DEPTH, 9, NCH, 128)
    sh["modb"] = np.ascontiguousarray(mb.transpose(3, 0, 1, 2)).reshape(128, DEPTH * 9 * NCH)
    ng = np.asarray(inp["norm_g"], f32).reshape(DEPTH, 3, NCH, 128)
    sh["normg"] = np.ascontiguousarray(ng.transpose(3, 0, 1, 2)).reshape(128, DEPTH * 3 * NCH)
    sh["fnormg"] = np.ascontiguousarray(np.asarray(inp["final_norm_g"], f32).reshape(NCH, 128).T)
    wi = np.asarray(inp["ffn_w_in"], f32).reshape(DEPTH, 2, NCH, 128, 2, 11, 256)
    sh["ffn_in"] = np.ascontiguousarray(wi.transpose(0, 1, 5, 3, 2, 4, 6)).reshape(DEPTH, 2, 11, 128, NCH * 512)
    wo = np.asarray(inp["ffn_w_out"], f32).reshape(DEPTH, 2, NF, 128, NCH, 128)
    sh["ffn_out"] = np.ascontiguousarray(wo.transpose(0, 1, 4, 3, 2, 5)).reshape(DEPTH, 2, NCH, 128, NF * 128)
    perm = np.arange(32) ^ 8
    wi_ = np.asarray(inp["mla_w_in"], f32)
    def pcn(w):
        rows, n = w.shape
        return np.ascontiguousarray(w.reshape(rows // 128, 128, n).transpose(1, 0, 2)).reshape(128, (rows // 128) * n)
    sh["mla_w1"] = np.stack([pcn(wi_[j][:, 0:512]) for j in range(2)])
    sh["mla_w2"] = np.stack([pcn(np.concatenate([wi_[j][:, 512:800], wi_[j][:, 704:768], wi_[j][:, 768 + perm]], 1)) for j in range(2)])
    wq_ = np.asarray(inp["mla_wq_b"], f32)
    colsw = np.arange(1536).reshape(16, 96).copy()
    colsw[:, 64:96] = colsw[:, 64 + perm]
    colsw = colsw.reshape(-1)
    sh["mla_wq"] = np.stack([np.stack([np.stack([pcn(w[:, pc * 768:(pc + 1) * 768]) for pc in range(2)])
                                       for w in (wq_[j], wq_[j][:, colsw])]) for j in range(2)])
    sh["mla_wkv"] = np.stack([pcn(np.asarray(inp["mla_wkv_b"], f32)[j]) for j in range(2)])
    wo_ = np.asarray(inp["mla_wo"], f32)
    sh["mla_wo"] = np.stack([np.stack([pcn(wo_[j][:, hf * 512:(hf + 1) * 512]) for hf in range(2)]) for j in range(2)])
    qn_ = np.asarray(inp["mla_q_norm"], f32).reshape(2, 4, 128)
    kn_ = np.asarray(inp["mla_kv_norm"], f32).reshape(2, 2, 128)
    sh["mla_small"] = np.ascontiguousarray(np.concatenate([qn_, kn_], 1).transpose(2, 0, 1)).reshape(128, 12)
    sw = np.asarray(inp["ssm_w_in"], f32)
    sh["ssm_win"] = np.stack([np.stack([pcn(sw[j][:, q * 512:(q + 1) * 512]) for q in range(10)]) for j in range(2)])
    sh["ssm_wdt"] = np.stack([pcn(sw[j][:, 5120:5184]) for j in range(2)])
    so_ = np.asarray(inp["ssm_w_out"], f32)
    sh["ssm_wout"] = np.stack([np.stack([pcn(so_[j][:, q * 256:(q + 1) * 256]) for q in range(4)]) for j in range(2)])
    small = np.zeros((128, 2, 192), f32)
    cw_ = np.asarray(inp["ssm_conv_w"], f32)
    cb_ = np.asarray(inp["ssm_conv_b"], f32)
    sg_ = np.asarray(inp["ssm_norm_g"], f32)
    sd_ = np.asarray(inp["ssm_d"], f32)
    for j in range(2):
        small[:, j, 0:120] = cw_[j].reshape(5, 24, 128).transpose(2, 1, 0).reshape(128, 120)
        small[:, j, 120:144] = cb_[j].reshape(24, 128).T
        small[:, j, 144:160] = sg_[j].reshape(16, 128).T
        hidx = (np.arange(16)[None, :] * 2 + (np.arange(128)[:, None] // 64))
        small[:, j, 160:192] = np.stack([sd_[j, 0][hidx], sd_[j, 1][hidx]], -1).reshape(128, 32)
    sh["ssm_small"] = small.reshape(128, 384)
    bc = np.stack([np.asarray(inp["ssm_dt_bias"], f32).reshape(2, 64), np.asarray(inp["ssm_a_log"], f32).reshape(2, 64)], 1)
    sh["ssm_bc"] = np.ascontiguousarray(np.broadcast_to(bc[None], (128, 2, 2, 64)))
    ii = np.arange(128)
    cst = np.zeros((128, 5, 128), f32)
    cst[:, 0] = (ii[:, None] <= ii[None, :])
    cst[:, 1] = (ii[:, None] >= ii[None, :])
    cst[:, 2] = (ii[:, None] > ii[None, :])
    cst[:, 3] = (ii[:, None] < ii[None, :])
    cst[:, 4] = np.eye(128)
    sh["consts"] = cst
    sst = np.asarray(inp["state_ssm"], f32)
    cache = np.asarray(inp["cache_mla"], f32)
    freqs = 1.0 / (10000.0 ** (np.arange(0, 16, 2, dtype=np.float32) / 16.0))
    xp = np.asarray(inp["x_prompt"], f32)
    xs = np.asarray(inp["x_sample"], f32)
    c = np.asarray(inp["c"], f32)
    cc = np.asarray(inp["c_ctx"], f32)
    per = []
    for r in range(NCORES):
        gi, kq = r // 4, r % 4
        tok = np.concatenate([xp[2 * r], xp[2 * r + 1], xs[gi, kq * HT:(kq + 1) * HT]], 0)
        xT = np.ascontiguousarray(tok.T.reshape(NCH, 128, TT).transpose(1, 0, 2))
        cv = np.stack([cc, c[gi]], -1).reshape(NCH, 128, 2).transpose(1, 0, 2)
        d = dict(sh)
        d["xT"] = xT
        d["cvec"] = np.ascontiguousarray(cv)
        tg = kq * HT + np.arange(HT)
        pos = np.stack([tg // 64, tg % 64], 0).astype(np.float32)
        rope = np.zeros((96, 2, HT), np.float32)
        for ax in range(2):
            for hf in range(2):
                for fr in range(8):
                    f = ax * 16 + hf * 8 + fr
                    ang = (pos[ax] * freqs[fr]).astype(np.float32)
                    rope[64 + f, 0] = np.cos(ang)
                    rope[64 + f, 1] = np.sin(ang) * (-1.0 if hf == 0 else 1.0)
        d["ropeT"] = rope
        d["cacheT"] = np.ascontiguousarray(cache[gi].transpose(0, 2, 1))
        pm = np.zeros((128, 16), f32)
        for r_ in range(4):
            pm[:, r_] = float(r_ < kq)
            pm[:, 4 + r_] = float(r_ > kq)
            pm[:, 8 + r_] = float(r_ == kq - 1)
            pm[:, 12 + r_] = float(r_ == kq + 1)
        d["posm"] = pm
        d["stateT"] = np.ascontiguousarray(sst[gi].reshape(2, 2, 2048, 128).transpose(0, 1, 3, 2))
        per.append(d)
    return per


_NC_CACHE = {}


def run_device(inp, stage=99):
    if stage not in _NC_CACHE:
        _NC_CACHE[stage] = build_program(stage)
    nc = _NC_CACHE[stage]
    per = host_prep(inp)
    per = [{n: d[n] for n in nc._in_names} for d in per]
    res = run_bass_kernel_spmd(nc, per, core_ids=list(range(NCORES)))
    return res.results


def kernel(**inputs):
    return kernel_stage(inputs, 99)


def kernel_stage(inputs, stage):
    res = run_device(inputs, stage)
    B, S = 16, 256
    yp = np.zeros((B, S, D), np.float32)
    ys = np.zeros((2, 2048, D), np.float32)
    for r in range(NCORES):
        yT = res[r]["yT"]
        tok = yT.transpose(2, 1, 0).reshape(TT, D)
        yp[2 * r] = tok[0:256]
        yp[2 * r + 1] = tok[256:512]
        gi, kq = r // 4, r % 4
        ys[gi, kq * HT:(kq + 1) * HT] = tok[512:1024]
    nc_ = np.zeros((B, 2, S, 288), np.float32)
    for r in range(NCORES):
        co = res[r]["cacheO"]
        for s_ in range(2):
            nc_[2 * r + s_] = co[:, :, s_ * 256:(s_ + 1) * 256].transpose(0, 2, 1)
    ns_ = np.zeros((B, 2, 2, 32, 64, 128), np.float32)
    for r in range(NCORES):
        so = res[r]["stateO"]
        for s_ in range(2):
            ns_[2 * r + s_] = so[:, s_].transpose(0, 1, 3, 2).reshape(2, 2, 32, 64, 128)
    return yp, ys, nc_, ns_
```

```python
import numpy as np
from contextlib import ExitStack
import concourse.bass as bass
import concourse.mybir as mybir

F32 = mybir.dt.float32
BF16 = mybir.dt.bfloat16
AF = mybir.ActivationFunctionType
ALU = mybir.AluOpType
AX = mybir.AxisListType

EPOCH = 12000


class T:
    __slots__ = ("t", "name", "w", "rd", "dsem", "dcnt", "uid", "psum")
    _n = [0]

    def __init__(self, t, name="v"):
        T._n[0] += 1
        self.uid = T._n[0]
        self.t = t
        self.name = name
        self.w = []
        self.rd = []
        self.dsem = None
        self.dcnt = 0
        self.psum = False

    def __getitem__(self, idx):
        return self.t[idx]


class _Rec:
    def __init__(self):
        self.call = None

    def __getattr__(self, name):
        def f(*a, **kw):
            self.call = (name, a, kw)
            return self
        return f


class Prog:
    ENG = ("pe", "dve", "act", "pool", "sp")

    def __init__(self, nc, stack):
        self.nc = nc
        self.stack = stack
        self.h = {"pe": nc.tensor, "dve": nc.vector, "act": nc.scalar, "pool": nc.gpsimd, "sp": nc.sync}
        self.streams = {e: [] for e in self.ENG}
        self.count = {e: 0 for e in self.ENG}
        self.pending = {e: False for e in self.ENG}
        self.esems = {e: [] for e in self.ENG}
        self.seen = {e: {} for e in self.ENG}
        self.nsem = 0

    def sem(self, name):
        self.nsem += 1
        return self.stack.enter_context(self.nc.semaphore(f"{name}_{self.nsem}"))

    def sb(self, name, shape, dt=F32):
        return T(self.stack.enter_context(self.nc.sbuf_tensor(name, list(shape), dt)), name)

    def ps(self, name, shape, dt=F32):
        t = T(self.stack.enter_context(self.nc.psum_tensor(name, list(shape), dt)), name)
        t.psum = True
        return t

    def dram(self, name, shape, dt=F32, kind="Internal"):
        return T(self.nc.dram_tensor(name, list(shape), dt, kind=kind), name)

    def _esem(self, e, ep):
        while len(self.esems[e]) <= ep:
            self.esems[e].append(self.sem(f"c_{e}_{len(self.esems[e])}"))
        return self.esems[e][ep]

    def _wait(self, eng, ev):
        if ev[0] == "e":
            _, src, idx = ev
            ep, v = (idx - 1) // EPOCH, (idx - 1) % EPOCH + 1
            key = (src, ep)
            sem = self._esem(src, ep)
        else:
            _, sem, v, key = ev
        if self.seen[eng].get(key, 0) >= v:
            return
        if ev[0] == "e":
            for pe in range(ep):
                self.seen[eng][(src, pe)] = EPOCH
        self.seen[eng][key] = v
        self.streams[eng].append(lambda h, sem=sem, v=v: h.wait_ge(sem, v))

    def _deps(self, eng, reads, writes, same_eng_raw=True, waw=True):
        evs = []
        for t in reads:
            evs += t.w
            if t.psum:
                evs += [e for e in t.rd if not (e[0] == "e" and e[1] == eng)]
        for t in writes:
            if waw or t.psum:
                evs += t.w
            evs += t.rd
        for ev in evs:
            if ev[0] == "e" and ev[1] == eng:
                if eng == "pe" or not same_eng_raw:
                    continue
            self._wait(eng, ev)

    def op(self, eng, fn, reads=(), writes=(), sig=True, waw=True):
        reads = [r for r in reads if r is not None]
        writes = [w for w in writes if w is not None]
        self._deps(eng, reads, writes, waw=waw)
        rec = _Rec()
        fn(rec)
        name, a, kw = rec.call
        if sig:
            self.count[eng] += 1
            idx = self.count[eng]
            ep = (idx - 1) // EPOCH
            sem = self._esem(eng, ep)
            self.streams[eng].append(lambda h, name=name, a=a, kw=kw, sem=sem: getattr(h, name)(*a, **kw).then_inc(sem, 1))
            self.pending[eng] = False
        else:
            idx = self.count[eng] + 1
            self.streams[eng].append(lambda h, name=name, a=a, kw=kw: getattr(h, name)(*a, **kw))
            self.pending[eng] = True
        ev = ("e", eng, idx)
        for t in writes:
            if waw or t.psum or t.rd:
                t.w = [ev]
            else:
                t.w = self._compact(t.w + [ev]) if len(t.w) > 12 else t.w + [ev]
            t.rd = []
        for t in reads:
            if t not in writes:
                t.rd.append(ev)
                if len(t.rd) > 24:
                    t.rd = self._compact(t.rd)
        return ev

    @staticmethod
    def _compact(evs):
        best = {}
        out = []
        for ev in evs:
            if ev[0] == "e":
                k = ev[1]
                if k not in best or best[k][2] < ev[2]:
                    best[k] = ev
            else:
                k = ev[3]
                if k not in best or best[k][2] < ev[2]:
                    best[k] = ev
        return list(best.values())

    def dma(self, q, out_ap, in_ap, reads=(), writes=(), sem_tile=None, **kw):
        reads = [r for r in reads if r is not None]
        writes = [w for w in writes if w is not None]
        st = sem_tile if sem_tile is not None else (writes[0] if writes else reads[0])
        cls = "sw" if q == "pool" else "hw"
        if st.dsem is None:
            st.dsem = {}
            st.dcnt = {}
        if cls not in st.dsem:
            st.dsem[cls] = self.sem("d_" + st.name)
            st.dcnt[cls] = 0
        self._deps(q, reads, writes, same_eng_raw=True)
        st.dcnt[cls] += 1
        v = 16 * st.dcnt[cls]
        sem = st.dsem[cls]
        self.streams[q].append(
            lambda h, o=out_ap, i=in_ap, sem=sem, kw=kw: h.dma_start(out=o, in_=i, **kw).then_inc(sem, 16))
        ev = ("d", sem, v, ("d", st.uid, cls))
        for t in writes:
            t.w = [e for e in t.w if e[0] == "d" and e[3][1] == st.uid and e[3] != ev[3]] + [ev]
            t.rd = []
        for t in reads:
            if t not in writes:
                t.rd = [e for e in t.rd if not (e[0] == "d" and e[3] == ev[3])] + [ev]
        return ev

    def collective(self, kind, groups, src, dst):
        self._deps("pool", [src], [dst])
        if dst.dsem is None:
            dst.dsem = {}
            dst.dcnt = {}
        if "cc" not in dst.dsem:
            dst.dsem["cc"] = self.sem("cc_" + dst.name)
            dst.dcnt["cc"] = 0
        dst.dcnt["cc"] += 1
        v = dst.dcnt["cc"]
        sem = dst.dsem["cc"]
        sa, da = src.t.ap().opt(), dst.t.ap().opt()
        self.streams["pool"].append(lambda h: h.collective_compute(
            kind, ALU.bypass, replica_groups=groups, ins=[sa], outs=[da]).then_inc(sem))
        ev = ("d", sem, v, ("d", dst.uid, "cc"))
        dst.w = [ev]
        dst.rd = []
        src.rd.append(ev)
        return ev

    def barrier(self, engs=("pe", "dve", "act", "sp"), tiles=()):
        for e in engs:
            for src in self.ENG:
                if src == e or self.count[src] == 0:
                    continue
                self._wait(e, ("e", src, self.count[src]))
            for t in tiles:
                for ev in t.w + t.rd:
                    if ev[0] == "d":
                        self._wait(e, ev)

    def wait_all(self, eng, tiles):
        for t in tiles:
            for ev in t.w + t.rd:
                self._wait(eng, ev)

    def emit(self):
        nc = self.nc
        with nc.Block() as block:
            def mk(e):
                def body(h):
                    for f in self.streams[e]:
                        f(h)
                return body
            block.tensor(mk("pe"))
            block.vector(mk("dve"))
            block.scalar(mk("act"))
            block.gpsimd(mk("pool"))
            block.sync(mk("sp"))

from concourse.bass_utils import run_bass_kernel_spmd

import math

NCORES = 8
D = 1024
TT = 1024
HT = 512
NCH = 8
DFF = 2816
NF = 22
EPS = 1e-6
DEPTH = 4


class Arena:
    def __init__(self, P, words):
        self.P = P
        self.words = words
        self.t = P.stack.enter_context(P.nc.sbuf_tensor("arena", [128, words], F32))
        self.off = 0
        self.tiles = []

    def alloc(self, shape, dt=F32, name="a"):
        n = 1
        for s in shape[1:]:
            n *= s
        w = n if dt == F32 else (n + 1) // 2
        assert self.off + w <= self.words, ("arena overflow", name, self.off, w, self.words)
        ap = self.t[0:shape[0], self.off:self.off + w]
        if dt != F32:
            ap = ap.bitcast(dt)
        if len(shape) > 2:
            names = " ".join(f"d{i}" for i in range(len(shape) - 1))
            kw = {f"d{i}": shape[i + 1] for i in range(len(shape) - 1)}
            ap = ap.rearrange(f"p ({names}) -> p {names}", **kw)
        self.off += w
        t = T(ap, name)
        self.tiles.append(t)
        return t

    def reset(self, keep=0, keep_tiles=(), pool=False):
        engs = ("pe", "dve", "act", "sp") + (("pool",) if pool else ())
        self.P.barrier(engs=engs, tiles=self.tiles)
        self.off = keep
        self.tiles = list(keep_tiles)


class K:
    pass


def build_program(stage=99):
    nc = bass.Bass("TRN2", target_bir_lowering=False)

    def din(name, shape, dt=F32):
        return nc.dram_tensor(name, list(shape), dt, kind="ExternalInput").ap()

    def dout(name, shape, dt=F32):
        return nc.dram_tensor(name, list(shape), dt, kind="ExternalOutput").ap()

    I = {}
    I["xT"] = din("xT", [128, NCH, TT])
    I["cvec"] = din("cvec", [128, NCH, 2])
    if stage >= 0:
        I["modw"] = din("modw", [DEPTH, 9, 2, 128, NCH * 512])
    I["modb"] = din("modb", [128, DEPTH * 9 * NCH])
    I["normg"] = din("normg", [128, DEPTH * 3 * NCH])
    I["fnormg"] = din("fnormg", [128, NCH])
    if stage >= 0:
        I["ffn_in"] = din("ffn_in", [DEPTH, 2, 11, 128, NCH * 512])
        I["ffn_out"] = din("ffn_out", [DEPTH, 2, NCH, 128, NF * 128])
    I["mla_w1"] = din("mla_w1", [2, 128, NCH * 512])
    I["mla_w2"] = din("mla_w2", [2, 128, NCH * 384])
    I["mla_wq"] = din("mla_wq", [2, 2, 2, 128, 4 * 768])
    I["mla_wkv"] = din("mla_wkv", [2, 128, 2 * 2048])
    I["mla_wo"] = din("mla_wo", [2, 2, 128, 8 * 512])
    I["mla_small"] = din("mla_small", [128, 2 * 6])
    I["ropeT"] = din("ropeT", [96, 2, HT])
    I["cacheT"] = din("cacheT", [2, 288, 256])
    I["ssm_win"] = din("ssm_win", [2, 10, 128, NCH * 512])
    I["ssm_wdt"] = din("ssm_wdt", [2, 128, NCH * 64])
    I["ssm_wout"] = din("ssm_wout", [2, 4, 128, 16 * 256])
    I["ssm_small"] = din("ssm_small", [128, 2 * 192])
    I["ssm_bc"] = din("ssm_bc", [128, 2, 2, 64])
    I["consts"] = din("consts", [128, 5, 128])
    I["posm"] = din("posm", [128, 16])
    I["stateT"] = din("stateT", [2, 2, 128, 2048])
    O = {}
    O["stateO"] = dout("stateO", [2, 2, 2, 128, 2048])
    O["yT"] = dout("yT", [128, NCH, TT])
    O["cacheO"] = dout("cacheO", [2, 288, HT])

    with ExitStack() as st:
        P = Prog(nc, st)
        k = K()
        k.P, k.nc, k.I, k.O = P, nc, I, O
        Xt = st.enter_context(nc.sbuf_tensor("X", [128, NCH, TT], F32))
        k.X = [[T(Xt[:, c, h * HT:(h + 1) * HT], f"X{c}{h}") for h in range(2)] for c in range(NCH)]
        k.Xall = [k.X[c][h] for c in range(NCH) for h in range(2)]
        k.wslots = [P.sb(f"ws{i}", [128, 4096], BF16) for i in range(4)]
        k.wi = 0
        k.out_tiles = []
        k.banks = [P.ps(f"bk{i}", [128, 512], F32) for i in range(8)]
        k.bi = 0
        k.ri = 0
        k.ones = P.sb("ones", [128, 128], BF16)
        k.mod = P.sb("s_mod", [128, DEPTH * 9 * NCH * 2], F32)
        k.modb = P.sb("s_modb", [128, DEPTH * 9 * NCH], F32)
        k.normg = P.sb("s_normg", [128, DEPTH * 3 * NCH], F32)
        k.fnormg = P.sb("s_fnormg", [128, NCH], F32)
        k.cv = P.sb("cv", [128, NCH, 2], F32)
        k.scb = P.sb("scb", [128, NCH, 2], BF16)
        k.gsc = P.sb("gsc", [128, NCH, 2], F32)
        k.hgate = P.sb("hgate", [128, NCH, 2], F32)
        k.arena = Arena(P, 29 * 1024)
        k.ssmall = P.sb("s_ssmall", [128, 384], F32)
        k.sbc = P.sb("s_sbc", [128, 2, 2, 64], F32)
        k.cst = P.sb("s_cst", [128, 5, 128], F32)
        k.posm = P.sb("s_posm", [128, 16], F32)
        k.ones32 = P.sb("ones32", [128, 128], F32)
        k.identb = P.sb("identb", [128, 128], BF16)
        k.oneb = P.sb("oneb", [128, 1], F32)
        k.zer = P.sb("zer", [128, 512], F32)
        P.op("dve", lambda h: h.memset(k.zer[:], 0.0), writes=[k.zer])
        P.dma("sp", k.ssmall[:], I["ssm_small"], writes=[k.ssmall])
        P.dma("sp", k.sbc[:], I["ssm_bc"], writes=[k.sbc])
        P.dma("sp", k.cst[:], I["consts"], writes=[k.cst])
        P.dma("sp", k.posm[:], I["posm"], writes=[k.posm])
        P.op("dve", lambda h: h.memset(k.ones32[:], 1.0), writes=[k.ones32])
        P.op("dve", lambda h: h.memset(k.oneb[:], 1.0), writes=[k.oneb])
        P.op("dve", lambda h: h.tensor_copy(k.identb[:], k.cst[:, 4, :]), reads=[k.cst], writes=[k.identb])
        k.msmall = P.sb("s_msmall", [128, 12], F32)
        k.rope = P.sb("s_rope", [96, 2, HT], F32)
        k.epsb = P.sb("epsb", [128, 1], F32)
        P.op("dve", lambda h: h.memset(k.epsb[:], EPS), writes=[k.epsb])
        P.dma("sp", k.msmall[:], I["mla_small"], writes=[k.msmall])
        P.dma("sp", k.rope[64:96, :, :], I["ropeT"][64:96, :, :], writes=[k.rope])

        P.dma("sp", Xt[:], I["xT"], writes=k.Xall)
        P.dma("sp", k.modb[:], I["modb"], writes=[k.modb])
        P.dma("sp", k.normg[:], I["normg"], writes=[k.normg])
        P.dma("sp", k.fnormg[:], I["fnormg"], writes=[k.fnormg])
        P.dma("sp", k.cv[:], I["cvec"], writes=[k.cv])
        P.op("dve", lambda h: h.memset(k.ones[:], 1.0), writes=[k.ones])
        P.op("act", lambda h: h.activation(k.scb[:], k.cv[:], AF.Silu), reads=[k.cv], writes=[k.scb])

        if stage < 0:
            import os
            k.cut = float(os.environ.get("MLA_CUT", "99"))
            P.op("dve", lambda h: h.memset(k.mod[:], 0.01), writes=[k.mod])
            if stage == -1:
                mla(k, 0, 0)
            else:
                ssm(k, 1, 0)
            k.arena.reset()
        for i in range(DEPTH if stage >= 0 else 0):
            if i == 0:
                modulation(k, 0)
            ffn(k, i, 0)
            k.arena.reset()
            if stage <= 1:
                break
            if i % 2 == 0:
                mla(k, i, i // 2)
            else:
                ssm(k, i, i // 2)
            k.arena.reset()
            if stage == 2 + 2 * i:
                break
            nxt = modulation_gen(k, i + 1) if i + 1 < DEPTH else None
            ffn(k, i, 1, extra=nxt)
            if nxt is not None:
                for _ in nxt:
                    pass
            k.arena.reset()
        final_norm(k)
        P.emit()
    nc._in_names = list(I.keys())
    return nc


def _mark(k, name):
    pass


def next_bank(k):
    b = k.banks[k.bi % 6]
    k.bi += 1
    return b


def acc_bank(k):
    b = k.banks[6 + k.ri % 2]
    k.ri += 1
    return b


def next_slot(k):
    s = k.wslots[k.wi % 4]
    k.wi += 1
    return s


def modv(k, i, kk):
    base = ((i * 9 + kk) * NCH) * 2
    return k.mod[:, base:base + NCH * 2].rearrange("p (c r) -> p c r", r=2)


def modulation(k, i):
    for _ in modulation_gen(k, i):
        pass


def modulation_gen(k, i):
    P = k.P
    for kk in range(9):
        for hc in range(2):
            yield
            s = next_slot(k)
            P.dma("pool", s[:, 0:NCH * 512], k.I["modw"][i, kk, hc], writes=[s])
            w = s[:, 0:NCH * 512].rearrange("p (c n) -> p c n", n=512)
            b = next_bank(k)
            for m in range(4):
                for c in range(NCH):
                    P.op("pe", lambda h, m=m, c=c, b=b, w=w: h.matmul(
                        b[:, m * 2:m * 2 + 2], w[:, c, m * 128:(m + 1) * 128], k.scb[:, c, :],
                        start=(c == 0), stop=(c == NCH - 1)),
                        reads=[s, k.scb], writes=[b], sig=(c == NCH - 1))
            base = (i * 9 + kk) * NCH + hc * 4
            ob = base * 2
            P.op("dve", lambda h, b=b, base=base, ob=ob: h.tensor_tensor(
                k.mod[:, ob:ob + 8].rearrange("p (c r) -> p c r", r=2),
                b[:, 0:8].rearrange("p (c r) -> p c r", r=2),
                k.modb[:, base:base + 4].unsqueeze(2).broadcast_to([128, 4, 2]), ALU.add),
                reads=[b, k.modb], writes=[k.mod])


def rsqrt_from_bank(k, b, out_rstd, n):
    P = k.P
    P.op("act", lambda hh: hh.activation(out_rstd[:], b[:], AF.Ln, bias=k.epsb[:], scale=1.0 / n),
         reads=[b, k.epsb], writes=[out_rstd])
    P.op("act", lambda hh: hh.activation(out_rstd[:], out_rstd[:], AF.Exp, scale=-0.5),
         reads=[out_rstd], writes=[out_rstd])


def rms_stats(k, h, out_rstd):
    P = k.P
    A = k.arena
    b = next_bank(k)
    for c in range(NCH):
        sq = k.sq[c % 2]
        xt = k.X[c][h]
        P.op("act", lambda hh, sq=sq, xt=xt: hh.activation(sq[:], xt[:], AF.Square), reads=[xt], writes=[sq])
        P.op("pe", lambda hh, sq=sq, b=b, c=c: hh.matmul(b[:], k.ones[:], sq[:], start=(c == 0), stop=(c == NCH - 1)),
             reads=[sq, k.ones], writes=[b], sig=True)
    P.op("act", lambda hh: hh.activation(out_rstd[:], b[:], AF.Ln, bias=k.epsb[:], scale=1.0 / D),
         reads=[b, k.epsb], writes=[out_rstd])
    P.op("act", lambda hh: hh.activation(out_rstd[:], out_rstd[:], AF.Exp, scale=-0.5),
         reads=[out_rstd], writes=[out_rstd])


def adaln(k, i, ksh, ksc, ng, halves=(0, 1)):
    P = k.P
    A = k.arena
    k.sq = [A.alloc([128, HT], BF16, f"sq{j}") for j in range(2)]
    rstd = [A.alloc([128, HT], F32, f"rstd{h}") if h in halves else None for h in range(2)]
    tmp = [A.alloc([128, HT], F32, f"ntmp{j}") for j in range(2)]
    k.last_rstd, k.last_tmp = rstd, tmp
    HN = [[A.alloc([128, HT], BF16, f"hn{c}{h}") if h in halves else None for h in range(2)] for c in range(NCH)]
    gb = (i * 3 + ng) * NCH
    sc = modv(k, i, ksc)
    sh = modv(k, i, ksh)
    P.op("dve", lambda h: h.scalar_tensor_tensor(
        k.gsc[:], sc, 1.0, k.normg[:, gb:gb + NCH].unsqueeze(2).broadcast_to([128, NCH, 2]), ALU.add, ALU.mult),
        reads=[k.mod, k.normg], writes=[k.gsc])
    for h in halves:
        rms_stats(k, h, rstd[h])
    for h in halves:
        for c in range(NCH):
            t = tmp[c % 2]
            xt = k.X[c][h]
            P.op("dve", lambda hh, t=t, xt=xt, c=c, h=h: hh.scalar_tensor_tensor(
                t[:], xt[:], k.gsc[:, c, h:h + 1], rstd[h][:], ALU.mult, ALU.mult),
                reads=[xt, k.gsc, rstd[h]], writes=[t])
            P.op("act", lambda hh, t=t, c=c, h=h: hh.activation(
                HN[c][h][:], t[:], AF.Identity, bias=sh[:, c, h:h + 1], scale=1.0),
                reads=[t, k.mod], writes=[HN[c][h]])
    return HN


def ffn(k, i, j, extra=None):
    P = k.P
    A = k.arena
    k3 = 0 if j == 0 else 6
    HN = adaln(k, i, k3 + 0, k3 + 1, 0 if j == 0 else 2)
    ACTT = [[A.alloc([128, HT], BF16, f"act{f}{h}") for h in range(2)] for f in range(NF)]
    sg = [A.alloc([128, HT], F32, f"sg{j2}") for j2 in range(2)]
    gt = modv(k, i, k3 + 2)
    P.op("dve", lambda h: h.tensor_scalar(k.hgate[:], gt, 0.5, 0.0, ALU.mult, ALU.add), reads=[k.mod], writes=[k.hgate])
    n = 0
    for g in range(11):
        if extra is not None:
            next(extra, None)
        s = next_slot(k)
        P.dma("pool", s[:, 0:NCH * 512], k.I["ffn_in"][i, j, g], writes=[s])
        w = s[:, 0:NCH * 512].rearrange("p (c n) -> p c n", n=512)
        for m in range(2):
            f = 2 * g + m
            for h in range(2):
                ba = next_bank(k)
                bb = next_bank(k)
                for (bk, co) in ((ba, m * 128), (bb, 256 + m * 128)):
                    for c in range(NCH):
                        P.op("pe", lambda hh, bk=bk, co=co, c=c, h=h, w=w: hh.matmul(
                            bk[:], w[:, c, co:co + 128], HN[c][h][:], start=(c == 0), stop=(c == NCH - 1)),
                            reads=[s, HN[c][h]], writes=[bk], sig=(c == NCH - 1))
                sgt = sg[n % 2]
                n += 1
                P.op("act", lambda hh, sgt=sgt, ba=ba: hh.activation(sgt[:], ba[:], AF.Silu), reads=[ba], writes=[sgt])
                P.op("dve", lambda hh, sgt=sgt, bb=bb, f=f, h=h: hh.tensor_tensor(
                    ACTT[f][h][:], sgt[:], bb[:], ALU.mult), reads=[sgt, bb], writes=[ACTT[f][h]])
    for dc in range(NCH):
        if extra is not None:
            next(extra, None)
        s = next_slot(k)
        P.dma("pool", s[:, 0:NF * 128], k.I["ffn_out"][i, j, dc], writes=[s])
        w = s[:, 0:NF * 128].rearrange("p (f n) -> p f n", n=128)
        for h in range(2):
            b = next_bank(k)
            for f in range(NF):
                P.op("pe", lambda hh, b=b, f=f, h=h, w=w: hh.matmul(
                    b[:], w[:, f, :], ACTT[f][h][:], start=(f == 0), stop=(f == NF - 1)),
                    reads=[s, ACTT[f][h]], writes=[b], sig=(f == NF - 1))
            xt = k.X[dc][h]
            P.op("dve", lambda hh, b=b, xt=xt, dc=dc, h=h: hh.scalar_tensor_tensor(
                xt[:], b[:], k.hgate[:, dc, h:h + 1], xt[:], ALU.mult, ALU.add),
                reads=[b, k.hgate, xt], writes=[xt])


GROUPS4 = [[0, 1, 2, 3], [4, 5, 6, 7]]
NKS = 2304
NKC = 18


def mla(k, i, j):
    P, A, I = k.P, k.arena, k.I
    scale = 1.0 / math.sqrt(96.0)
    QT = A.alloc([96, 16, TT], BF16, "QT")
    CKVb = [[A.alloc([128, HT], BF16, f"ckvb{m}{h}") for h in range(2)] for m in range(2)]
    KRb = A.alloc([128, TT], BF16, "KRb")
    P.op("dve", lambda hh: hh.memset(KRb[:], 0.0), writes=[KRb])
    LAT = A.alloc([128, 3, NKS], BF16, "LAT")
    keep, keep_tiles = A.off, list(A.tiles)
    HN = adaln(k, i, 3, 4, 1)
    QA = [[A.alloc([128, HT], F32, f"qa{m}{h}") for h in range(2)] for m in range(4)]
    QN = [[A.alloc([128, HT], BF16, f"qn{m}{h}") for h in range(2)] for m in range(4)]
    CKVf = [[A.alloc([128, HT], F32, f"ckvf{m}{h}") for h in range(2)] for m in range(2)]
    KRf = A.alloc([96, HT], F32, "KRf")
    rq = k.last_rstd
    rk = k.last_rstd
    t1 = [k.last_tmp[0], A.alloc([96, HT], F32, "t1b")]
    t2 = [k.last_tmp[1], A.alloc([96, HT], F32, "t2b")]
    qn = k.msmall[:, j * 6:j * 6 + 4]
    kvn = k.msmall[:, j * 6 + 4:j * 6 + 6]
    cosr = k.rope[64:96, 0, :]
    sinr = k.rope[64:96, 1, :]

    s1 = next_slot(k)
    P.dma("pool", s1[:, 0:NCH * 512], I["mla_w1"][j], writes=[s1])
    w1 = s1[:, 0:NCH * 512].rearrange("p (c n) -> p c n", n=512)
    s2 = next_slot(k)
    P.dma("pool", s2[:, 0:NCH * 384], I["mla_w2"][j], writes=[s2])
    w2 = s2[:, 0:NCH * 384].rearrange("p (c n) -> p c n", n=384)

    def proj_norm(ws, w, col0, nm, raw, rstd, nfeat):
        for h in range(2):
            sb = acc_bank(k)
            for m in range(nm):
                b = next_bank(k)
                for c in range(NCH):
                    P.op("pe", lambda hh, b=b, c=c, m=m, h=h: hh.matmul(
                        b[:], w[:, c, col0 + m * 128:col0 + (m + 1) * 128], HN[c][h][:],
                        start=(c == 0), stop=(c == NCH - 1)), reads=[ws, HN[c][h]], writes=[b], sig=(c == NCH - 1))
                sq = k.sq[m % 2]
                P.op("act", lambda hh, sq=sq, b=b: hh.activation(sq[:], b[:], AF.Square), reads=[b], writes=[sq])
                P.op("pe", lambda hh, sq=sq, sb=sb, m=m: hh.matmul(sb[:], k.ones[:], sq[:], start=(m == 0), stop=(m == nm - 1)),
                     reads=[sq, k.ones], writes=[sb])
                P.op("dve", lambda hh, b=b, m=m, h=h: hh.tensor_tensor(raw[m][h][:], b[:], k.zer[:], ALU.add), reads=[b, k.zer], writes=[raw[m][h]])
            rsqrt_from_bank(k, sb, rstd[h], nfeat)

    if getattr(k, "cut", 99) <= -4:
        return
    proj_norm(s1, w1, 0, 4, QA, rq, 512)
    if getattr(k, "cut", 99) <= -3.5:
        return
    for h in range(2):
        for m in range(4):
            P.op("dve", lambda hh, m=m, h=h: hh.scalar_tensor_tensor(
                QN[m][h][:], QA[m][h][:], qn[:, m:m + 1], rq[h][:], ALU.mult, ALU.mult),
                reads=[QA[m][h], k.msmall, rq[h]], writes=[QN[m][h]])
    if getattr(k, "cut", 99) <= -3:
        return
    KVA = CKVf
    proj_norm(s2, w2, 0, 2, KVA, rk, 256)
    for h in range(2):
        for m in range(2):
            P.op("dve", lambda hh, m=m, h=h: hh.scalar_tensor_tensor(
                CKVf[m][h][:], KVA[m][h][:], kvn[:, m:m + 1], rk[h][:], ALU.mult, ALU.mult),
                reads=[KVA[m][h], k.msmall, rk[h]], writes=[CKVf[m][h]])
            P.op("act", lambda hh, m=m, h=h: hh.activation(CKVb[m][h][:], CKVf[m][h][:], AF.Copy),
                 reads=[CKVf[m][h]], writes=[CKVb[m][h]])
    if getattr(k, "cut", 99) <= -2:
        return
    for h in range(2):
        if getattr(k, "cut", 99) <= -1 and h == 1:
            return
        bk = next_bank(k)
        for c in range(NCH):
            P.op("pe", lambda hh, bk=bk, c=c, h=h: hh.matmul(bk[0:96, :], w2[:, c, 192:288], HN[c][h][:],
                 start=(c == 0), stop=(c == NCH - 1)), reads=[s2, HN[c][h]], writes=[bk], sig=(c == NCH - 1))
        if h == 0:
            P.op("dve", lambda hh, bk=bk: hh.tensor_tensor(KRf[64:96, :], bk[64:96, :], k.zer[64:96, :], ALU.add), reads=[bk, k.zer], writes=[KRf])
            P.op("act", lambda hh, bk=bk: hh.activation(KRb[64:96, 0:HT], bk[64:96, :], AF.Copy), reads=[bk], writes=[KRb])
        else:
            bs = next_bank(k)
            for c in range(NCH):
                P.op("pe", lambda hh, bs=bs, c=c, h=h: hh.matmul(bs[0:96, :], w2[:, c, 288:384], HN[c][h][:],
                     start=(c == 0), stop=(c == NCH - 1)), reads=[s2, HN[c][h]], writes=[bs], sig=(c == NCH - 1))
            P.op("dve", lambda hh, bk=bk: hh.tensor_tensor(t1[0][64:96, :], bk[64:96, :], cosr, ALU.mult),
                 reads=[bk, k.rope], writes=[t1[0]])
            P.op("dve", lambda hh, bs=bs: hh.tensor_tensor(t2[0][64:96, :], bs[64:96, :], sinr, ALU.mult),
                 reads=[bs, k.rope], writes=[t2[0]])
            P.op("dve", lambda hh: hh.tensor_tensor(KRb[64:96, HT:TT], t1[0][64:96, :], t2[0][64:96, :], ALU.add),
                 reads=[t1[0], t2[0]], writes=[KRb])
    if getattr(k, "cut", 99) <= 0:
        return
    for m in range(2):
        P.dma("sp", k.O["cacheO"][j, m * 128:(m + 1) * 128, :], CKVf[m][0][:], reads=[CKVf[m][0]])
    P.dma("sp", k.O["cacheO"][j, 256:288, :], KRf[64:96, :], reads=[KRf])
    if getattr(k, "cut", 99) <= 1:
        return
    latb = P.dram(f"latb{j}", [384, HT], BF16)
    latg = P.dram(f"latg{j}", [4 * 384, HT], BF16)
    for m in range(2):
        P.dma("sp", latb.t.ap()[m * 128:(m + 1) * 128, :], CKVb[m][1][:], reads=[CKVb[m][1]], writes=[latb])
    P.dma("sp", latb.t.ap()[256:384, :], KRb[:, HT:TT], reads=[KRb], writes=[latb])
    import os
    if os.environ.get("NO_CC"):
        P.dma("sp", latg.t.ap()[0:384, :], latb.t.ap(), reads=[latb], writes=[latg])
        for r_ in range(1, 4):
            P.dma("sp", latg.t.ap()[r_ * 384:(r_ + 1) * 384, :], latb.t.ap(), reads=[latb], writes=[latg])
    else:
        P.collective("AllGather", GROUPS4, latb, latg)
    if getattr(k, "cut", 99) <= 2:
        return
    for pc in range(2):
        sq_ = next_slot(k)
        P.dma("pool", sq_[:, 0:4 * 768], I["mla_wq"][j, 0, pc], writes=[sq_])
        wq = sq_[:, 0:4 * 768].rearrange("p (c n) -> p c n", n=768)
        ss_ = next_slot(k)
        P.dma("pool", ss_[:, 0:4 * 768], I["mla_wq"][j, 1, pc], writes=[ss_])
        wqs = ss_[:, 0:4 * 768].rearrange("p (c n) -> p c n", n=768)
        for hl in range(8):
            hd = pc * 8 + hl
            b0 = next_bank(k)
            for c in range(4):
                P.op("pe", lambda hh, b0=b0, c=c, hl=hl, wq=wq: hh.matmul(b0[0:96, :], wq[:, c, hl * 96:(hl + 1) * 96], QN[c][0][:],
                     start=(c == 0), stop=(c == 3)), reads=[sq_, QN[c][0]], writes=[b0], sig=(c == 3))
            P.op("act", lambda hh, b0=b0, hd=hd: hh.activation(QT[0:96, hd, 0:HT], b0[0:96, :], AF.Copy), reads=[b0], writes=[QT], waw=False)
            b1 = next_bank(k)
            for c in range(4):
                P.op("pe", lambda hh, b1=b1, c=c, hl=hl, wq=wq: hh.matmul(b1[0:96, :], wq[:, c, hl * 96:(hl + 1) * 96], QN[c][1][:],
                     start=(c == 0), stop=(c == 3)), reads=[sq_, QN[c][1]], writes=[b1], sig=(c == 3))
            b2 = next_bank(k)
            for c in range(4):
                P.op("pe", lambda hh, b2=b2, c=c, hl=hl, wqs=wqs: hh.matmul(b2[0:96, :], wqs[:, c, hl * 96:(hl + 1) * 96], QN[c][1][:],
                     start=(c == 0), stop=(c == 3)), reads=[ss_, QN[c][1]], writes=[b2], sig=(c == 3))
            P.op("act", lambda hh, b1=b1, hd=hd: hh.activation(QT[0:64, hd, HT:TT], b1[0:64, :], AF.Copy), reads=[b1], writes=[QT], waw=False)
            ta, tb = t1[hl % 2], t2[hl % 2]
            P.op("dve", lambda hh, b1=b1, ta=ta: hh.tensor_tensor(ta[64:96, :], b1[64:96, :], cosr, ALU.mult),
                 reads=[b1, k.rope], writes=[ta])
            P.op("dve", lambda hh, b2=b2, tb=tb: hh.tensor_tensor(tb[64:96, :], b2[64:96, :], sinr, ALU.mult),
                 reads=[b2, k.rope], writes=[tb])
            P.op("dve", lambda hh, ta=ta, tb=tb, hd=hd: hh.tensor_tensor(QT[64:96, hd, HT:TT], ta[64:96, :], tb[64:96, :], ALU.add),
                 reads=[ta, tb], writes=[QT], waw=False)

    lg = latg.t.ap().rearrange("(r c p) n -> p c r n", r=4, c=3)
    for m in range(3):
        P.dma("sp", LAT[:, m, 0:2048].rearrange("p (r n) -> p r n", r=4), lg[:, m, :, :], reads=[latg], writes=[LAT])
    for m in range(2):
        P.dma("pool", LAT[:, m, 2048:NKS], I["cacheT"][j, m * 128:(m + 1) * 128, :], writes=[LAT])
    P.dma("pool", LAT[64:96, 2, 2048:NKS], I["cacheT"][j, 256:288, :], writes=[LAT])

    if getattr(k, "cut", 99) <= 3:
        return
    _mark(k, "attn")
    A.reset(keep, keep_tiles)
    KTs = [A.alloc([96, NKS], BF16, f"KTs{q}") for q in range(2)]
    KTp = [A.alloc([96, HT], BF16, f"KTp{q}") for q in range(2)]
    VEs = [A.alloc([128, NKC, 128], BF16, f"VEs{q}") for q in range(2)]
    VEp = [A.alloc([128, 4, 128], BF16, f"VEp{q}") for q in range(2)]
    PT = [A.alloc([128, HT], BF16, f"PT{q}") for q in range(5)]
    OT = [A.alloc([128, TT], BF16, f"OT{q}") for q in range(8)]
    rec = [A.alloc([128, HT], F32, f"rec{q}") for q in range(2)]
    for q in range(2):
        P.op("dve", lambda hh, q=q: hh.tensor_copy(KTs[q][64:96, :], LAT[64:96, 2, :]), reads=[LAT], writes=[KTs[q]])
        P.op("dve", lambda hh, q=q: hh.tensor_copy(KTp[q][64:96, :], KRb[64:96, 0:HT]), reads=[KRb], writes=[KTp[q]])
        oc = 64 if q == 0 else 0
        P.op("dve", lambda hh, q=q, oc=oc: hh.memset(VEs[q][:, :, oc:oc + 64], 1.0), writes=[VEs[q]])
        P.op("dve", lambda hh, q=q, oc=oc: hh.memset(VEp[q][:, :, oc:oc + 64], 1.0), writes=[VEp[q]])
    sk = next_slot(k)
    P.dma("pool", sk[:, 0:4096], I["mla_wkv"][j], writes=[sk])
    wkv = sk[:, 0:4096].rearrange("p (c n) -> p c n", n=2048)
    ncp = [0]

    def evac(dst_ap, src_ap, reads, writes, zview=None):
        ncp[0] += 1
        if ncp[0] % 2 == 0 or zview is None:
            P.op("act", lambda hh: hh.activation(dst_ap, src_ap, AF.Copy), reads=reads, writes=writes, waw=False)
        else:
            P.op("dve", lambda hh: hh.tensor_tensor(dst_ap, src_ap, zview, ALU.add), reads=list(reads) + [k.zer], writes=writes, waw=False)

    npt = [0]

    def attend(hd, KT, VE, kcs, q0, nq, Ob):
        LOOK = 3
        sbs = {}

        def s_mm(n_):
            kc = kcs[n_]
            Sb = next_bank(k)
            P.op("pe", lambda hh: hh.matmul(Sb[:, 0:nq], KT[0:96, kc * 128:(kc + 1) * 128], QT[0:96, hd, q0:q0 + nq],
                 start=True, stop=True), reads=[KT, QT], writes=[Sb])
            sbs[n_] = Sb

        for n_ in range(min(LOOK, len(kcs))):
            s_mm(n_)
        for n_, kc in enumerate(kcs):
            Sb = sbs.pop(n_)
            pt = PT[npt[0] % 5]
            npt[0] += 1
            P.op("act", lambda hh: hh.activation(pt[:, 0:nq], Sb[:, 0:nq], AF.Exp, scale=scale), reads=[Sb], writes=[pt])
            if n_ + LOOK < len(kcs):
                s_mm(n_ + LOOK)
            P.op("pe", lambda hh: hh.matmul(Ob[:, 0:nq], VE[:, kc, :], pt[:, 0:nq],
                 start=(n_ == 0), stop=(n_ == len(kcs) - 1)), reads=[VE, pt], writes=[Ob], sig=(n_ == len(kcs) - 1))

    def finish(hd, Ob, q0, nq):
        par = hd % 2
        o0, s0 = (0, 64) if par == 0 else (64, 0)
        r = rec[par]
        P.op("dve", lambda hh: hh.tensor_tensor(r[s0:s0 + 64, 0:nq], Ob[s0:s0 + 64, 0:nq], k.zer[s0:s0 + 64, 0:nq], ALU.add), reads=[Ob, k.zer], writes=[r])
        P.op("dve", lambda hh: hh.reciprocal(r[s0:s0 + 64, 0:nq], r[s0:s0 + 64, 0:nq]), reads=[r], writes=[r])
        P.op("dve", lambda hh: hh.tensor_tensor(OT[hd // 2][o0:o0 + 64, q0:q0 + nq], Ob[o0:o0 + 64, 0:nq], r[s0:s0 + 64, 0:nq], ALU.mult),
             reads=[Ob, r], writes=[OT[hd // 2]], waw=False)

    def build_kv(hd):
        par = hd % 2
        voff = 0 if par == 0 else 64
        kcol = hd * 128
        for sl in range(5):
            n = 512 if sl < 4 else 256
            b = next_bank(k)
            for c in range(2):
                P.op("pe", lambda hh, b=b, c=c, sl=sl, n=n: hh.matmul(b[0:64, 0:n], wkv[:, c, kcol:kcol + 64], LAT[:, c, sl * 512:sl * 512 + n],
                     start=(c == 0), stop=(c == 1)), reads=[sk, LAT], writes=[b], sig=(c == 1))
            evac(KTs[par][0:64, sl * 512:sl * 512 + n], b[0:64, 0:n], [b], [KTs[par]], k.zer[0:64, 0:n])
        b = next_bank(k)
        for c in range(2):
            P.op("pe", lambda hh, b=b, c=c: hh.matmul(b[0:64, :], wkv[:, c, kcol:kcol + 64], CKVb[c][0][:],
                 start=(c == 0), stop=(c == 1)), reads=[sk, CKVb[c][0]], writes=[b], sig=(c == 1))
        evac(KTp[par][0:64, :], b[0:64, :], [b], [KTp[par]], k.zer[0:64, :])
        for g0 in range(0, NKC, 8):
            ng = min(8, NKC - g0)
            b = next_bank(k)
            for q in range(ng):
                kc = g0 + q
                for c in range(2):
                    P.op("pe", lambda hh, b=b, c=c, kc=kc, q=q: hh.matmul(b[:, q * 64:(q + 1) * 64], LAT[:, c, kc * 128:(kc + 1) * 128],
                         wkv[:, c, kcol + 64:kcol + 128], start=(c == 0), stop=(c == 1)),
                         reads=[sk, LAT], writes=[b], sig=(c == 1 and q == ng - 1))
            evac(VEs[par][:, g0:g0 + ng, voff:voff + 64], b[:, 0:ng * 64].rearrange("p (q n) -> p q n", n=64), [b], [VEs[par]],
                 k.zer[:, 0:ng * 64].rearrange("p (q n) -> p q n", n=64))
        b = next_bank(k)
        for q in range(4):
            for c in range(2):
                P.op("pe", lambda hh, b=b, c=c, q=q: hh.matmul(b[:, q * 64:(q + 1) * 64], CKVb[c][0][:, q * 128:(q + 1) * 128],
                     wkv[:, c, kcol + 64:kcol + 128], start=(c == 0), stop=(c == 1)),
                     reads=[sk, CKVb[c][0]], writes=[b], sig=(c == 1 and q == 3))
        evac(VEp[par][:, 0:4, voff:voff + 64], b[:, 0:256].rearrange("p (q n) -> p q n", n=64), [b], [VEp[par]],
             k.zer[:, 0:256].rearrange("p (q n) -> p q n", n=64))
    def do_attn(hd):
        par = hd % 2
        Ob = acc_bank(k)
        attend(hd, KTs[par], VEs[par], list(range(NKC)), HT, HT, Ob)
        finish(hd, Ob, HT, HT)
        for s_ in range(2):
            Ob = acc_bank(k)
            attend(hd, KTp[par], VEp[par], [2 * s_, 2 * s_ + 1], s_ * 256, 256, Ob)
            finish(hd, Ob, s_ * 256, 256)

    build_kv(0)
    for hd in range(16):
        if hd + 1 < 16:
            build_kv(hd + 1)
        do_attn(hd)
    if getattr(k, "cut", 99) <= 5:
        return
    _mark(k, "wo")
    g5 = modv(k, i, 5)
    for half in range(2):
        so = next_slot(k)
        P.dma("pool", so[:, 0:4096], I["mla_wo"][j, half], writes=[so])
        wo = so[:, 0:4096].rearrange("p (h n) -> p h n", n=512)
        for dl in range(4):
            dc = half * 4 + dl
            for h in range(2):
                b = next_bank(k)
                for hp in range(8):
                    P.op("pe", lambda hh, b=b, hp=hp, dl=dl, h=h, wo=wo: hh.matmul(b[:], wo[:, hp, dl * 128:(dl + 1) * 128],
                         OT[hp][:, h * HT:(h + 1) * HT], start=(hp == 0), stop=(hp == 7)),
                         reads=[so, OT[hp]], writes=[b], sig=(hp == 7))
                xt = k.X[dc][h]
                P.op("dve", lambda hh, b=b, xt=xt, dc=dc, h=h: hh.scalar_tensor_tensor(
                    xt[:], b[:], g5[:, dc, h:h + 1], xt[:], ALU.mult, ALU.add), reads=[b, k.mod, xt], writes=[xt])


def ssm(k, i, j):
    hg = ssm_halo_prepass(k, i, j)
    k.arena.reset()
    for hf in range(2):
        ssm_half(k, i, j, hf, hg)
        k.arena.reset()


def ssm_halo_prepass(k, i, j):
    P, A, I = k.P, k.arena, k.I
    HN = adaln(k, i, 3, 4, 1, halves=(1,))
    eb = acc_bank(k)
    for pi in range(4, 10):
        s = next_slot(k)
        P.dma("pool", s[:, 0:NCH * 512], I["ssm_win"][j, pi], writes=[s])
        w = s[:, 0:NCH * 512].rearrange("p (c n) -> p c n", n=512)
        for m in range(4):
            ch = (pi - 4) * 4 + m
            for (o0, t0) in ((0, 0), (2, HT - 2)):
                for c in range(NCH):
                    P.op("pe", lambda hh, c=c, m=m, ch=ch, o0=o0, t0=t0, w=w: hh.matmul(
                        eb[:, ch * 4 + o0:ch * 4 + o0 + 2], w[:, c, m * 128:(m + 1) * 128], HN[c][1][:, t0:t0 + 2],
                        start=(c == 0), stop=(c == NCH - 1)), reads=[s, HN[c][1]], writes=[eb],
                        sig=(c == NCH - 1 and o0 == 2 and m == 3))
    EDGE = A.alloc([128, 96], F32, "EDGE")
    P.op("dve", lambda hh: hh.tensor_tensor(EDGE[:], eb[:, 0:96], k.zer[:, 0:96], ALU.add), reads=[eb, k.zer], writes=[EDGE])
    hb = P.dram(f"hb{j}", [128, 96], F32)
    hg = P.dram(f"hg{j}", [4 * 128, 96], F32)
    P.dma("sp", hb.t.ap(), EDGE[:], reads=[EDGE], writes=[hb])
    P.collective("AllGather", GROUPS4, hb, hg)
    return hg


def ssm_half(k, i, j, hf, hg):
    P, A, I = k.P, k.arena, k.I
    sm0 = j * 192
    convw = k.ssmall[:, sm0:sm0 + 120].rearrange("p (c w) -> p c w", w=5)
    convb = k.ssmall[:, sm0 + 120:sm0 + 144]
    ngs = k.ssmall[:, sm0 + 144:sm0 + 160]
    dd = k.ssmall[:, sm0 + 160:sm0 + 192].rearrange("p (c r) -> p c r", r=2)
    dtb_bc = k.sbc[:, j, 0, :]
    alog_bc = k.sbc[:, j, 1, :]
    triI = [k.cst[:, 0, :], k.cst[:, 1, :]]
    SLm = [k.cst[:, 2, :], k.cst[:, 3, :]]
    ones32 = k.ones32
    TCS = [slice(tc * 128, (tc + 1) * 128) for tc in range(4)]

    YT = A.alloc([128, 16, HT], F32, "YT")
    n_y = (A.off, list(A.tiles))
    XTOK = A.alloc([128, 4, 2048], BF16, "XTOK")
    BTOK = A.alloc([128, 4, 512], BF16, "BTOK")
    BT = [A.alloc([128, HT], BF16, f"BT{g}") for g in range(4)]
    CT = [A.alloc([128, HT], BF16, f"CT{g}") for g in range(4)]
    DT = [A.alloc([128, 64], F32, f"DT{t}") for t in range(4)]
    DTA = [A.alloc([128, 64], F32, f"DTA{t}") for t in range(4)]
    n_k = (A.off, list(A.tiles))
    HN = adaln(k, i, 3, 4, 1, halves=(hf,))
    dsum = A.alloc([128, 16], F32, "dsum")
    P.op("dve", lambda hh: hh.tensor_tensor(dsum[:], dd[:, :, 0], dd[:, :, 1], ALU.add), reads=[k.ssmall], writes=[dsum])
    NA = A.alloc([128, 64], F32, "NA")
    P.op("act", lambda hh: hh.activation(NA[:], alog_bc, AF.Exp), reads=[k.sbc], writes=[NA])
    P.op("dve", lambda hh: hh.tensor_scalar(NA[:], NA[:], -1.0, 0.0, ALU.mult, ALU.add), reads=[NA], writes=[NA])
    sdt = next_slot(k)
    P.dma("pool", sdt[:, 0:NCH * 64], I["ssm_wdt"][j], writes=[sdt])
    wdt = sdt[:, 0:NCH * 64].rearrange("p (c n) -> p c n", n=64)
    ut = [A.alloc([128, 64], F32, f"ut{q}") for q in range(2)]
    for tc in range(4):
        b = next_bank(k)
        for c in range(NCH):
            P.op("pe", lambda hh, b=b, c=c, tc=tc: hh.matmul(b[:, 0:64], HN[c][hf][:, TCS[tc]], wdt[:, c, :],
                 start=(c == 0), stop=(c == NCH - 1)), reads=[sdt, HN[c][hf]], writes=[b], sig=(c == NCH - 1))
        u = ut[tc % 2]
        P.op("dve", lambda hh, b=b, u=u: hh.tensor_tensor(u[:], b[:, 0:64], dtb_bc, ALU.add), reads=[b, k.sbc], writes=[u])
        P.op("act", lambda hh, u=u: hh.activation(u[:], u[:], AF.Exp), reads=[u], writes=[u])
        P.op("act", lambda hh, u=u, tc=tc: hh.activation(DT[tc][:], u[:], AF.Ln, bias=k.oneb[:], scale=1.0), reads=[u, k.oneb], writes=[DT[tc]])
        P.op("dve", lambda hh, tc=tc: hh.tensor_tensor(DTA[tc][:], DT[tc][:], NA[:], ALU.mult), reads=[DT[tc], NA], writes=[DTA[tc]])

    HALO = None
    if hf == 1:
        G = A.alloc([128, 4, 96], F32, "G")
        P.dma("sp", G[:], hg.t.ap().rearrange("(r p) n -> p r n", p=128), reads=[hg], writes=[G])
        HALO = A.alloc([128, 24, 4], F32, "HALO")
        Gv = G[:].rearrange("p r (c e) -> p r c e", e=4)
        for (dst, src, mo) in ((slice(0, 2), slice(2, 4), 8), (slice(2, 4), slice(0, 2), 12)):
            P.op("dve", lambda hh, dst=dst, src=src, mo=mo: hh.tensor_scalar(
                HALO[:, :, dst], Gv[:, 0, :, src], k.posm[:, mo:mo + 1], 0.0, ALU.mult, ALU.add),
                reads=[G, k.posm], writes=[HALO])
            for r in range(1, 4):
                P.op("dve", lambda hh, dst=dst, src=src, mo=mo, r=r: hh.scalar_tensor_tensor(
                    HALO[:, :, dst], Gv[:, r, :, src], k.posm[:, mo + r:mo + r + 1], HALO[:, :, dst], ALU.mult, ALU.add),
                    reads=[G, k.posm, HALO], writes=[HALO])

    _mark(k, "xbc")
    PRE = [A.alloc([128, 520], F32, f"PRE{q}") for q in range(3)]
    for q in range(3):
        P.op("dve", lambda hh, q=q: hh.memset(PRE[q][:], 0.0), writes=[PRE[q]])
    acc = [A.alloc([128, 516], F32, f"cacc{q}") for q in range(2)]
    sil = [A.alloc([128, HT], F32, f"sil{q}") for q in range(2)]
    XSr = [A.alloc([128, HT], BF16, f"XSr{q}") for q in range(3)]
    NU = 516 if hf == 0 else 512
    slots = {}

    def chunk_gen(pi, m, n):
        if pi not in slots:
            s_ = next_slot(k)
            P.dma("pool", s_[:, 0:NCH * 512], I["ssm_win"][j, pi], writes=[s_])
            slots[pi] = s_
        s = slots[pi]
        w = s[:, 0:NCH * 512].rearrange("p (c n) -> p c n", n=512)
        ch = (pi - 4) * 4 + m
        b = next_bank(k)
        for c in range(NCH):
            P.op("pe", lambda hh, b=b, c=c, m=m, w=w: hh.matmul(b[:], w[:, c, m * 128:(m + 1) * 128], HN[c][hf][:],
                 start=(c == 0), stop=(c == NCH - 1)), reads=[s, HN[c][hf]], writes=[b], sig=(c == NCH - 1))
        pre = PRE[n % 3]
        a = acc[n % 2]
        if hf == 0:
            P.op("act", lambda hh, b=b, pre=pre: hh.activation(pre[:, 2:258], b[:, 0:256], AF.Copy), reads=[b], writes=[pre])
            P.op("act", lambda hh, b=b, pre=pre: hh.activation(pre[:, 262:518], b[:, 256:512], AF.Copy), reads=[b], writes=[pre])
        else:
            P.op("act", lambda hh, b=b, pre=pre: hh.activation(pre[:, 2:514], b[:], AF.Copy), reads=[b], writes=[pre])
            P.op("dve", lambda hh, pre=pre, ch=ch: hh.tensor_copy(pre[:, 0:2], HALO[:, ch, 0:2]), reads=[HALO], writes=[pre])
            P.op("dve", lambda hh, pre=pre, ch=ch: hh.tensor_copy(pre[:, 514:516], HALO[:, ch, 2:4]), reads=[HALO], writes=[pre])
        yield
        P.op("dve", lambda hh, a=a, pre=pre, ch=ch: hh.tensor_scalar(
            a[:, 0:NU], pre[:, 0:NU], convw[:, ch, 0:1], 0.0, ALU.mult, ALU.add), reads=[pre, k.ssmall], writes=[a])
        yield
        for wi_ in range(1, 5):
            P.op("dve", lambda hh, a=a, pre=pre, ch=ch, wi_=wi_: hh.scalar_tensor_tensor(
                a[:, 0:NU], pre[:, wi_:wi_ + NU], convw[:, ch, wi_:wi_ + 1], a[:, 0:NU], ALU.mult, ALU.add),
                reads=[pre, k.ssmall, a], writes=[a])
            yield
        if ch < 16:
            dst, dt_ = sil[n % 2], sil[n % 2]
        elif ch < 20:
            dst = BT[ch - 16]
        else:
            dst = CT[ch - 20]
        segs = ((0, 0, 256), (256, 260, 256)) if hf == 0 else ((0, 0, 512),)
        for (o0, a0, ln) in segs:
            P.op("act", lambda hh, dst=dst, a=a, ch=ch, o0=o0, a0=a0, ln=ln: hh.activation(
                dst[:, o0:o0 + ln], a[:, a0:a0 + ln], AF.Silu, bias=convb[:, ch:ch + 1], scale=1.0),
                reads=[a, k.ssmall], writes=[dst])
        if ch < 16:
            P.op("dve", lambda hh, dst=dst, ch=ch: hh.tensor_scalar(
                YT[:, ch, :], dst[:], dsum[:, ch:ch + 1], 0.0, ALU.mult, ALU.add), reads=[dst, dsum], writes=[YT], waw=False)
            xs = XSr[n % 3]
            P.op("act", lambda hh, dst=dst, xs=xs: hh.activation(xs[:], dst[:], AF.Copy), reads=[dst], writes=[xs])
            src_t, tok, tcol = xs, XTOK, ch * 128
        elif ch < 20:
            src_t, tok, tcol = dst, BTOK, (ch - 16) * 128
        else:
            src_t = None
        if src_t is not None:
            tb = next_bank(k)
            tbv = tb[:, 0:256].bitcast(BF16)
            for tc in range(4):
                P.op("pe", lambda hh, tbv=tbv, tc=tc, src_t=src_t: hh.transpose(tbv[:, TCS[tc]], src_t[:, TCS[tc]], k.identb[:]),
                     reads=[src_t, k.identb], writes=[tb], sig=(tc == 3))
            P.op("act", lambda hh, tbv=tbv, tok=tok, tcol=tcol: hh.activation(
                tok[:, :, tcol:tcol + 128], tbv.rearrange("p (t n) -> p t n", n=128), AF.Copy), reads=[tb], writes=[tok], waw=False)

    pend = [(pi, m) for pi in range(4, 10) for m in range(4)]
    act_, n = [], 0
    while pend or act_:
        while pend and len(act_) < 2:
            pi_, m_ = pend.pop(0)
            act_.append(chunk_gen(pi_, m_, n))
            n += 1
        for g_ in list(act_):
            try:
                next(g_)
            except StopIteration:
                act_.remove(g_)

    _mark(k, "ssd")
    A.reset(n_k[0], n_k[1], pool=True)
    H = A.alloc([128, 2048], F32, "H")
    HS = [T(H[:, q_ * 256:(q_ + 1) * 256], f"HS{q_}") for q_ in range(8)]
    Hb = A.alloc([128, 16, 2, 128], BF16, "Hb")
    XD = [A.alloc([128, 2, 2, 128], BF16, f"XD{q}") for q in range(2)]
    XW = [A.alloc([128, 256], BF16, f"XW{q}") for q in range(2)]
    R = [A.alloc([128, 8, 128], F32, f"R{q}") for q in range(2)]
    LT = [A.alloc([128, 4, 128], F32, f"LT{q}") for q in range(2)]
    EC = [A.alloc([128, 4, 128], F32, f"EC{q}") for q in range(2)]
    MT = [A.alloc([128, 4, 128], BF16, f"MT{q}") for q in range(2)]
    CW = [A.alloc([128, 4, 128], BF16, f"CW{q}") for q in range(2)]
    CBm = [A.alloc([128, 4, 128], F32, f"CBm{q}") for q in range(2)]
    W4 = [A.alloc([128, 4], F32, f"W4{q}") for q in range(2)]
    DBt = [A.alloc([128, 64], F32, f"DBt{q}") for q in range(2)]
    P.op("dve", lambda hh: hh.memset(Hb[:], 0.0), writes=[Hb])
    for q in range(2):
        P.op("dve", lambda hh, q=q: hh.memset(XD[q][:], 0.0), writes=[XD[q]])
    cnt = {"p": 0, "q": 0}

    def process(tc, d, state_only):
        last = 127 if d == 0 else 0
        pc = cnt["p"]
        cnt["p"] += 1
        tb_ = next_bank(k)
        P.op("pe", lambda hh: hh.matmul(tb_[:, 0:64], ones32[:], DTA[tc][:], start=True, stop=True),
             reads=[k.ones32, DTA[tc]], writes=[tb_])
        dbt = DBt[pc % 2]
        P.op("act", lambda hh: hh.activation(dbt[:], tb_[:, 0:64], AF.Exp), reads=[tb_], writes=[dbt])
        cbm = CBm[pc % 2]
        if not state_only:
            cb = next_bank(k)
            for g in range(4):
                P.op("pe", lambda hh, g=g: hh.matmul(cb[:, g * 128:(g + 1) * 128], BT[g][:, TCS[tc]], CT[g][:, TCS[tc]],
                     start=True, stop=True), reads=[BT[g], CT[g]], writes=[cb], sig=(g == 3))
            P.op("dve", lambda hh: hh.tensor_tensor(cbm[:], cb[:].rearrange("p (g n) -> p g n", n=128),
                 triI[d].unsqueeze(1).broadcast_to([128, 4, 128]), ALU.mult), reads=[cb, k.cst], writes=[cbm])
            Hv = H[:].rearrange("p (a w e) -> p a w e", w=2, e=64)
            P.op("act", lambda hh: hh.activation(Hb[:, :, 0, 0:64], Hv[:, :, 0, :], AF.Copy), reads=HS, writes=[Hb])
            P.op("act", lambda hh: hh.activation(Hb[:, :, 1, 64:128], Hv[:, :, 1, :], AF.Copy), reads=HS, writes=[Hb])
        def qiter(hg, q):
            r_ = R[hg % 2]
            c0 = d * 32 + hg * 8
            if q == 0:
                P.op("pool", lambda hh, r_=r_, c0=c0: hh.tensor_tensor(
                    r_[:], triI[d].unsqueeze(1).broadcast_to([128, 8, 128]),
                    DTA[tc][:, c0:c0 + 8].unsqueeze(2).broadcast_to([128, 8, 128]), ALU.mult),
                    reads=[k.cst, DTA[tc]], writes=[r_])
            yield
            qq = cnt["q"]
            cnt["q"] += 1
            h0 = hg * 8 + q * 4
            p0 = h0 // 2
            dcol = d * 32 + h0
            hv = H[:, h0 * 64:(h0 + 4) * 64].rearrange("p (a e) -> p a e", e=64)
            P.op("pool", lambda hh, hv=hv, dbt=dbt, dcol=dcol: hh.tensor_tensor(
                hv, hv, dbt[:, dcol:dcol + 4].unsqueeze(2).broadcast_to([128, 4, 64]), ALU.mult), reads=[HS[hg * 2 + q], dbt], writes=[HS[hg * 2 + q]])
            yield
            rr = r_[:, q * 4:(q + 1) * 4, :]
            bs = next_bank(k)
            P.op("pe", lambda hh, bs=bs, rr=rr: hh.matmul(bs[:], SLm[d], rr, start=True, stop=True),
                 reads=[k.cst, r_], writes=[bs])
            lt = LT[qq % 2]
            P.op("act", lambda hh, bs=bs, lt=lt: hh.activation(lt[:], bs[:].rearrange("p (a n) -> p a n", n=128), AF.Exp),
                 reads=[bs], writes=[lt])
            yield
            w4 = W4[qq % 2]
            dcol = d * 32 + h0
            P.op("dve", lambda hh, w4=w4, lt=lt, dcol=dcol: hh.tensor_tensor(
                w4[:], DT[tc][:, dcol:dcol + 4], lt[:, :, last], ALU.mult), reads=[DT[tc], lt], writes=[w4])
            yield
            xw = XW[qq % 2]
            xin = XTOK[:, tc, h0 * 64:(h0 + 4) * 64].rearrange("p (a e) -> p a e", e=64)
            P.op("pool", lambda hh, xw=xw, xin=xin, w4=w4: hh.tensor_tensor(
                xw[:].rearrange("p (a e) -> p a e", e=64), xin, w4[:].unsqueeze(2).broadcast_to([128, 4, 64]), ALU.mult),
                reads=[XTOK, w4], writes=[xw])
            yield
            if not state_only:
                bc = next_bank(k)
                P.op("pe", lambda hh, bc=bc, rr=rr: hh.matmul(bc[:], ones32[:], rr, start=True, stop=True),
                     reads=[k.ones32, r_], writes=[bc])
                ec = EC[qq % 2]
                P.op("act", lambda hh, bc=bc, ec=ec: hh.activation(ec[:], bc[:].rearrange("p (a n) -> p a n", n=128), AF.Exp),
                     reads=[bc], writes=[ec])
                yield
                mt = MT[qq % 2]
                P.op("dve", lambda hh, mt=mt, lt=lt, hg=hg: hh.tensor_tensor(
                    mt[:], lt[:], cbm[:, hg, :].unsqueeze(1).broadcast_to([128, 4, 128]), ALU.mult),
                    reads=[lt, cbm], writes=[mt])
                yield
                cw = CW[qq % 2]
                P.op("dve", lambda hh, cw=cw, ec=ec, hg=hg: hh.tensor_tensor(
                    cw[:], ec[:], CT[hg][:, TCS[tc]].unsqueeze(1).broadcast_to([128, 4, 128]), ALU.mult),
                    reads=[ec, CT[hg]], writes=[cw])
                yield
                xd = XD[qq % 2]
                xdv = xd[:].rearrange("p a w e -> p a (w e)").rearrange("p a (s e) -> p a s e", e=64)[:, :, 0:4:3, :]
                xin4 = XTOK[:, tc, h0 * 64:(h0 + 4) * 64].rearrange("p (a w e) -> p a w e", w=2, e=64)
                dtv = DT[tc][:, dcol:dcol + 4].rearrange("p (a w) -> p a w", w=2).unsqueeze(3).broadcast_to([128, 2, 2, 64])
                P.op("pool", lambda hh, xdv=xdv, xin4=xin4, dtv=dtv: hh.tensor_tensor(xdv, xin4, dtv, ALU.mult),
                     reads=[XTOK, DT[tc]], writes=[xd])
                yield
                yb = next_bank(k)
                for pp in range(2):
                    ops = ((xd[:, pp, 0, :], mt[:, 2 * pp, :]), (xd[:, pp, 1, :], mt[:, 2 * pp + 1, :]),
                           (Hb[:, p0 + pp, 0, :], cw[:, 2 * pp, :]), (Hb[:, p0 + pp, 1, :], cw[:, 2 * pp + 1, :]))
                    for n_, (l_, r2) in enumerate(ops):
                        P.op("pe", lambda hh, yb=yb, pp=pp, l_=l_, r2=r2, n_=n_: hh.matmul(
                            yb[:, pp * 128:(pp + 1) * 128], l_, r2, start=(n_ == 0), stop=(n_ == 3)),
                            reads=[xd, mt, Hb, cw], writes=[yb], sig=(n_ == 3 and pp == 1))
                yv = YT[:, p0:p0 + 2, TCS[tc]]
                P.op("dve", lambda hh, yv=yv, yb=yb: hh.tensor_tensor(
                    yv, yv, yb[:, 0:256].rearrange("p (a n) -> p a n", n=128), ALU.add), reads=[YT, yb], writes=[YT])
                yield
            sbk = next_bank(k)
            P.op("pe", lambda hh, sbk=sbk, xw=xw, hg=hg: hh.matmul(sbk[:, 0:256], BTOK[:, tc, hg * 128:(hg + 1) * 128], xw[:],
                 start=True, stop=True), reads=[BTOK, xw], writes=[sbk])
            P.op("dve", lambda hh, hv=hv, sbk=sbk: hh.tensor_tensor(
                hv, hv, sbk[:, 0:256].rearrange("p (a e) -> p a e", e=64), ALU.add), reads=[HS[hg * 2 + q], sbk], writes=[HS[hg * 2 + q]])

        pending = [(hg, q) for hg in range(4) for q in range(2)]
        active = []
        while pending or active:
            while pending and len(active) < 2:
                active.append(qiter(*pending.pop(0)))
            for g_ in list(active):
                try:
                    next(g_)
                except StopIteration:
                    active.remove(g_)

    if hf == 0:
        for d in range(2):
            for s_ in range(2):
                P.op("dve", lambda hh: hh.memset(H[:], 0.0), writes=HS)
                order = (2 * s_, 2 * s_ + 1) if d == 0 else (2 * s_ + 1, 2 * s_)
                for tc in order:
                    process(tc, d, False)
                P.dma("sp", k.O["stateO"][j, s_, d], H[:], reads=HS)
    else:
        SLD = A.alloc([128, 2048], F32, "SLD")
        TLD = A.alloc([128, 4, 64], F32, "TLD")
        TL = A.alloc([128, 64], F32, "TL")
        EE = A.alloc([128, 32], F32, "EE")
        sbn = [P.dram(f"sbn{j}_{d}", [128, 2048], F32) for d in range(2)]
        sgg = [P.dram(f"sgg{j}_{d}", [4 * 128, 2048], F32) for d in range(2)]
        tlbd = P.dram(f"tlb{j}", [128, 64], F32)
        tlgd = P.dram(f"tlg{j}", [4 * 128, 64], F32)
        tlb = acc_bank(k)
        for tc in range(4):
            P.op("pe", lambda hh, tc=tc: hh.matmul(tlb[:, 0:64], ones32[:], DTA[tc][:], start=(tc == 0), stop=(tc == 3)),
                 reads=[k.ones32, DTA[tc]], writes=[tlb])
        P.op("dve", lambda hh: hh.tensor_tensor(TL[:], tlb[:, 0:64], k.zer[:, 0:64], ALU.add), reads=[tlb, k.zer], writes=[TL])
        P.dma("sp", tlbd.t.ap(), TL[:], reads=[TL], writes=[tlbd])
        P.collective("AllGather", GROUPS4, tlbd, tlgd)
        _mark(k, "prepass")
        for d in range(2):
            P.op("dve", lambda hh: hh.memset(H[:], 0.0), writes=HS)
            for tc in ((0, 1, 2, 3) if d == 0 else (3, 2, 1, 0)):
                process(tc, d, True)
            P.dma("sp", sbn[d].t.ap(), H[:], reads=HS, writes=[sbn[d]])
            P.collective("AllGather", GROUPS4, sbn[d], sgg[d])
        _mark(k, "fold")
        P.dma("sp", TLD[:], tlgd.t.ap().rearrange("(r p) n -> p r n", p=128), reads=[tlgd], writes=[TLD])
        for d in range(2):
            sgv = sgg[d].t.ap()
            P.dma("sp", H[:], I["stateT"][j, d], writes=HS)
            for r in ((0, 1, 2, 3) if d == 0 else (3, 2, 1, 0)):
                mcol = (0 if d == 0 else 4) + r
                mk = k.posm[:, mcol:mcol + 1]
                P.dma("sp", SLD[:], sgv[r * 128:(r + 1) * 128, :], reads=[sgg[d]], writes=[SLD])
                P.op("act", lambda hh, r=r, d=d, mk=mk: hh.activation(EE[:], TLD[:, r, d * 32:(d + 1) * 32], AF.Exp, scale=mk),
                     reads=[TLD, k.posm], writes=[EE])
                hv = H[:].rearrange("p (a e) -> p a e", e=64)
                P.op("dve", lambda hh, hv=hv: hh.tensor_tensor(hv, hv, EE[:].unsqueeze(2).broadcast_to([128, 32, 64]), ALU.mult),
                     reads=HS + [EE], writes=HS)
                P.op("dve", lambda hh, mk=mk: hh.scalar_tensor_tensor(H[:], SLD[:], mk, H[:], ALU.mult, ALU.add),
                     reads=[SLD, k.posm] + HS, writes=HS)
            for tc in ((0, 1, 2, 3) if d == 0 else (3, 2, 1, 0)):
                process(tc, d, False)

    _mark(k, "gate")
    A.reset(n_y[0], n_y[1])
    HN = adaln(k, i, 3, 4, 1, halves=(hf,))
    sz = [A.alloc([128, HT], F32, f"sz{q}") for q in range(2)]
    YN = [A.alloc([128, HT], BF16, f"YN{c}") for c in range(16)]
    rs = A.alloc([128, HT], F32, "rsy")
    stb = acc_bank(k)
    n = 0
    for pz in range(4):
        s = next_slot(k)
        P.dma("pool", s[:, 0:NCH * 512], I["ssm_win"][j, pz], writes=[s])
        w = s[:, 0:NCH * 512].rearrange("p (c n) -> p c n", n=512)
        for m in range(4):
            ch = pz * 4 + m
            b = next_bank(k)
            for c in range(NCH):
                P.op("pe", lambda hh, b=b, c=c, m=m, w=w: hh.matmul(b[:], w[:, c, m * 128:(m + 1) * 128], HN[c][hf][:],
                     start=(c == 0), stop=(c == NCH - 1)), reads=[s, HN[c][hf]], writes=[b], sig=(c == NCH - 1))
            z = sz[n % 2]
            n += 1
            P.op("act", lambda hh, z=z, b=b: hh.activation(z[:], b[:], AF.Silu), reads=[b], writes=[z])
            P.op("dve", lambda hh, z=z, ch=ch: hh.tensor_tensor(YT[:, ch, :], YT[:, ch, :], z[:], ALU.mult), reads=[YT, z], writes=[YT])
            sq = k.sq[ch % 2]
            P.op("act", lambda hh, sq=sq, ch=ch: hh.activation(sq[:], YT[:, ch, :], AF.Square), reads=[YT], writes=[sq])
            P.op("pe", lambda hh, sq=sq, ch=ch: hh.matmul(stb[:], k.ones[:], sq[:], start=(ch == 0), stop=(ch == 15)),
                 reads=[sq, k.ones], writes=[stb])
    rsqrt_from_bank(k, stb, rs, 2048)
    for ch in range(16):
        P.op("dve", lambda hh, ch=ch: hh.scalar_tensor_tensor(
            YN[ch][:], YT[:, ch, :], ngs[:, ch:ch + 1], rs[:], ALU.mult, ALU.mult), reads=[YT, k.ssmall, rs], writes=[YN[ch]])
    g5 = modv(k, i, 5)
    for pw in range(4):
        s = next_slot(k)
        P.dma("pool", s[:, 0:4096], I["ssm_wout"][j, pw], writes=[s])
        w = s[:, 0:4096].rearrange("p (c n) -> p c n", n=256)
        for dl in range(2):
            dc = pw * 2 + dl
            b = next_bank(k)
            for ch in range(16):
                P.op("pe", lambda hh, b=b, ch=ch, dl=dl, w=w: hh.matmul(b[:], w[:, ch, dl * 128:(dl + 1) * 128], YN[ch][:],
                     start=(ch == 0), stop=(ch == 15)), reads=[s, YN[ch]], writes=[b], sig=(ch == 15))
            xt = k.X[dc][hf]
            P.op("dve", lambda hh, b=b, xt=xt, dc=dc: hh.scalar_tensor_tensor(
                xt[:], b[:], g5[:, dc, hf:hf + 1], xt[:], ALU.mult, ALU.add), reads=[b, k.mod, xt], writes=[xt])


def final_norm(k):
    P = k.P
    A = k.arena
    k.sq = [A.alloc([128, HT], BF16, f"sq{j}") for j in range(2)]
    rstd = [A.alloc([128, HT], F32, f"rstd{h}") for h in range(2)]
    yo = [A.alloc([128, HT], F32, f"yo{j}") for j in range(4)]
    n = 0
    outs = []
    for h in range(2):
        rms_stats(k, h, rstd[h])
        for c in range(NCH):
            y = yo[n % 4]
            n += 1
            xt = k.X[c][h]
            P.op("dve", lambda hh, y=y, xt=xt, c=c, h=h: hh.scalar_tensor_tensor(
                y[:], xt[:], k.fnormg[:, c:c + 1], rstd[h][:], ALU.mult, ALU.mult),
                reads=[xt, k.fnormg, rstd[h]], writes=[y])
            P.dma("sp", k.O["yT"][:, c, h * HT:(h + 1) * HT], y[:], reads=[y])
    P.wait_all("sp", yo + k.out_tiles)


def _fm(v):
    v = np.asarray(v, np.float32)
    lead = v.shape[:-1]
    a = v.reshape(lead + (NCH, 128))
    a = np.moveaxis(a, -1, 0)
    return np.ascontiguousarray(a)


def host_prep(inp):
    f32 = np.float32
    sh = {}
    mw = np.asarray(inp["mod_w"], f32).reshape(DEPTH, NCH, 128, 9, 2, 512)
    sh["modw"] = np.ascontiguousarray(mw.transpose(0, 3, 4, 2, 1, 5)).reshape(DEPTH, 9, 2, 128, NCH * 512)
    mb = np.asarray(inp["mod_b"], f32).reshape(DEPTH, 9, NCH, 128)
    sh["modb"] = np.ascontiguousarray(mb.transpose(3, 0, 1, 2)).reshape(128, DEPTH * 9 * NCH)
    ng = np.asarray(inp["norm_g"], f32).reshape(DEPTH, 3, NCH, 128)
    sh["normg"] = np.ascontiguousarray(ng.transpose(3, 0, 1, 2)).reshape(128, DEPTH * 3 * NCH)
    sh["fnormg"] = np.ascontiguousarray(np.asarray(inp["final_norm_g"], f32).reshape(NCH, 128).T)
    wi = np.asarray(inp["ffn_w_in"], f32).reshape(DEPTH, 2, NCH, 128, 2, 11, 256)
    sh["ffn_in"] = np.ascontiguousarray(wi.transpose(0, 1, 5, 3, 2, 4, 6)).reshape(DEPTH, 2, 11, 128, NCH * 512)
    wo = np.asarray(inp["ffn_w_out"], f32).reshape(DEPTH, 2, NF, 128, NCH, 128)
    sh["ffn_out"] = np.ascontiguousarray(wo.transpose(0, 1, 4, 3, 2, 5)).reshape(DEPTH, 2, NCH, 128, NF * 128)
    perm = np.arange(32) ^ 8
    wi_ = np.asarray(inp["mla_w_in"], f32)
    def pcn(w):
        rows, n = w.shape
        return np.ascontiguousarray(w.reshape(rows // 128, 128, n).transpose(1, 0, 2)).reshape(128, (rows // 128) * n)
    sh["mla_w1"] = np.stack([pcn(wi_[j][:, 0:512]) for j in range(2)])
    sh["mla_w2"] = np.stack([pcn(np.concatenate([wi_[j][:, 512:800], wi_[j][:, 704:768], wi_[j][:, 768 + perm]], 1)) for j in range(2)])
    wq_ = np.asarray(inp["mla_wq_b"], f32)
    colsw = np.arange(1536).reshape(16, 96).copy()
    colsw[:, 64:96] = colsw[:, 64 + perm]
    colsw = colsw.reshape(-1)
    sh["mla_wq"] = np.stack([np.stack([np.stack([pcn(w[:, pc * 768:(pc + 1) * 768]) for pc in range(2)])
                                       for w in (wq_[j], wq_[j][:, colsw])]) for j in range(2)])
    sh["mla_wkv"] = np.stack([pcn(np.asarray(inp["mla_wkv_b"], f32)[j]) for j in range(2)])
    wo_ = np.asarray(inp["mla_wo"], f32)
    sh["mla_wo"] = np.stack([np.stack([pcn(wo_[j][:, hf * 512:(hf + 1) * 512]) for hf in range(2)]) for j in range(2)])
    qn_ = np.asarray(inp["mla_q_norm"], f32).reshape(2, 4, 128)
    kn_ = np.asarray(inp["mla_kv_norm"], f32).reshape(2, 2, 128)
    sh["mla_small"] = np.ascontiguousarray(np.concatenate([qn_, kn_], 1).transpose(2, 0, 1)).reshape(128, 12)
    sw = np.asarray(inp["ssm_w_in"], f32)
    sh["ssm_win"] = np.stack([np.stack([pcn(sw[j][:, q * 512:(q + 1) * 512]) for q in range(10)]) for j in range(2)])
    sh["ssm_wdt"] = np.stack([pcn(sw[j][:, 5120:5184]) for j in range(2)])
    so_ = np.asarray(inp["ssm_w_out"], f32)
    sh["ssm_wout"] = np.stack([np.stack([pcn(so_[j][:, q * 256:(q + 1) * 256]) for q in range(4)]) for j in range(2)])
    small = np.zeros((128, 2, 192), f32)
    cw_ = np.asarray(inp["ssm_conv_w"], f32)
    cb_ = np.asarray(inp["ssm_conv_b"], f32)
    sg_ = np.asarray(inp["ssm_norm_g"], f32)
    sd_ = np.asarray(inp["ssm_d"], f32)
    for j in range(2):
        small[:, j, 0:120] = cw_[j].reshape(5, 24, 128).transpose(2, 1, 0).reshape(128, 120)
        small[:, j, 120:144] = cb_[j].reshape(24, 128).T
        small[:, j, 144:160] = sg_[j].reshape(16, 128).T
        hidx = (np.arange(16)[None, :] * 2 + (np.arange(128)[:, None] // 64))
        small[:, j, 160:192] = np.stack([sd_[j, 0][hidx], sd_[j, 1][hidx]], -1).reshape(128, 32)
    sh["ssm_small"] = small.reshape(128, 384)
    bc = np.stack([np.asarray(inp["ssm_dt_bias"], f32).reshape(2, 64), np.asarray(inp["ssm_a_log"], f32).reshape(2, 64)], 1)
    sh["ssm_bc"] = np.ascontiguousarray(np.broadcast_to(bc[None], (128, 2, 2, 64)))
    ii = np.arange(128)
    cst = np.zeros((128, 5, 128), f32)
    cst[:, 0] = (ii[:, None] <= ii[None, :])
    cst[:, 1] = (ii[:, None] >= ii[None, :])
    cst[:, 2] = (ii[:, None] > ii[None, :])
    cst[:, 3] = (ii[:, None] < ii[None, :])
    cst[:, 4] = np.eye(128)
    sh["consts"] = cst
    sst = np.asarray(inp["state_ssm"], f32)
    cache = np.asarray(inp["cache_mla"], f32)
    freqs = 1.0 / (10000.0 ** (np.arange(0, 16, 2, dtype=np.float32) / 16.0))
    xp = np.asarray(inp["x_prompt"], f32)
    xs = np.asarray(inp["x_sample"], f32)
    c = np.asarray(inp["c"], f32)
    cc = np.asarray(inp["c_ctx"], f32)
    per = []
    for r in range(NCORES):
        gi, kq = r // 4, r % 4
        tok = np.concatenate([xp[2 * r], xp[2 * r + 1], xs[gi, kq * HT:(kq + 1) * HT]], 0)
        xT = np.ascontiguousarray(tok.T.reshape(NCH, 128, TT).transpose(1, 0, 2))
        cv = np.stack([cc, c[gi]], -1).reshape(NCH, 128, 2).transpose(1, 0, 2)
        d = dict(sh)
        d["xT"] = xT
        d["cvec"] = np.ascontiguousarray(cv)
        tg = kq * HT + np.arange(HT)
        pos = np.stack([tg // 64, tg % 64], 0).astype(np.float32)
        rope = np.zeros((96, 2, HT), np.float32)
        for ax in range(2):
            for hf in range(2):
                for fr in range(8):
                    f = ax * 16 + hf * 8 + fr
                    ang = (pos[ax] * freqs[fr]).astype(np.float32)
                    rope[64 + f, 0] = np.cos(ang)
                    rope[64 + f, 1] = np.sin(ang) * (-1.0 if hf == 0 else 1.0)
        d["ropeT"] = rope
        d["cacheT"] = np.ascontiguousarray(cache[gi].transpose(0, 2, 1))
        pm = np.zeros((128, 16), f32)
        for r_ in range(4):
            pm[:, r_] = float(r_ < kq)
            pm[:, 4 + r_] = float(r_ > kq)
            pm[:, 8 + r_] = float(r_ == kq - 1)
            pm[:, 12 + r_] = float(r_ == kq + 1)
        d["posm"] = pm
        d["stateT"] = np.ascontiguousarray(sst[gi].reshape(2, 2, 2048, 128).transpose(0, 1, 3, 2))
        per.append(d)
    return per


_NC_CACHE = {}


def run_device(inp, stage=99):
    if stage not in _NC_CACHE:
        _NC_CACHE[stage] = build_program(stage)
    nc = _NC_CACHE[stage]
    per = host_prep(inp)
    per = [{n: d[n] for n in nc._in_names} for d in per]
    res = run_bass_kernel_spmd(nc, per, core_ids=list(range(NCORES)))
    return res.results


def kernel(**inputs):
    return kernel_stage(inputs, 99)


def kernel_stage(inputs, stage):
    res = run_device(inputs, stage)
    B, S = 16, 256
    yp = np.zeros((B, S, D), np.float32)
    ys = np.zeros((2, 2048, D), np.float32)
    for r in range(NCORES):
        yT = res[r]["yT"]
        tok = yT.transpose(2, 1, 0).reshape(TT, D)
        yp[2 * r] = tok[0:256]
        yp[2 * r + 1] = tok[256:512]
        gi, kq = r // 4, r % 4
        ys[gi, kq * HT:(kq + 1) * HT] = tok[512:1024]
    nc_ = np.zeros((B, 2, S, 288), np.float32)
    for r in range(NCORES):
        co = res[r]["cacheO"]
        for s_ in range(2):
            nc_[2 * r + s_] = co[:, :, s_ * 256:(s_ + 1) * 256].transpose(0, 2, 1)
    ns_ = np.zeros((B, 2, 2, 32, 64, 128), np.float32)
    for r in range(NCORES):
        so = res[r]["stateO"]
        for s_ in range(2):
            ns_[2 * r + s_] = so[:, s_].transpose(0, 1, 3, 2).reshape(2, 2, 32, 64, 128)
    return yp, ys, nc_, ns_
```
